# Optimizing a Trainium2 kernel written in Bass

```python
import jax, jax.numpy as jnp
from jax import lax
import numpy as np

D_MODEL = 1024
BATCH = 8
SEQ = 8192
DEPTH = 1

HEAD_DIM = 64
MIX_WIDTH = D_MODEL
FOURIER_WIDTH = MIX_WIDTH // 2
ATTN_WIDTH = MIX_WIDTH - FOURIER_WIDTH
N_FOURIER_GROUPS = FOURIER_WIDTH // HEAD_DIM
N_SLOTS = ATTN_WIDTH // HEAD_DIM
DILATED_CONFIGS = ((128, 1), (512, 4), (2048, 16))
N_CFG = len(DILATED_CONFIGS)
N_ATTN_HEADS = N_SLOTS * N_CFG
QKV_WIDTH = N_ATTN_HEADS * HEAD_DIM
IN_WIDTH = 2 * FOURIER_WIDTH + 3 * QKV_WIDTH + ATTN_WIDTH
SPLIT_POINTS = tuple(int(p) for p in np.cumsum([FOURIER_WIDTH, FOURIER_WIDTH, QKV_WIDTH, QKV_WIDTH, QKV_WIDTH]))
NORM_EPS = 1e-6
MASK_VALUE = -1e30

kernel_name = "hymba_fnet_longnet_encoder_block"


def alibi_slopes(n):
    return 2.0 ** (-8.0 * jnp.arange(1, n + 1, dtype=jnp.float32) / n)


def rms_norm(t, w):
    t32 = t.astype(jnp.float32)
    return t32 * lax.rsqrt(jnp.mean(t32 * t32, axis=-1, keepdims=True) + NORM_EPS) * w.astype(jnp.float32)


def dilated_window_attention(q, k, v, window, dilation, slopes):
    B, S, H, DH = q.shape
    radius = window // (2 * dilation)
    blk = radius
    L = S // dilation
    N = B * dilation
    nb = -(-L // blk)
    Lp = nb * blk

    def to_sub(t):
        return t.reshape(B, L, dilation, H, DH).transpose(0, 2, 1, 3, 4).reshape(N, L, H, DH)

    qs, ks, vs = to_sub(q), to_sub(k), to_sub(v)
    qb = jnp.pad(qs, ((0, 0), (0, Lp - L), (0, 0), (0, 0))).reshape(N, nb, blk, H, DH)
    pad_kv = ((0, 0), (blk, Lp - L + blk), (0, 0), (0, 0))
    kp = jnp.pad(ks, pad_kv).reshape(N, nb + 2, blk, H, DH)
    vp = jnp.pad(vs, pad_kv).reshape(N, nb + 2, blk, H, DH)
    kb = jnp.concatenate([kp[:, :-2], kp[:, 1:-1], kp[:, 2:]], axis=2)
    vb = jnp.concatenate([vp[:, :-2], vp[:, 1:-1], vp[:, 2:]], axis=2)

    scores = jnp.einsum('nbqhd,nbkhd->nbhqk', qb, kb)
    a = jnp.arange(blk)[:, None]
    c = jnp.arange(3 * blk)[None, :]
    rel = c - blk - a
    key_pos = jnp.arange(nb)[:, None, None] * blk - blk + c[None]
    valid = (jnp.abs(rel)[None] <= radius) & (key_pos >= 0) & (key_pos < L)
    dist = (jnp.abs(rel) * dilation).astype(jnp.float32)
    bias = -slopes[:, None, None] * dist[None]
    scores = jnp.where(valid[None, :, None], scores + bias[None, None], MASK_VALUE)

    m = jnp.max(scores, axis=-1, keepdims=True)
    p = jnp.exp(scores - m)
    s = jnp.sum(p, axis=-1, keepdims=True)
    o = jnp.einsum('nbhqk,nbkhd->nbqhd', p / s, vb)
    lse = (m + jnp.log(s))[..., 0].transpose(0, 1, 3, 2)

    o = o.reshape(N, Lp, H, DH)[:, :L].reshape(B, dilation, L, H, DH).transpose(0, 2, 1, 3, 4).reshape(B, S, H, DH)
    lse = lse.reshape(N, Lp, H)[:, :L].reshape(B, dilation, L, H).transpose(0, 2, 1, 3).reshape(B, S, H)
    return o, lse


def setup_inputs(seed: int = 0) -> dict:
    key = jax.random.key(seed)
    ks = jax.random.split(key, 7)
    x = jax.random.normal(ks[0], (BATCH, SEQ, D_MODEL), jnp.float32)
    norm_w = 1.0 + 0.02 * jax.random.normal(ks[1], (D_MODEL,), jnp.float32)
    w_in = jax.random.normal(ks[2], (D_MODEL, IN_WIDTH), jnp.float32) * D_MODEL ** -0.5
    q_norm_w = 1.0 + 0.02 * jax.random.normal(ks[3], (N_ATTN_HEADS, HEAD_DIM), jnp.float32)
    k_norm_w = 1.0 + 0.02 * jax.random.normal(ks[4], (N_ATTN_HEADS, HEAD_DIM), jnp.float32)
    w_fourier = jax.random.normal(ks[5], (N_FOURIER_GROUPS, HEAD_DIM, HEAD_DIM), jnp.float32) * HEAD_DIM ** -0.5
    w_out = jax.random.normal(ks[6], (MIX_WIDTH, D_MODEL), jnp.float32) * MIX_WIDTH ** -0.5
    return {"x": x, "norm_w": norm_w, "w_in": w_in, "q_norm_w": q_norm_w,
            "k_norm_w": k_norm_w, "w_fourier": w_fourier, "w_out": w_out}


def reference(x, norm_w, w_in, q_norm_w, k_norm_w, w_fourier, w_out):
    B, S, _ = x.shape
    slopes = alibi_slopes(N_SLOTS)
    h = x.astype(jnp.float32)
    for _layer in range(DEPTH):
        hn = rms_norm(h, norm_w)
        proj = hn @ w_in.astype(jnp.float32)
        u_f, g_f, q, k, v, g_a = jnp.split(proj, SPLIT_POINTS, axis=-1)

        u_f = u_f.reshape(B, S, N_FOURIER_GROUPS, HEAD_DIM)
        f = jnp.fft.fft2(u_f, axes=(1, 3), norm="ortho").real.astype(jnp.float32)
        f = jnp.einsum('bsgc,gcd->bsgd', f, w_fourier.astype(jnp.float32)).reshape(B, S, FOURIER_WIDTH)
        y_f = f * jax.nn.silu(g_f)

        q = rms_norm(q.reshape(B, S, N_ATTN_HEADS, HEAD_DIM), q_norm_w) * HEAD_DIM ** -0.5
        k = rms_norm(k.reshape(B, S, N_ATTN_HEADS, HEAD_DIM), k_norm_w)
        v = v.reshape(B, S, N_ATTN_HEADS, HEAD_DIM)
        outs, lses = [], []
        for c, (window, dilation) in enumerate(DILATED_CONFIGS):
            sl = slice(c * N_SLOTS, (c + 1) * N_SLOTS)
            o_c, lse_c = dilated_window_attention(q[:, :, sl], k[:, :, sl], v[:, :, sl], window, dilation, slopes)
            outs.append(o_c)
            lses.append(lse_c)
        alpha = jax.nn.softmax(jnp.stack(lses, axis=0), axis=0)
        o = jnp.sum(alpha[..., None] * jnp.stack(outs, axis=0), axis=0).reshape(B, S, ATTN_WIDTH)
        y_a = o * jax.nn.silu(g_a)

        mixed = jnp.concatenate([y_f, y_a], axis=-1) @ w_out.astype(jnp.float32)
        h = h + mixed
    return h.astype(x.dtype)
```

```python
import os
from contextlib import ExitStack

import ml_dtypes
import numpy as np

import concourse.bass as bass
import concourse.mybir as mybir
from concourse.alu_op_type import AluOpType as ALU
from concourse.bass_utils import run_bass_kernel_spmd

F32 = mybir.dt.float32
BF16 = mybir.dt.bfloat16
AF = mybir.ActivationFunctionType
AX = mybir.AxisListType

S = 8192
D = 1024
NT = S // 128
INW = 6144
NCOL = 6656
DILS = (1, 4, 16)
EPS = 1e-6
DEBUG = bool(int(os.environ.get("MK_DEBUG", "0")))
NTILES_DBG = int(os.environ.get("MK_NT", str(NT)))
STOP = int(os.environ.get("MK_STOP", "9"))
NCORES = int(os.environ.get("MK_CORES", "8"))
P3MODE = int(os.environ.get("MK_P3", "9"))
P3X = int(os.environ.get("MK_P3X", "9"))
SKIP01 = bool(int(os.environ.get("MK_SKIP01", "0")))


def _consts():
    bf = ml_dtypes.bfloat16
    c = {}
    c["identb"] = np.eye(128, dtype=np.float32).astype(bf)
    kap = 1.0 / np.sqrt(S * 64.0)
    cc = np.arange(64)
    ang = 2 * np.pi * np.outer(cc, cc) / 64.0
    cos2 = np.concatenate([np.cos(ang), np.cos(ang)], axis=1) * kap
    sin2 = np.concatenate([np.sin(ang), np.sin(ang)], axis=1) * kap
    c["cs64"] = np.concatenate([cos2, sin2], axis=1).astype(np.float32)
    s1 = np.arange(128)[:, None, None].astype(np.float64)
    s2 = np.arange(64)[None, :, None].astype(np.float64)
    k1 = np.arange(128)[None, None, :].astype(np.float64)
    ph = 2 * np.pi * (s1 * k1 / 128.0 + s2 * k1 / 8192.0)
    A = np.cos(ph)
    B = np.sin(ph)
    c["tabA"] = A.reshape(128, 8192).astype(np.float32).astype(bf)
    c["tabB"] = B.reshape(128, 8192).astype(np.float32).astype(bf)
    c["tabC"] = (-B).reshape(128, 8192).astype(np.float32).astype(bf)
    s2v = np.arange(64)[:, None].astype(np.float64)
    k2v = np.arange(64)[None, :].astype(np.float64)
    ph2 = 2 * np.pi * s2v * k2v / 64.0
    c["cs2"] = np.concatenate([np.cos(ph2), -np.sin(ph2)], axis=0).astype(np.float32).astype(bf)
    j = np.arange(128)[:, None].astype(np.float64)
    i = np.arange(128)[None, :].astype(np.float64)
    em = np.zeros((128, 3, 4, 2, 2, 128), dtype=np.float64)
    for ci, dil in enumerate(DILS):
        for h in range(8):
            slope = 2.0 ** (-(h + 1))
            lo = np.where(j >= i, np.exp(-slope * dil * np.abs(j - 64 - i)), 0.0)
            hi = np.where(j <= i, np.exp(-slope * dil * np.abs(64 + j - i)), 0.0)
            em[:, ci, h // 2, h % 2, 0, :] = lo
            em[:, ci, h // 2, h % 2, 1, :] = hi
    c["emask"] = em.reshape(128, 12 * 512).astype(np.float32).astype(bf)
    return c


class Sem:
    def __init__(self, K, name):
        self.h = K.root.enter_context(K.nc.semaphore(name))
        self.n = 0


class Buf:
    def __init__(self, t, dsem=None):
        self.t = t
        self.w = None
        self.r = []
        self.dsem = dsem


class KB:
    def __init__(self, nc):
        self.nc = nc
        self.root = ExitStack()
        self.E = {"pe": nc.tensor, "act": nc.scalar, "dve": nc.vector, "pool": nc.gpsimd, "sp": nc.sync}
        self.esem = {e: Sem(self, "e_" + e) for e in ("pe", "act", "dve", "pool")}
        self.waited = {e: {} for e in self.E}
        self.dpool = [Sem(self, "d%d" % i) for i in range(72)]
        self.dnext = 0
        self.phase_sem = Sem(self, "phase")
        self.allbufs = []

    def sb(self, es, name, shape, dt, dma=False):
        t = es.enter_context(self.nc.sbuf_tensor("sb_" + name, list(shape), dt))
        b = Buf(t, self.new_dsem() if dma else None)
        self.allbufs.append(b)
        return b

    def ps(self, es, name):
        t = es.enter_context(self.nc.psum_tensor("ps_" + name, [128, 512], F32))
        b = Buf(t)
        self.allbufs.append(b)
        return b

    def new_dsem(self):
        s = self.dpool[self.dnext]
        self.dnext += 1
        return s

    def _waits(self, eng, r, w):
        need = {}

        def add(ev):
            if ev is None:
                return
            s, v = ev
            if need.get(s, 0) < v:
                need[s] = v

        for b in r:
            add(b.w)
        for b in w:
            add(b.w)
            for ev in b.r:
                add(ev)
        for s, v in need.items():
            if self.waited[eng].get(s, 0) < v:
                self.E[eng].wait_ge(s.h, v)
                self.waited[eng][s] = v

    def _record(self, ev, r, w):
        for b in r:
            b.r.append(ev)
        for b in w:
            b.w = ev
            b.r = []

    def op(self, eng, fn, r=(), w=()):
        self._waits(eng, r, w)
        ins = fn(self.E[eng])
        s = self.esem[eng]
        ins.then_inc(s.h, 1)
        s.n += 1
        self._record((s, s.n), r, w)
        return ins

    def mm_group(self, fns, r=(), w=()):
        self._waits("pe", r, w)
        ins = None
        for fn in fns:
            ins = fn(self.nc.tensor)
        s = self.esem["pe"]
        ins.then_inc(s.h, 1)
        s.n += 1
        self._record((s, s.n), r, w)

    def dma(self, q, out, in_, sem, r=(), w=(), **kw):
        self._waits(q, r, w)
        ins = self.E[q].dma_start(out=out, in_=in_, **kw)
        ins.then_inc(sem.h, 16)
        sem.n += 16
        self._record((sem, sem.n), r, w)

    def drain(self):
        sp = self.nc.sync
        for s in list(self.esem.values()) + self.dpool[: self.dnext]:
            if s.n > 0 and self.waited["sp"].get(s, 0) < s.n:
                sp.wait_ge(s.h, s.n)
                self.waited["sp"][s] = s.n
        ps = self.phase_sem
        sp.sem_inc(ps.h, 1)
        ps.n += 1
        for e in ("pe", "act", "dve", "pool"):
            self.E[e].wait_ge(ps.h, ps.n)
            for s in list(self.esem.values()) + self.dpool[: self.dnext]:
                self.waited[e][s] = s.n
        for b in self.allbufs:
            b.w = None
            b.r = []
        self.allbufs = []
        self.dnext = 0


def bc(ap, shape):
    return ap.to_broadcast(list(shape))


def build():
    nc = bass.Bass("TRN2", target_bir_lowering=False)
    K = KB(nc)
    dt_in = lambda n, s, d=F32: nc.dram_tensor(n, list(s), d, kind="ExternalInput").ap()
    x_d = dt_in("x", [S, D])
    nw_d = dt_in("norm_w", [D])
    win_d = dt_in("w_in", [D, INW])
    qn_d = dt_in("q_norm_w", [24, 64])
    kn_d = dt_in("k_norm_w", [24, 64])
    wf_d = dt_in("w_fourier", [8, 64, 64])
    wout_d = dt_in("w_out", [D, D])
    identb_d = dt_in("identb", [128, 128], BF16)
    cs64_d = dt_in("cs64", [64, 256])
    tabA_d = dt_in("tabA", [128, 8192], BF16)
    tabB_d = dt_in("tabB", [128, 8192], BF16)
    tabC_d = dt_in("tabC", [128, 8192], BF16)
    cs2_d = dt_in("cs2", [128, 64], BF16)
    em_d = dt_in("emask", [128, 12 * 512], BF16)
    y_d = nc.dram_tensor("y", [S, D], F32, kind="ExternalOutput").ap()
    skind = "ExternalOutput" if DEBUG else "Internal"
    scr = lambda n, s, d: nc.dram_tensor(n, list(s), d, kind=skind).ap()
    PQ_d = scr("PQ_s", [S, 1024], BF16)
    GF_d = scr("GF_s", [S, 512], BF16)
    GA_d = scr("GA_s", [S, 512], BF16)
    QK_d = scr("QK_s", [S, 3072], BF16)
    V_d = scr("V_s", [S, 1536], BF16)
    Y_d = scr("Y_s", [64, 128, 2, 512], BF16)
    O_d = [scr("O%d_s" % c, [S, 520], F32) for c in range(3)]

    es0 = K.root
    identb = K.sb(es0, "identb", [128, 128], BF16, dma=True)
    mhalf = K.sb(es0, "mhalf", [128, 48], F32)
    psb = [K.ps(es0, "psb%d" % i) for i in range(8)]
    wout_bf = K.sb(es0, "wout_bf", [128, 8, 1024], BF16)
    K.dma("sp", identb.t[:], identb_d[:, :], identb.dsem, w=[identb])
    K.op("pool", lambda e: e.memset(mhalf.t[:], -0.5), w=[mhalf])

    with ExitStack() as es:
      if not SKIP01:
        w_bf = K.sb(es, "w_bf", [128, 8, NCOL], BF16)
        with ExitStack() as es_p0:
            nw = K.sb(es_p0, "nw", [128, 8], F32, dma=True)
            stage = [K.sb(es_p0, "stage%d" % i, [128, INW], F32, dma=True) for i in range(2)]
            wu_bf = K.sb(es_p0, "wu_bf", [128, 8, 512], BF16)
            wuT = K.sb(es_p0, "wuT", [128, 4, 1024], BF16)
            cs64 = K.sb(es_p0, "cs64", [64, 256], F32, dma=True)
            wf_sb = K.sb(es_p0, "wf_sb", [64, 8, 64], F32, dma=True)
            BD = K.sb(es_p0, "BD", [128, 4, 256], BF16)
            K.dma("sp", nw.t[:], nw_d.rearrange("(kc p) -> p kc", p=128), nw.dsem, w=[nw],
                  allow_slow_non_contiguous=True)
            K.dma("sp", cs64.t[:], cs64_d[:, :], cs64.dsem, w=[cs64])
            K.dma("sp", wf_sb.t[:], wf_d.rearrange("g c d -> c g d"), wf_sb.dsem, w=[wf_sb])
            win_v = win_d.rearrange("(kc p) n -> p kc n", p=128)
            for kc in range(8):
                st = stage[kc % 2]
                K.dma("sp", st.t[:], win_v[:, kc, :], st.dsem, w=[st])
                sc = nw.t[:, kc:kc + 1]
                K.op("dve", lambda e: e.tensor_scalar(out=w_bf.t[:, kc, 1024:4096], in0=st.t[:, 512:3584],
                                                      scalar1=sc, scalar2=None, op0=ALU.mult),
                     r=[st, nw], w=[w_bf])
                K.op("act", lambda e: e.activation(out=w_bf.t[:, kc, 4096:NCOL], in_=st.t[:, 3584:INW],
                                                   func=AF.Copy, scale=sc), r=[st, nw], w=[w_bf])
                K.op("pool", lambda e: e.tensor_scalar(out=wu_bf.t[:, kc, :], in0=st.t[:, 0:512],
                                                       scalar1=sc, scalar2=None, op0=ALU.mult),
                     r=[st, nw], w=[wu_bf])
            wout_v = wout_d.rearrange("(cc p) n -> p cc n", p=128)
            for h in range(2):
                st = stage[h]
                K.dma("sp", st.t[:, 0:4096].rearrange("p (c n) -> p c n", c=4), wout_v[:, 4 * h:4 * h + 4, :],
                      st.dsem, w=[st])
                K.op("dve", lambda e: e.tensor_scalar(
                    out=wout_bf.t[:, 4 * h:4 * h + 4, :], in0=st.t[:, 0:4096].rearrange("p (c n) -> p c n", c=4),
                    scalar1=0.5, scalar2=None, op0=ALU.mult), r=[st], w=[wout_bf])
            wf2 = wf_sb.t[:].rearrange("c g d -> c (g d)")
            K.mm_group([lambda e: e.matmul(psb[0].t[:, :], lhsT=cs64.t[:, 0:128], rhs=wf2, start=True, stop=True)],
                       r=[cs64, wf_sb], w=[psb[0]])
            K.mm_group([lambda e: e.matmul(psb[1].t[:, :], lhsT=cs64.t[:, 128:256], rhs=wf2, start=True, stop=True)],
                       r=[cs64, wf_sb], w=[psb[1]])
            K.op("pool", lambda e: e.memset(BD.t[:], 0.0), w=[BD])
            for half in range(2):
                pr = slice(64 * half, 64 * half + 64)
                for t in range(2):
                    src = psb[t].t[pr, :].rearrange("p (j two d) -> p j two d", two=2, d=64)[:, :, half, :]
                    dst = BD.t[pr, :, 128 * t + 64 * half:128 * t + 64 * half + 64]
                    K.op("dve", lambda e: e.tensor_copy(out=dst, in_=src), r=[psb[t]], w=[BD])
            for j in range(4):
                pb = psb[2 + j % 2]
                pv = pb.t[:].bitcast(BF16)
                K.mm_group([(lambda e, kc=kc: e.transpose(out=pv[:, kc * 128:(kc + 1) * 128],
                                                          in_=wu_bf.t[:, kc, j * 128:(j + 1) * 128],
                                                          identity=identb.t[:])) for kc in range(8)],
                           r=[wu_bf, identb], w=[pb])
                K.op("act", lambda e: e.activation(out=wuT.t[:, j, :], in_=pv, func=AF.Copy), r=[pb], w=[wuT])
            for kc in range(8):
                pa, pb2 = psb[4 + 2 * (kc % 2)], psb[5 + 2 * (kc % 2)]
                for jj, pb in ((0, pa), (1, pb2)):
                    K.mm_group([(lambda e, j=j: e.matmul(pb.t[:, (j % 2) * 256:(j % 2) * 256 + 256],
                                                         lhsT=wuT.t[:, j, kc * 128:(kc + 1) * 128],
                                                         rhs=BD.t[:, j, :], start=True, stop=True))
                                for j in (2 * jj, 2 * jj + 1)], r=[wuT, BD], w=[pb])
                    for t in range(2):
                        src = pb.t[:, :].rearrange("p (j two d) -> p j two d", two=2, d=128)[:, :, t, :]
                        dst = w_bf.t[:, kc, 512 * t + 256 * jj:512 * t + 256 * jj + 256].rearrange(
                            "p (j d) -> p j d", d=128)
                        K.op("dve" if t == 0 else "act",
                             (lambda e: e.tensor_copy(out=dst, in_=src)) if t == 0 else
                             (lambda e: e.activation(out=dst, in_=src, func=AF.Copy)),
                             r=[pb], w=[w_bf])
        with ExitStack() as es1:
            xb = [K.sb(es1, "xb%d" % i, [128, D], F32, dma=True) for i in range(3)]
            xs = [K.sb(es1, "xs%d" % i, [128, D], BF16) for i in range(2)]
            xT = [K.sb(es1, "xT%d" % i, [128, D], BF16) for i in range(2)]
            ssx = [K.sb(es1, "ssx%d" % i, [128, 2], F32) for i in range(2)]
            pq_sb = [K.sb(es1, "pq_sb%d" % i, [128, 1024], BF16, dma=True) for i in range(2)]
            gf_sb = [K.sb(es1, "gf_sb%d" % i, [128, 512], BF16, dma=True) for i in range(2)]
            ga_sb = [K.sb(es1, "ga_sb%d" % i, [128, 512], BF16, dma=True) for i in range(2)]
            qk_sb = [K.sb(es1, "qk_sb%d" % i, [128, 3072], BF16, dma=True) for i in range(2)]
            v_sb = [K.sb(es1, "v_sb%d" % i, [128, 1536], BF16, dma=True) for i in range(2)]
            sq = [K.sb(es1, "sq%d" % i, [128, 512], BF16) for i in range(3)]
            th = [K.sb(es1, "th%d" % i, [128, 512], F32) for i in range(2)]
            ss8 = [K.sb(es1, "ss8_%d" % i, [128, 8], F32) for i in range(3)]
            rs8 = [K.sb(es1, "rs8_%d" % i, [128, 8], F32) for i in range(3)]
            xTp = psb[7]
            gps = psb[0:6]
            gcount = 0
            for i in range(NTILES_DBG):
                xbi, xsi, xTi, ssi = xb[i % 3], xs[i % 2], xT[i % 2], ssx[i % 2]
                K.dma("sp", xbi.t[:], x_d[128 * i:128 * i + 128, :], xbi.dsem, w=[xbi])
                K.op("act", lambda e: e.activation(out=xsi.t[:], in_=xbi.t[:], func=AF.Square,
                                                   accum_out=ssi.t[:, 0:1]), r=[xbi], w=[xsi, ssi])
                K.op("pool", lambda e: e.tensor_scalar(out=ssi.t[:, 1:2], in0=ssi.t[:, 0:1], scalar1=1.0 / D,
                                                       scalar2=EPS, op0=ALU.mult, op1=ALU.add), r=[ssi], w=[ssi])
                K.op("pool", lambda e: e.tensor_tensor(out=ssi.t[:, 1:2], in0=ssi.t[:, 1:2], in1=mhalf.t[:, 0:1],
                                                       op=ALU.pow), r=[ssi, mhalf], w=[ssi])
                K.op("dve", lambda e: e.tensor_scalar(out=xsi.t[:], in0=xbi.t[:], scalar1=ssi.t[:, 1:2],
                                                      scalar2=None, op0=ALU.mult), r=[xbi, ssi], w=[xsi])
                xTv = xTp.t[:].bitcast(BF16)
                K.mm_group([(lambda e, kc=kc: e.transpose(out=xTv[:, kc * 128:(kc + 1) * 128],
                                                          in_=xsi.t[:, kc * 128:(kc + 1) * 128],
                                                          identity=identb.t[:])) for kc in range(8)],
                           r=[xsi, identb], w=[xTp])
                K.op("dve", lambda e: e.tensor_copy(out=xTi.t[:], in_=xTv), r=[xTp], w=[xTi])
                o = i % 2
                for g in range(13):
                    pb = gps[gcount % 6]
                    gcount += 1
                    K.mm_group([(lambda e, kc=kc: e.matmul(pb.t[:, :], lhsT=xTi.t[:, kc * 128:(kc + 1) * 128],
                                                           rhs=w_bf.t[:, kc, g * 512:(g + 1) * 512],
                                                           start=(kc == 0), stop=(kc == 7))) for kc in range(8)],
                               r=[xTi, w_bf], w=[pb])
                    if g < 2:
                        dst = pq_sb[o]
                        K.op("act", lambda e: e.activation(out=dst.t[:, g * 512:(g + 1) * 512], in_=pb.t[:, :],
                                                           func=AF.Copy), r=[pb], w=[dst])
                    elif g == 2 or g == 12:
                        dst = gf_sb[o] if g == 2 else ga_sb[o]
                        tb = th[0 if g == 2 else 1]
                        K.op("act", lambda e: e.activation(out=tb.t[:], in_=pb.t[:, :], func=AF.Tanh, scale=0.5),
                             r=[pb], w=[tb])
                        K.op("dve", lambda e: e.scalar_tensor_tensor(out=dst.t[:], in0=tb.t[:], scalar=1.0,
                                                                     in1=pb.t[:, :], op0=ALU.add, op1=ALU.mult),
                             r=[tb, pb], w=[dst])
                    elif g < 9:
                        k3 = gcount % 3
                        sqb, s8, r8 = sq[k3], ss8[k3], rs8[k3]
                        dst = qk_sb[o]
                        K.op("act", lambda e: e.activation(out=sqb.t[:], in_=pb.t[:, :], func=AF.Square),
                             r=[pb], w=[sqb])
                        K.op("dve", lambda e: e.tensor_reduce(out=s8.t[:], in_=sqb.t[:].rearrange(
                            "p (h d) -> p h d", d=64), axis=AX.X, op=ALU.add), r=[sqb], w=[s8])
                        K.op("pool", lambda e: e.tensor_scalar(out=r8.t[:], in0=s8.t[:], scalar1=1.0 / 64,
                                                               scalar2=EPS, op0=ALU.mult, op1=ALU.add),
                             r=[s8], w=[r8])
                        K.op("pool", lambda e: e.tensor_tensor(out=r8.t[:], in0=r8.t[:], in1=mhalf.t[:, 0:8],
                                                               op=ALU.pow), r=[r8, mhalf], w=[r8])
                        K.op("dve", lambda e: e.tensor_tensor(
                            out=dst.t[:, (g - 3) * 512:(g - 2) * 512].rearrange("p (h d) -> p h d", d=64),
                            in0=pb.t[:, :].rearrange("p (h d) -> p h d", d=64),
                            in1=bc(r8.t[:].unsqueeze(2), [128, 8, 64]), op=ALU.mult), r=[pb, r8], w=[dst])
                    else:
                        dst = v_sb[o]
                        K.op("act", lambda e: e.activation(out=dst.t[:, (g - 9) * 512:(g - 8) * 512],
                                                           in_=pb.t[:, :], func=AF.Copy), r=[pb], w=[dst])
                rows = slice(128 * i, 128 * i + 128)
                K.dma("pool", PQ_d[rows, :], pq_sb[o].t[:], pq_sb[o].dsem, r=[pq_sb[o]])
                K.dma("pool", GF_d[rows, :], gf_sb[o].t[:], gf_sb[o].dsem, r=[gf_sb[o]])
                K.dma("pool", GA_d[rows, :], ga_sb[o].t[:], ga_sb[o].dsem, r=[ga_sb[o]])
                K.dma("pool", QK_d[rows, :], qk_sb[o].t[:], qk_sb[o].dsem, r=[qk_sb[o]])
                K.dma("pool", V_d[rows, :], v_sb[o].t[:], v_sb[o].dsem, r=[v_sb[o]])
            K.drain()


    fT = K.sb(es0, "fT", [128, 4, S], BF16)
    if STOP >= 2 and not SKIP01:
        with ExitStack() as es:
            tabs = [K.sb(es, "tab%d" % t, [128, 8192], BF16, dma=True) for t in range(3)]
            zin = [K.sb(es, "zin%d" % i, [128, 8, 1024], BF16, dma=True) for i in range(2)]
            ysb = [K.sb(es, "ysb%d" % i, [128, 2, 512], BF16, dma=True) for i in range(4)]
            for t, td in enumerate((tabA_d, tabB_d, tabC_d)):
                K.dma("sp", tabs[t].t[:], td[:, :], tabs[t].dsem, w=[tabs[t]])
            PQv = PQ_d.rearrange("(s1 s2) c -> s1 s2 c", s2=64)
            for s2c in range(8):
                z = zin[s2c % 2]
                K.dma("sp", z.t[:], PQv[:, s2c * 8:(s2c + 1) * 8, :], z.dsem, w=[z])
                for s2l in range(8):
                    s2 = s2c * 8 + s2l
                    cs = slice(s2 * 128, (s2 + 1) * 128)
                    A, B, C = tabs[0].t[:, cs], tabs[1].t[:, cs], tabs[2].t[:, cs]
                    P, Q = z.t[:, s2l, 0:512], z.t[:, s2l, 512:1024]
                    pr, pi = psb[(2 * s2) % 8], psb[(2 * s2 + 1) % 8]
                    K.mm_group([lambda e: e.matmul(pr.t[:, :], lhsT=A, rhs=P, start=True, stop=False),
                                lambda e: e.matmul(pr.t[:, :], lhsT=C, rhs=Q, start=False, stop=True)],
                               r=[tabs[0], tabs[2], z], w=[pr])
                    K.mm_group([lambda e: e.matmul(pi.t[:, :], lhsT=B, rhs=P, start=True, stop=False),
                                lambda e: e.matmul(pi.t[:, :], lhsT=A, rhs=Q, start=False, stop=True)],
                               r=[tabs[0], tabs[1], z], w=[pi])
                    yb = ysb[s2 % 4]
                    K.op("act", lambda e: e.activation(out=yb.t[:, 0, :], in_=pr.t[:, :], func=AF.Copy), r=[pr], w=[yb])
                    K.op("dve", lambda e: e.tensor_copy(out=yb.t[:, 1, :], in_=pi.t[:, :]), r=[pi], w=[yb])
                    K.dma("pool", Y_d[s2], yb.t[:], yb.dsem, r=[yb])
            K.drain()
        with ExitStack() as es:
            cs2 = K.sb(es, "cs2", [128, 64], BF16, dma=True)
            K.dma("sp", cs2.t[:], cs2_d[:, :], cs2.dsem, w=[cs2])
            yin_t = [es.enter_context(nc.sbuf_tensor("sb_yin%d" % i, [128, 16, 512], BF16)) for i in range(2)]
            yin = [[Buf(yin_t[i], K.new_dsem()) for _ in range(2)] for i in range(2)]
            for i in range(2):
                K.allbufs.extend(yin[i])
            cnt = 0
            for k1c in range(8):
                yt = yin_t[k1c % 2]
                yl, yh = yin[k1c % 2]
                for ri, yb in ((0, yl), (1, yh)):
                    K.dma("sp", yt[64 * ri:64 * ri + 64, :, :], Y_d[:, k1c * 16:(k1c + 1) * 16, ri, :], yb.dsem, w=[yb])
                for chc in range(4):
                    for hh in range(2):
                        pb = psb[cnt % 8]
                        K.mm_group([(lambda e, k=k: e.matmul(pb.t[:, k * 64:(k + 1) * 64],
                                                             lhsT=yt[:, hh * 8 + k, chc * 128:(chc + 1) * 128],
                                                             rhs=cs2.t[:, :], start=True, stop=True)) for k in range(8)],
                                   r=[yl, yh, cs2], w=[pb])
                        k10 = k1c * 16 + hh * 8
                        dst = fT.t[:, chc, :].rearrange("p (k2 k1) -> p k1 k2", k1=128)[:, k10:k10 + 8, :]
                        src = pb.t[:, :].rearrange("p (k1 k2) -> p k1 k2", k2=64)
                        if cnt % 2 == 0:
                            K.op("act", lambda e: e.activation(out=dst, in_=src, func=AF.Copy), r=[pb])
                        else:
                            K.op("dve", lambda e: e.tensor_copy(out=dst, in_=src), r=[pb])
                        cnt += 1
            K.drain()

    if STOP >= 3:
        with ExitStack() as es:
            em = K.sb(es, "em", [128, 12, 512], BF16, dma=True)
            gqT = K.sb(es, "gqT", [128, 12], F32, dma=True)
            gkT = K.sb(es, "gkT", [128, 12], F32, dma=True)
            GT = K.sb(es, "GT", [128, 12], F32)
            K.dma("sp", em.t[:], em_d.rearrange("p (a b) -> p a b", b=512), em.dsem, w=[em])
            K.dma("sp", gqT.t[:], qn_d.rearrange("(pr h2) d -> (h2 d) pr", h2=2), gqT.dsem, w=[gqT],
                  allow_slow_non_contiguous=True)
            K.dma("sp", gkT.t[:], kn_d.rearrange("(pr h2) d -> (h2 d) pr", h2=2), gkT.dsem, w=[gkT],
                  allow_slow_non_contiguous=True)
            K.op("dve", lambda e: e.scalar_tensor_tensor(out=GT.t[:], in0=gqT.t[:], scalar=0.125, in1=gkT.t[:],
                                                         op0=ALU.mult, op1=ALU.mult), r=[gqT, gkT], w=[GT])
            qsb = [K.sb(es, "qsb%d" % i, [128, 512], BF16, dma=True) for i in range(3)]
            ksb = [K.sb(es, "ksb%d" % i, [128, 512], BF16, dma=True) for i in range(4)]
            vsb = [K.sb(es, "vsb%d" % i, [128, 8, 65], BF16, dma=True) for i in range(4)]
            qTb = [K.sb(es, "qT%d" % i, [128, 2, 4, 128], BF16) for i in range(2)]
            for qz in qTb:
                K.op("pool", lambda e: e.memset(qz.t[:], 0.0), w=[qz])
            kTb = [K.sb(es, "kT%d" % i, [128, 4, 128], BF16) for i in range(4)]
            pex = [K.sb(es, "pex%d" % i, [128, 512], BF16) for i in range(3)]
            pmk = [K.sb(es, "pmk%d" % i, [128, 512], BF16) for i in range(8)]
            osb = [K.sb(es, "osb%d" % i, [128, 520], F32, dma=True) for i in range(2)]
            tq_ps, tk_ps = psb[0], psb[1]
            S_ps = [psb[2], psb[3]]
            O_ps = [[psb[4], psb[5]], [psb[6], psb[7]]]
            blk = 0
            scnt = 0
            for c, dil in enumerate(DILS if P3MODE > 0 else ()):
                L = S // dil
                nqb = L // 128
                QKv = QK_d.rearrange("(i d) c -> d i c", d=dil)
                Vv = V_d.rearrange("(i d) c -> d i c", d=dil)
                Ov = O_d[c].rearrange("(i d) e -> d i e", d=dil)
                for r in range(dil if P3MODE > 1 else 1):
                    def load_q(qb):
                        qb_ = qsb[qb % 3]
                        K.dma("sp", qb_.t[:], QKv[r, 128 * qb:128 * qb + 128, c * 512:(c + 1) * 512], qb_.dsem, w=[qb_])

                    def load_kv(kt):
                        kb, vb = ksb[kt % 4], vsb[kt % 4]
                        a, b = 128 * kt - 64, 128 * kt + 64
                        p0, p1 = 0, 128
                        if kt == 0:
                            a, p0 = 0, 64
                        if kt == nqb:
                            b, p1 = L, 64
                        if kt == 0 or kt == nqb:
                            K.op("pool", lambda e: e.memset(kb.t[:], 0.0), w=[kb])
                            K.op("pool", lambda e: e.memset(vb.t[:], 0.0), w=[vb])
                        K.op("pool", lambda e: e.memset(vb.t[p0:p1, :, 64:65], 1.0), w=[vb])
                        K.dma("sp", kb.t[p0:p1, :], QKv[r, a:b, 1536 + c * 512:1536 + (c + 1) * 512], kb.dsem, w=[kb])
                        K.dma("sp", vb.t[p0:p1, :, 0:64], Vv[r, a:b, c * 512:(c + 1) * 512].rearrange(
                            "n (h d) -> n h d", d=64), vb.dsem, w=[vb])

                    def tr_q(qb):
                        qb_, qt = qsb[qb % 3], qTb[qb % 2]
                        tv = tq_ps.t[:].bitcast(BF16)
                        K.mm_group([(lambda e, pr=pr: e.transpose(out=tv[:, pr * 128:(pr + 1) * 128],
                                                                  in_=qb_.t[:, pr * 128:(pr + 1) * 128],
                                                                  identity=identb.t[:])) for pr in range(4)],
                                   r=[qb_, identb], w=[tq_ps])
                        for h2 in range(2):
                            rs = slice(64 * h2, 64 * h2 + 64)
                            K.op("act", lambda e: e.activation(out=qt.t[rs, h2, :, :].rearrange("p a b -> p (a b)"),
                                                               in_=tv[rs, 0:512], func=AF.Copy), r=[tq_ps], w=[qt])

                    def tr_k(kt):
                        kb, kt_ = ksb[kt % 4], kTb[kt % 4]
                        tv = tk_ps.t[:].bitcast(BF16)
                        K.mm_group([(lambda e, pr=pr: e.transpose(out=tv[:, pr * 128:(pr + 1) * 128],
                                                                  in_=kb.t[:, pr * 128:(pr + 1) * 128],
                                                                  identity=identb.t[:])) for pr in range(4)],
                                   r=[kb, identb], w=[tk_ps])
                        K.op("dve", lambda e: e.tensor_tensor(
                            out=kt_.t[:], in0=tv[:, 0:512].rearrange("p (a b) -> p a b", b=128),
                            in1=bc(GT.t[:, 4 * c:4 * c + 4].unsqueeze(2), [128, 4, 128]), op=ALU.mult),
                            r=[tk_ps, GT], w=[kt_])

                    load_q(0)
                    load_kv(0)
                    load_kv(1)
                    if nqb > 1:
                        load_q(1)
                    if nqb >= 2:
                        load_kv(2)
                    tr_q(0)
                    tr_k(0)
                    tr_k(1)
                    for qb in range(nqb if P3MODE > 1 else 2):
                        if qb + 2 < nqb:
                            load_q(qb + 2)
                        if qb + 3 <= nqb:
                            load_kv(qb + 3)
                        qt, klo, khi = qTb[qb % 2], kTb[qb % 4], kTb[(qb + 1) % 4]
                        vlo, vhi = vsb[qb % 4], vsb[(qb + 1) % 4]
                        Ob = O_ps[blk % 2]
                        ob = osb[blk % 2]
                        blk += 1
                        pms = []
                        for pr in range(4):
                            Sp = S_ps[scnt % 2]
                            px, pm = pex[scnt % 3], pmk[(blk % 2) * 4 + pr]
                            scnt += 1
                            fns = []
                            for h2 in range(2):
                                rs = slice(64 * h2, 64 * h2 + 64)
                                fns.append(lambda e, h2=h2, rs=rs: e.matmul(
                                    Sp.t[:, h2 * 256:h2 * 256 + 128], lhsT=klo.t[:, pr, :], rhs=qt.t[:, h2, pr, :],
                                    start=True, stop=True))
                                fns.append(lambda e, h2=h2, rs=rs: e.matmul(
                                    Sp.t[:, h2 * 256 + 128:h2 * 256 + 256], lhsT=khi.t[:, pr, :], rhs=qt.t[:, h2, pr, :],
                                    start=True, stop=True))
                            if P3X < 2:
                                pms.append(pm)
                                continue
                            K.mm_group(fns, r=[klo, khi, qt], w=[Sp])
                            K.op("act", lambda e: e.activation(out=px.t[:], in_=Sp.t[:, :], func=AF.Exp), r=[Sp], w=[px])
                            K.op("dve", lambda e: e.tensor_tensor(out=pm.t[:], in0=px.t[:], in1=em.t[:, 4 * c + pr, :],
                                                                  op=ALU.mult), r=[px, em], w=[pm])
                            pms.append(pm)
                            if pr == 1:
                                if qb + 1 < nqb:
                                    tr_q(qb + 1)
                                if qb + 2 <= nqb:
                                    tr_k(qb + 2)
                        for pr in range(4 if P3X >= 3 else 0):
                            pm = pms[pr]
                            for h2 in range(2):
                                h = 2 * pr + h2
                                Obk = Ob[h // 4]
                                oc = slice((h % 4) * 65, (h % 4) * 65 + 65)
                                K.mm_group([
                                    lambda e: e.matmul(Obk.t[:, oc], lhsT=pm.t[:, h2 * 256:h2 * 256 + 128],
                                                       rhs=vlo.t[:, h, :], start=True, stop=False),
                                    lambda e: e.matmul(Obk.t[:, oc], lhsT=pm.t[:, h2 * 256 + 128:h2 * 256 + 256],
                                                       rhs=vhi.t[:, h, :], start=False, stop=True)],
                                    r=[pm, vlo, vhi], w=[Obk])
                        K.op("act", lambda e: e.activation(out=ob.t[:, 0:260], in_=Ob[0].t[:, 0:260], func=AF.Copy),
                             r=[Ob[0]], w=[ob])
                        K.op("dve", lambda e: e.tensor_copy(out=ob.t[:, 260:520], in_=Ob[1].t[:, 0:260]),
                             r=[Ob[1]], w=[ob])
                        K.dma("pool", Ov[r, 128 * qb:128 * qb + 128, :], ob.t[:], ob.dsem, r=[ob])
            K.drain()

    if STOP >= 4:
        with ExitStack() as es:
            o_in = [[K.sb(es, "oin%d_%d" % (c, i), [128, 520], F32, dma=True) for c in range(3)] for i in range(2)]
            ga_in = [K.sb(es, "ga_in%d" % i, [128, 512], BF16, dma=True) for i in range(2)]
            gf_in = [K.sb(es, "gf_in%d" % i, [128, 512], BF16, dma=True) for i in range(2)]
            xr = [K.sb(es, "xr%d" % i, [128, D], F32, dma=True) for i in range(2)]
            osum = [K.sb(es, "osum%d" % i, [128, 520], F32) for i in range(2)]
            rden = [K.sb(es, "rden%d" % i, [128, 8], F32) for i in range(2)]
            ya32 = [K.sb(es, "ya32_%d" % i, [128, 512], F32) for i in range(2)]
            ya_bf = [K.sb(es, "ya_bf%d" % i, [128, 512], BF16) for i in range(2)]
            yT = [K.sb(es, "yT%d" % i, [128, 8, 128], BF16) for i in range(2)]
            out_sb = [K.sb(es, "out_sb%d" % i, [128, D], F32, dma=True) for i in range(2)]
            tpa, tpg = psb[0], psb[1]
            outp = [[psb[2], psb[3]], [psb[4], psb[5]]]
            for i in range(NT):
                o = i % 2
                rows = slice(128 * i, 128 * i + 128)
                for c in range(3):
                    K.dma("sp", o_in[o][c].t[:], O_d[c][rows, :], o_in[o][c].dsem, w=[o_in[o][c]])
                K.dma("sp", ga_in[o].t[:], GA_d[rows, :], ga_in[o].dsem, w=[ga_in[o]])
                K.dma("sp", gf_in[o].t[:], GF_d[rows, :], gf_in[o].dsem, w=[gf_in[o]])
                K.dma("sp", xr[o].t[:], x_d[rows, :], xr[o].dsem, w=[xr[o]])
                os_, rd, y32, ybf, yt = osum[o], rden[o], ya32[o], ya_bf[o], yT[o]
                K.op("dve", lambda e: e.tensor_tensor(out=os_.t[:], in0=o_in[o][0].t[:], in1=o_in[o][1].t[:], op=ALU.add),
                     r=[o_in[o][0], o_in[o][1]], w=[os_])
                K.op("dve", lambda e: e.tensor_tensor(out=os_.t[:], in0=os_.t[:], in1=o_in[o][2].t[:], op=ALU.add),
                     r=[os_, o_in[o][2]], w=[os_])
                osv = os_.t[:].rearrange("p (h e) -> p h e", e=65)
                K.op("dve", lambda e: e.reciprocal(out=rd.t[:], in_=osv[:, :, 64]), r=[os_], w=[rd])
                K.op("dve", lambda e: e.tensor_tensor(out=y32.t[:].rearrange("p (h d) -> p h d", d=64),
                                                      in0=osv[:, :, 0:64], in1=bc(rd.t[:].unsqueeze(2), [128, 8, 64]),
                                                      op=ALU.mult), r=[os_, rd], w=[y32])
                K.op("pool", lambda e: e.tensor_tensor(out=ybf.t[:], in0=y32.t[:], in1=ga_in[o].t[:], op=ALU.mult),
                     r=[y32, ga_in[o]], w=[ybf])
                tav = tpa.t[:].bitcast(BF16)
                tgv = tpg.t[:].bitcast(BF16)
                K.mm_group([(lambda e, j=j: e.transpose(out=tav[:, j * 128:(j + 1) * 128],
                                                        in_=ybf.t[:, j * 128:(j + 1) * 128], identity=identb.t[:]))
                            for j in range(4)], r=[ybf, identb], w=[tpa])
                K.op("act", lambda e: e.activation(out=yt.t[:, 4:8, :].rearrange("p a b -> p (a b)"), in_=tav[:, 0:512],
                                                   func=AF.Copy), r=[tpa], w=[yt])
                K.mm_group([(lambda e, j=j: e.transpose(out=tgv[:, j * 128:(j + 1) * 128],
                                                        in_=gf_in[o].t[:, j * 128:(j + 1) * 128], identity=identb.t[:]))
                            for j in range(4)], r=[gf_in[o], identb], w=[tpg])
                K.op("dve", lambda e: e.tensor_tensor(out=yt.t[:, 0:4, :],
                                                      in0=tgv[:, 0:512].rearrange("p (a b) -> p a b", b=128),
                                                      in1=fT.t[:, :, 128 * i:128 * i + 128], op=ALU.mult),
                     r=[tpg], w=[yt])
                for half in range(2):
                    pb = outp[o][half]
                    K.mm_group([(lambda e, cc=cc: e.matmul(pb.t[:, :], lhsT=yt.t[:, cc, :],
                                                           rhs=wout_bf.t[:, cc, half * 512:(half + 1) * 512],
                                                           start=(cc == 0), stop=(cc == 7))) for cc in range(8)],
                               r=[yt, wout_bf], w=[pb])
                    K.op("dve", lambda e: e.tensor_tensor(out=out_sb[o].t[:, half * 512:(half + 1) * 512],
                                                          in0=pb.t[:, :], in1=xr[o].t[:, half * 512:(half + 1) * 512],
                                                          op=ALU.add), r=[pb, xr[o]], w=[out_sb[o]])
                K.dma("pool", y_d[rows, :], out_sb[o].t[:], out_sb[o].dsem, r=[out_sb[o]])
    K.drain()
    K.root.close()
    return nc


_CONSTS = None


def kernel(x, norm_w, w_in, q_norm_w, k_norm_w, w_fourier, w_out):
    global _CONSTS
    if _CONSTS is None:
        _CONSTS = _consts()
    nc = build()
    x = np.ascontiguousarray(np.asarray(x, dtype=np.float32))
    shared = {
        "norm_w": np.asarray(norm_w, np.float32), "w_in": np.asarray(w_in, np.float32),
        "q_norm_w": np.asarray(q_norm_w, np.float32), "k_norm_w": np.asarray(k_norm_w, np.float32),
        "w_fourier": np.asarray(w_fourier, np.float32), "w_out": np.asarray(w_out, np.float32),
    }
    shared.update(_CONSTS)
    in_maps = [dict(shared, x=x[b]) for b in range(NCORES)]
    res = run_bass_kernel_spmd(nc, in_maps, core_ids=list(range(NCORES)))
    if DEBUG:
        kernel.last = res
    ys = [np.asarray(r["y"]) for r in res.results]
    ys += [np.zeros_like(ys[0])] * (8 - len(ys))
    return np.stack(ys, axis=0).astype(np.float32)
```

```python
import os
from contextlib import ExitStack

import ml_dtypes
import numpy as np

import concourse.bass as bass
import concourse.mybir as mybir
from concourse.alu_op_type import AluOpType as ALU
from concourse.bass_utils import run_bass_kernel_spmd

F32 = mybir.dt.float32
BF16 = mybir.dt.bfloat16
AF = mybir.ActivationFunctionType
AX = mybir.AxisListType

S = 8192
D = 1024
NT = S // 128
INW = 6144
NCOL = 6656
DILS = (1, 4, 16)
EPS = 1e-6
DEBUG = bool(int(os.environ.get("MK_DEBUG", "0")))
NTILES_DBG = int(os.environ.get("MK_NT", str(NT)))
STOP = int(os.environ.get("MK_STOP", "9"))
NCORES = int(os.environ.get("MK_CORES", "8"))
P3MODE = int(os.environ.get("MK_P3", "9"))
P3X = int(os.environ.get("MK_P3X", "9"))
SKIP01 = bool(int(os.environ.get("MK_SKIP01", "0")))


def _consts():
    bf = ml_dtypes.bfloat16
    c = {}
    c["identb"] = np.eye(128, dtype=np.float32).astype(bf)
    kap = 1.0 / np.sqrt(S * 64.0)
    cc = np.arange(64)
    ang = 2 * np.pi * np.outer(cc, cc) / 64.0
    cos2 = np.concatenate([np.cos(ang), np.cos(ang)], axis=1) * kap
    sin2 = np.concatenate([np.sin(ang), np.sin(ang)], axis=1) * kap
    c["cs64"] = np.concatenate([cos2, sin2], axis=1).astype(np.float32)
    s1 = np.arange(128)[:, None, None].astype(np.float64)
    s2 = np.arange(64)[None, :, None].astype(np.float64)
    k1 = np.arange(128)[None, None, :].astype(np.float64)
    ph = 2 * np.pi * (s1 * k1 / 128.0 + s2 * k1 / 8192.0)
    A = np.cos(ph)
    B = np.sin(ph)
    c["tabA"] = A.reshape(128, 8192).astype(np.float32).astype(bf)
    c["tabB"] = B.reshape(128, 8192).astype(np.float32).astype(bf)
    c["tabC"] = (-B).reshape(128, 8192).astype(np.float32).astype(bf)
    s2v = np.arange(64)[:, None].astype(np.float64)
    k2v = np.arange(64)[None, :].astype(np.float64)
    ph2 = 2 * np.pi * s2v * k2v / 64.0
    c["cs2"] = np.concatenate([np.cos(ph2), -np.sin(ph2)], axis=0).astype(np.float32).astype(bf)
    j = np.arange(128)[:, None].astype(np.float64)
    i = np.arange(128)[None, :].astype(np.float64)
    em = np.zeros((128, 3, 4, 2, 2, 128), dtype=np.float64)
    for ci, dil in enumerate(DILS):
        for h in range(8):
            slope = 2.0 ** (-(h + 1))
            lo = np.where(j >= i, np.exp(-slope * dil * np.abs(j - 64 - i)), 0.0)
            hi = np.where(j <= i, np.exp(-slope * dil * np.abs(64 + j - i)), 0.0)
            em[:, ci, h // 2, h % 2, 0, :] = lo
            em[:, ci, h // 2, h % 2, 1, :] = hi
    c["emask"] = em.reshape(128, 12 * 512).astype(np.float32).astype(bf)
    return c


class Sem:
    def __init__(self, K, name):
        self.h = K.root.enter_context(K.nc.semaphore(name))
        self.n = 0


class Buf:
    def __init__(self, t, dsem=None):
        self.t = t
        self.w = None
        self.r = []
        self.dsem = dsem


class KB:
    def __init__(self, nc):
        self.nc = nc
        self.root = ExitStack()
        self.E = {"pe": nc.tensor, "act": nc.scalar, "dve": nc.vector, "pool": nc.gpsimd, "sp": nc.sync}
        self.esem = {e: Sem(self, "e_" + e) for e in ("pe", "act", "dve", "pool")}
        self.waited = {e: {} for e in self.E}
        self.dpool = [Sem(self, "d%d" % i) for i in range(72)]
        self.dnext = 0
        self.phase_sem = Sem(self, "phase")
        self.allbufs = []

    def sb(self, es, name, shape, dt, dma=False):
        t = es.enter_context(self.nc.sbuf_tensor("sb_" + name, list(shape), dt))
        b = Buf(t, self.new_dsem() if dma else None)
        self.allbufs.append(b)
        return b

    def ps(self, es, name):
        t = es.enter_context(self.nc.psum_tensor("ps_" + name, [128, 512], F32))
        b = Buf(t)
        self.allbufs.append(b)
        return b

    def new_dsem(self):
        s = self.dpool[self.dnext]
        self.dnext += 1
        return s

    def _waits(self, eng, r, w):
        need = {}

        def add(ev):
            if ev is None:
                return
            s, v = ev
            if need.get(s, 0) < v:
                need[s] = v

        for b in r:
            add(b.w)
        for b in w:
            add(b.w)
            for ev in b.r:
                add(ev)
        for s, v in need.items():
            if self.waited[eng].get(s, 0) < v:
                self.E[eng].wait_ge(s.h, v)
                self.waited[eng][s] = v

    def _record(self, ev, r, w):
        for b in r:
            b.r.append(ev)
        for b in w:
            b.w = ev
            b.r = []

    def op(self, eng, fn, r=(), w=()):
        self._waits(eng, r, w)
        ins = fn(self.E[eng])
        s = self.esem[eng]
        ins.then_inc(s.h, 1)
        s.n += 1
        self._record((s, s.n), r, w)
        return ins

    def mm_group(self, fns, r=(), w=()):
        self._waits("pe", r, w)
        ins = None
        for fn in fns:
            ins = fn(self.nc.tensor)
        s = self.esem["pe"]
        ins.then_inc(s.h, 1)
        s.n += 1
        self._record((s, s.n), r, w)

    def dma(self, q, out, in_, sem, r=(), w=(), **kw):
        self._waits(q, r, w)
        ins = self.E[q].dma_start(out=out, in_=in_, **kw)
        ins.then_inc(sem.h, 16)
        sem.n += 16
        self._record((sem, sem.n), r, w)

    def drain(self):
        sp = self.nc.sync
        for s in list(self.esem.values()) + self.dpool[: self.dnext]:
            if s.n > 0 and self.waited["sp"].get(s, 0) < s.n:
                sp.wait_ge(s.h, s.n)
                self.waited["sp"][s] = s.n
        ps = self.phase_sem
        sp.sem_inc(ps.h, 1)
        ps.n += 1
        for e in ("pe", "act", "dve", "pool"):
            self.E[e].wait_ge(ps.h, ps.n)
            for s in list(self.esem.values()) + self.dpool[: self.dnext]:
                self.waited[e][s] = s.n
        for b in self.allbufs:
            b.w = None
            b.r = []
        self.allbufs = []
        self.dnext = 0


def bc(ap, shape):
    return ap.to_broadcast(list(shape))


def build():
    nc = bass.Bass("TRN2", target_bir_lowering=False)
    K = KB(nc)
    dt_in = lambda n, s, d=F32: nc.dram_tensor(n, list(s), d, kind="ExternalInput").ap()
    x_d = dt_in("x", [S, D])
    nw_d = dt_in("norm_w", [D])
    win_d = dt_in("w_in", [D, INW])
    qn_d = dt_in("q_norm_w", [24, 64])
    kn_d = dt_in("k_norm_w", [24, 64])
    wf_d = dt_in("w_fourier", [8, 64, 64])
    wout_d = dt_in("w_out", [D, D])
    identb_d = dt_in("identb", [128, 128], BF16)
    cs64_d = dt_in("cs64", [64, 256])
    tabA_d = dt_in("tabA", [128, 8192], BF16)
    tabB_d = dt_in("tabB", [128, 8192], BF16)
    tabC_d = dt_in("tabC", [128, 8192], BF16)
    cs2_d = dt_in("cs2", [128, 64], BF16)
    em_d = dt_in("emask", [128, 12 * 512], BF16)
    y_d = nc.dram_tensor("y", [S, D], F32, kind="ExternalOutput").ap()
    skind = "ExternalOutput" if DEBUG else "Internal"
    scr = lambda n, s, d: nc.dram_tensor(n, list(s), d, kind=skind).ap()
    PQ_d = scr("PQ_s", [S, 1024], BF16)
    GF_d = scr("GF_s", [S, 512], BF16)
    GA_d = scr("GA_s", [S, 512], BF16)
    QK_d = scr("QK_s", [S, 3072], BF16)
    V_d = scr("V_s", [S, 1536], BF16)
    Y_d = scr("Y_s", [64, 128, 2, 512], BF16)
    O_d = [scr("O%d_s" % c, [S, 520], F32) for c in range(3)]

    es0 = K.root
    identb = K.sb(es0, "identb", [128, 128], BF16, dma=True)
    mhalf = K.sb(es0, "mhalf", [128, 48], F32)
    psb = [K.ps(es0, "psb%d" % i) for i in range(8)]
    wout_bf = K.sb(es0, "wout_bf", [128, 8, 1024], BF16)
    K.dma("sp", identb.t[:], identb_d[:, :], identb.dsem, w=[identb])
    K.op("pool", lambda e: e.memset(mhalf.t[:], -0.5), w=[mhalf])

    with ExitStack() as es:
      if not SKIP01:
        w_bf = K.sb(es, "w_bf", [128, 8, NCOL], BF16)
        with ExitStack() as es_p0:
            nw = K.sb(es_p0, "nw", [128, 8], F32, dma=True)
            stage = [K.sb(es_p0, "stage%d" % i, [128, INW], F32, dma=True) for i in range(2)]
            wu_bf = K.sb(es_p0, "wu_bf", [128, 8, 512], BF16)
            wuT = K.sb(es_p0, "wuT", [128, 4, 1024], BF16)
            cs64 = K.sb(es_p0, "cs64", [64, 256], F32, dma=True)
            wf_sb = K.sb(es_p0, "wf_sb", [64, 8, 64], F32, dma=True)
            BD = K.sb(es_p0, "BD", [128, 4, 256], BF16)
            K.dma("sp", nw.t[:], nw_d.rearrange("(kc p) -> p kc", p=128), nw.dsem, w=[nw],
                  allow_slow_non_contiguous=True)
            K.dma("sp", cs64.t[:], cs64_d[:, :], cs64.dsem, w=[cs64])
            K.dma("sp", wf_sb.t[:], wf_d.rearrange("g c d -> c g d"), wf_sb.dsem, w=[wf_sb])
            win_v = win_d.rearrange("(kc p) n -> p kc n", p=128)
            for kc in range(8):
                st = stage[kc % 2]
                K.dma("sp", st.t[:], win_v[:, kc, :], st.dsem, w=[st])
                sc = nw.t[:, kc:kc + 1]
                K.op("dve", lambda e: e.tensor_scalar(out=w_bf.t[:, kc, 1024:4096], in0=st.t[:, 512:3584],
                                                      scalar1=sc, scalar2=None, op0=ALU.mult),
                     r=[st, nw], w=[w_bf])
                K.op("act", lambda e: e.activation(out=w_bf.t[:, kc, 4096:NCOL], in_=st.t[:, 3584:INW],
                                                   func=AF.Copy, scale=sc), r=[st, nw], w=[w_bf])
                K.op("pool", lambda e: e.tensor_scalar(out=wu_bf.t[:, kc, :], in0=st.t[:, 0:512],
                                                       scalar1=sc, scalar2=None, op0=ALU.mult),
                     r=[st, nw], w=[wu_bf])
            wout_v = wout_d.rearrange("(cc p) n -> p cc n", p=128)
            for h in range(2):
                st = stage[h]
                K.dma("sp", st.t[:, 0:4096].rearrange("p (c n) -> p c n", c=4), wout_v[:, 4 * h:4 * h + 4, :],
                      st.dsem, w=[st])
                K.op("dve", lambda e: e.tensor_scalar(
                    out=wout_bf.t[:, 4 * h:4 * h + 4, :], in0=st.t[:, 0:4096].rearrange("p (c n) -> p c n", c=4),
                    scalar1=0.5, scalar2=None, op0=ALU.mult), r=[st], w=[wout_bf])
            wf2 = wf_sb.t[:].rearrange("c g d -> c (g d)")
            K.mm_group([lambda e: e.matmul(psb[0].t[:, :], lhsT=cs64.t[:, 0:128], rhs=wf2, start=True, stop=True)],
                       r=[cs64, wf_sb], w=[psb[0]])
            K.mm_group([lambda e: e.matmul(psb[1].t[:, :], lhsT=cs64.t[:, 128:256], rhs=wf2, start=True, stop=True)],
                       r=[cs64, wf_sb], w=[psb[1]])
            K.op("pool", lambda e: e.memset(BD.t[:], 0.0), w=[BD])
            for half in range(2):
                pr = slice(64 * half, 64 * half + 64)
                for t in range(2):
                    src = psb[t].t[pr, :].rearrange("p (j two d) -> p j two d", two=2, d=64)[:, :, half, :]
                    dst = BD.t[pr, :, 128 * t + 64 * half:128 * t + 64 * half + 64]
                    K.op("dve", lambda e: e.tensor_copy(out=dst, in_=src), r=[psb[t]], w=[BD])
            for j in range(4):
                pb = psb[2 + j % 2]
                pv = pb.t[:].bitcast(BF16)
                K.mm_group([(lambda e, kc=kc: e.transpose(out=pv[:, kc * 128:(kc + 1) * 128],
                                                          in_=wu_bf.t[:, kc, j * 128:(j + 1) * 128],
                                                          identity=identb.t[:])) for kc in range(8)],
                           r=[wu_bf, identb], w=[pb])
                K.op("act", lambda e: e.activation(out=wuT.t[:, j, :], in_=pv, func=AF.Copy), r=[pb], w=[wuT])
            for kc in range(8):
                pa, pb2 = psb[4 + 2 * (kc % 2)], psb[5 + 2 * (kc % 2)]
                for jj, pb in ((0, pa), (1, pb2)):
                    K.mm_group([(lambda e, j=j: e.matmul(pb.t[:, (j % 2) * 256:(j % 2) * 256 + 256],
                                                         lhsT=wuT.t[:, j, kc * 128:(kc + 1) * 128],
                                                         rhs=BD.t[:, j, :], start=True, stop=True))
                                for j in (2 * jj, 2 * jj + 1)], r=[wuT, BD], w=[pb])
                    for t in range(2):
                        src = pb.t[:, :].rearrange("p (j two d) -> p j two d", two=2, d=128)[:, :, t, :]
                        dst = w_bf.t[:, kc, 512 * t + 256 * jj:512 * t + 256 * jj + 256].rearrange(
                            "p (j d) -> p j d", d=128)
                        K.op("dve" if t == 0 else "act",
                             (lambda e: e.tensor_copy(out=dst, in_=src)) if t == 0 else
                             (lambda e: e.activation(out=dst, in_=src, func=AF.Copy)),
                             r=[pb], w=[w_bf])
        with ExitStack() as es1:
            xb = [K.sb(es1, "xb%d" % i, [128, D], F32, dma=True) for i in range(3)]
            xs = [K.sb(es1, "xs%d" % i, [128, D], BF16) for i in range(2)]
            xT = [K.sb(es1, "xT%d" % i, [128, D], BF16) for i in range(2)]
            ssx = [K.sb(es1, "ssx%d" % i, [128, 2], F32) for i in range(2)]
            pq_sb = [K.sb(es1, "pq_sb%d" % i, [128, 1024], BF16, dma=True) for i in range(2)]
            gf_sb = [K.sb(es1, "gf_sb%d" % i, [128, 512], BF16, dma=True) for i in range(2)]
            ga_sb = [K.sb(es1, "ga_sb%d" % i, [128, 512], BF16, dma=True) for i in range(2)]
            qk_sb = [K.sb(es1, "qk_sb%d" % i, [128, 3072], BF16, dma=True) for i in range(2)]
            v_sb = [K.sb(es1, "v_sb%d" % i, [128, 1536], BF16, dma=True) for i in range(2)]
            sq = [K.sb(es1, "sq%d" % i, [128, 512], BF16) for i in range(3)]
            th = [K.sb(es1, "th%d" % i, [128, 512], F32) for i in range(2)]
            ss8 = [K.sb(es1, "ss8_%d" % i, [128, 8], F32) for i in range(3)]
            rs8 = [K.sb(es1, "rs8_%d" % i, [128, 8], F32) for i in range(3)]
            xTp = psb[7]
            gps = psb[0:6]
            gcount = 0
            xTv = xTp.t[:].bitcast(BF16)

            def load_x(i):
                xbi = xb[i % 3]
                K.dma("sp", xbi.t[:], x_d[128 * i:128 * i + 128, :], xbi.dsem, w=[xbi])

            def xchain_a(i):
                xbi, xsi, ssi = xb[i % 3], xs[i % 2], ssx[i % 2]
                K.op("act", lambda e: e.activation(out=xsi.t[:], in_=xbi.t[:], func=AF.Square,
                                                   accum_out=ssi.t[:, 0:1]), r=[xbi], w=[xsi, ssi])
                K.op("pool", lambda e: e.tensor_scalar(out=ssi.t[:, 1:2], in0=ssi.t[:, 0:1], scalar1=1.0 / D,
                                                       scalar2=EPS, op0=ALU.mult, op1=ALU.add), r=[ssi], w=[ssi])
                K.op("pool", lambda e: e.tensor_tensor(out=ssi.t[:, 1:2], in0=ssi.t[:, 1:2], in1=mhalf.t[:, 0:1],
                                                       op=ALU.pow), r=[ssi, mhalf], w=[ssi])

            def xchain_b(i):
                xbi, xsi, ssi = xb[i % 3], xs[i % 2], ssx[i % 2]
                K.op("dve", lambda e: e.tensor_scalar(out=xsi.t[:], in0=xbi.t[:], scalar1=ssi.t[:, 1:2],
                                                      scalar2=None, op0=ALU.mult), r=[xbi, ssi], w=[xsi])

            def xtrans(i):
                xsi, xTi = xs[i % 2], xT[i % 2]
                K.mm_group([(lambda e, kc=kc: e.transpose(out=xTv[:, kc * 128:(kc + 1) * 128],
                                                          in_=xsi.t[:, kc * 128:(kc + 1) * 128],
                                                          identity=identb.t[:])) for kc in range(8)],
                           r=[xsi, identb], w=[xTp])
                K.op("dve", lambda e: e.tensor_copy(out=xTi.t[:], in_=xTv), r=[xTp], w=[xTi])

            load_x(0)
            if NTILES_DBG > 1:
                load_x(1)
            xchain_a(0)
            xchain_b(0)
            xtrans(0)
            pending = []

            def flush_pending():
                while pending:
                    pb_, r8_, dst_, g_ = pending.pop(0)
                    K.op("dve", lambda e: e.tensor_tensor(
                        out=dst_.t[:, (g_ - 3) * 512:(g_ - 2) * 512].rearrange("p (h d) -> p h d", d=64),
                        in0=pb_.t[:, :].rearrange("p (h d) -> p h d", d=64),
                        in1=bc(r8_.t[:].unsqueeze(2), [128, 8, 64]), op=ALU.mult), r=[pb_, r8_], w=[dst_])

            for i in range(NTILES_DBG):
                xTi = xT[i % 2]
                if i + 2 < NTILES_DBG:
                    load_x(i + 2)
                if i + 1 < NTILES_DBG:
                    xchain_a(i + 1)
                o = i % 2
                for g in range(13):
                    pb = gps[gcount % 6]
                    gcount += 1
                    K.mm_group([(lambda e, kc=kc: e.matmul(pb.t[:, :], lhsT=xTi.t[:, kc * 128:(kc + 1) * 128],
                                                           rhs=w_bf.t[:, kc, g * 512:(g + 1) * 512],
                                                           start=(kc == 0), stop=(kc == 7))) for kc in range(8)],
                               r=[xTi, w_bf], w=[pb])
                    if g < 2:
                        dst = pq_sb[o]
                        K.op("act", lambda e: e.activation(out=dst.t[:, g * 512:(g + 1) * 512], in_=pb.t[:, :],
                                                           func=AF.Copy), r=[pb], w=[dst])
                    elif g == 2 or g == 12:
                        dst = gf_sb[o] if g == 2 else ga_sb[o]
                        tb = th[0 if g == 2 else 1]
                        K.op("act", lambda e: e.activation(out=tb.t[:], in_=pb.t[:, :], func=AF.Tanh, scale=0.5),
                             r=[pb], w=[tb])
                        flush_pending()
                        K.op("dve", lambda e: e.scalar_tensor_tensor(out=dst.t[:], in0=tb.t[:], scalar=1.0,
                                                                     in1=pb.t[:, :], op0=ALU.add, op1=ALU.mult),
                             r=[tb, pb], w=[dst])
                    elif g < 9:
                        k3 = gcount % 3
                        sqb, s8, r8 = sq[k3], ss8[k3], rs8[k3]
                        dst = qk_sb[o]
                        K.op("act", lambda e: e.activation(out=sqb.t[:], in_=pb.t[:, :], func=AF.Square),
                             r=[pb], w=[sqb])
                        K.op("dve", lambda e: e.tensor_reduce(out=s8.t[:], in_=sqb.t[:].rearrange(
                            "p (h d) -> p h d", d=64), axis=AX.X, op=ALU.add), r=[sqb], w=[s8])
                        flush_pending()
                        K.op("pool", lambda e: e.tensor_scalar(out=r8.t[:], in0=s8.t[:], scalar1=1.0 / 64,
                                                               scalar2=EPS, op0=ALU.mult, op1=ALU.add),
                             r=[s8], w=[r8])
                        K.op("pool", lambda e: e.tensor_tensor(out=r8.t[:], in0=r8.t[:], in1=mhalf.t[:, 0:8],
                                                               op=ALU.pow), r=[r8, mhalf], w=[r8])
                        pending.append((pb, r8, dst, g))
                    else:
                        flush_pending()
                        dst = v_sb[o]
                        K.op("act", lambda e: e.activation(out=dst.t[:, (g - 9) * 512:(g - 8) * 512],
                                                           in_=pb.t[:, :], func=AF.Copy), r=[pb], w=[dst])
                    if g == 3 and i + 1 < NTILES_DBG:
                        xchain_b(i + 1)
                    if g == 6 and i + 1 < NTILES_DBG:
                        xtrans(i + 1)
                rows = slice(128 * i, 128 * i + 128)
                K.dma("sp", PQ_d[rows, :], pq_sb[o].t[:], pq_sb[o].dsem, r=[pq_sb[o]])
                K.dma("sp", GF_d[rows, :], gf_sb[o].t[:], gf_sb[o].dsem, r=[gf_sb[o]])
                K.dma("sp", GA_d[rows, :], ga_sb[o].t[:], ga_sb[o].dsem, r=[ga_sb[o]])
                K.dma("sp", QK_d[rows, :], qk_sb[o].t[:], qk_sb[o].dsem, r=[qk_sb[o]])
                K.dma("sp", V_d[rows, :], v_sb[o].t[:], v_sb[o].dsem, r=[v_sb[o]])
            K.drain()


    fT = K.sb(es0, "fT", [128, 4, S], BF16)
    if STOP >= 2 and not SKIP01:
        with ExitStack() as es:
            tabs = [K.sb(es, "tab%d" % t, [128, 8192], BF16, dma=True) for t in range(3)]
            zin = [K.sb(es, "zin%d" % i, [128, 8, 1024], BF16, dma=True) for i in range(2)]
            ysb = [K.sb(es, "ysb%d" % i, [128, 2, 512], BF16, dma=True) for i in range(4)]
            for t, td in enumerate((tabA_d, tabB_d, tabC_d)):
                K.dma("sp", tabs[t].t[:], td[:, :], tabs[t].dsem, w=[tabs[t]])
            PQv = PQ_d.rearrange("(s1 s2) c -> s1 s2 c", s2=64)
            for s2c in range(8):
                z = zin[s2c % 2]
                K.dma("sp", z.t[:], PQv[:, s2c * 8:(s2c + 1) * 8, :], z.dsem, w=[z])
                for s2l in range(8):
                    s2 = s2c * 8 + s2l
                    cs = slice(s2 * 128, (s2 + 1) * 128)
                    A, B, C = tabs[0].t[:, cs], tabs[1].t[:, cs], tabs[2].t[:, cs]
                    P, Q = z.t[:, s2l, 0:512], z.t[:, s2l, 512:1024]
                    pr, pi = psb[(2 * s2) % 8], psb[(2 * s2 + 1) % 8]
                    K.mm_group([lambda e: e.matmul(pr.t[:, :], lhsT=A, rhs=P, start=True, stop=False),
                                lambda e: e.matmul(pr.t[:, :], lhsT=C, rhs=Q, start=False, stop=True)],
                               r=[tabs[0], tabs[2], z], w=[pr])
                    K.mm_group([lambda e: e.matmul(pi.t[:, :], lhsT=B, rhs=P, start=True, stop=False),
                                lambda e: e.matmul(pi.t[:, :], lhsT=A, rhs=Q, start=False, stop=True)],
                               r=[tabs[0], tabs[1], z], w=[pi])
                    yb = ysb[s2 % 4]
                    K.op("act", lambda e: e.activation(out=yb.t[:, 0, :], in_=pr.t[:, :], func=AF.Copy), r=[pr], w=[yb])
                    K.op("dve", lambda e: e.tensor_copy(out=yb.t[:, 1, :], in_=pi.t[:, :]), r=[pi], w=[yb])
                    K.dma("sp", Y_d[s2], yb.t[:], yb.dsem, r=[yb])
            K.drain()
        with ExitStack() as es:
            cs2 = K.sb(es, "cs2", [128, 64], BF16, dma=True)
            K.dma("sp", cs2.t[:], cs2_d[:, :], cs2.dsem, w=[cs2])
            yin_t = [es.enter_context(nc.sbuf_tensor("sb_yin%d" % i, [128, 16, 512], BF16)) for i in range(2)]
            yin = [[Buf(yin_t[i], K.new_dsem()) for _ in range(2)] for i in range(2)]
            for i in range(2):
                K.allbufs.extend(yin[i])
            cnt = 0
            for k1c in range(8):
                yt = yin_t[k1c % 2]
                yl, yh = yin[k1c % 2]
                for ri, yb in ((0, yl), (1, yh)):
                    K.dma("sp", yt[64 * ri:64 * ri + 64, :, :], Y_d[:, k1c * 16:(k1c + 1) * 16, ri, :], yb.dsem, w=[yb])
                for chc in range(4):
                    for hh in range(2):
                        pb = psb[cnt % 8]
                        K.mm_group([(lambda e, k=k: e.matmul(pb.t[:, k * 64:(k + 1) * 64],
                                                             lhsT=yt[:, hh * 8 + k, chc * 128:(chc + 1) * 128],
                                                             rhs=cs2.t[:, :], start=True, stop=True)) for k in range(8)],
                                   r=[yl, yh, cs2], w=[pb])
                        k10 = k1c * 16 + hh * 8
                        dst = fT.t[:, chc, :].rearrange("p (k2 k1) -> p k1 k2", k1=128)[:, k10:k10 + 8, :]
                        src = pb.t[:, :].rearrange("p (k1 k2) -> p k1 k2", k2=64)
                        if cnt % 2 == 0:
                            K.op("act", lambda e: e.activation(out=dst, in_=src, func=AF.Copy), r=[pb])
                        else:
                            K.op("dve", lambda e: e.tensor_copy(out=dst, in_=src), r=[pb])
                        cnt += 1
            K.drain()

    if STOP >= 3:
        with ExitStack() as es:
            em = K.sb(es, "em", [128, 12, 512], BF16, dma=True)
            gqT = K.sb(es, "gqT", [128, 12], F32, dma=True)
            gkT = K.sb(es, "gkT", [128, 12], F32, dma=True)
            GT = K.sb(es, "GT", [128, 12], F32)
            K.dma("sp", em.t[:], em_d.rearrange("p (a b) -> p a b", b=512), em.dsem, w=[em])
            K.dma("sp", gqT.t[:], qn_d.rearrange("(pr h2) d -> (h2 d) pr", h2=2), gqT.dsem, w=[gqT],
                  allow_slow_non_contiguous=True)
            K.dma("sp", gkT.t[:], kn_d.rearrange("(pr h2) d -> (h2 d) pr", h2=2), gkT.dsem, w=[gkT],
                  allow_slow_non_contiguous=True)
            K.op("dve", lambda e: e.scalar_tensor_tensor(out=GT.t[:], in0=gqT.t[:], scalar=0.125, in1=gkT.t[:],
                                                         op0=ALU.mult, op1=ALU.mult), r=[gqT, gkT], w=[GT])
            qsb = [K.sb(es, "qsb%d" % i, [128, 512], BF16, dma=True) for i in range(3)]
            ksb = [K.sb(es, "ksb%d" % i, [128, 512], BF16, dma=True) for i in range(4)]
            vsb = [K.sb(es, "vsb%d" % i, [128, 8, 65], BF16, dma=True) for i in range(6)]
            qTb = [K.sb(es, "qT%d" % i, [128, 2, 4, 128], BF16) for i in range(2)]
            for qz in qTb:
                K.op("pool", lambda e: e.memset(qz.t[:], 0.0), w=[qz])
            kTb = [K.sb(es, "kT%d" % i, [128, 4, 128], BF16) for i in range(4)]
            pex = [K.sb(es, "pex%d" % i, [128, 512], BF16) for i in range(3)]
            pmk = [K.sb(es, "pmk%d" % i, [128, 512], BF16) for i in range(8)]
            osb = [K.sb(es, "osb%d" % i, [128, 520], F32, dma=True) for i in range(2)]
            blk = 0
            scnt = 0
            tq_ps, tk_ps = psb[0], psb[1]
            S_ps = [psb[2], psb[3]]
            O_ps = [[psb[4], psb[5]], [psb[6], psb[7]]]
            blk = 0
            scnt = 0
            for c, dil in enumerate(DILS if P3MODE > 0 else ()):
                L = S // dil
                nqb = L // 128
                QKv = QK_d.rearrange("(i d) c -> d i c", d=dil)
                Vv = V_d.rearrange("(i d) c -> d i c", d=dil)
                Ov = O_d[c].rearrange("(i d) e -> d i e", d=dil)
                for r in range(dil if P3MODE > 1 else 1):
                    def load_q(qb):
                        qb_ = qsb[qb % 3]
                        K.dma("sp", qb_.t[:], QKv[r, 128 * qb:128 * qb + 128, c * 512:(c + 1) * 512], qb_.dsem, w=[qb_])

                    def load_kv(kt):
                        kb, vb = ksb[kt % 4], vsb[kt % 6]
                        a, b = 128 * kt - 64, 128 * kt + 64
                        p0, p1 = 0, 128
                        if kt == 0:
                            a, p0 = 0, 64
                        if kt == nqb:
                            b, p1 = L, 64
                        if kt == 0 or kt == nqb:
                            K.op("pool", lambda e: e.memset(kb.t[:], 0.0), w=[kb])
                            K.op("pool", lambda e: e.memset(vb.t[:], 0.0), w=[vb])
                        K.op("pool", lambda e: e.memset(vb.t[p0:p1, :, 64:65], 1.0), w=[vb])
                        K.dma("sp", kb.t[p0:p1, :], QKv[r, a:b, 1536 + c * 512:1536 + (c + 1) * 512], kb.dsem, w=[kb])
                        K.dma("sp", vb.t[p0:p1, :, 0:64], Vv[r, a:b, c * 512:(c + 1) * 512].rearrange(
                            "n (h d) -> n h d", d=64), vb.dsem, w=[vb])

                    def tr_q(qb):
                        qb_, qt = qsb[qb % 3], qTb[qb % 2]
                        tv = tq_ps.t[:].bitcast(BF16)
                        K.mm_group([(lambda e, pr=pr: e.transpose(out=tv[:, pr * 128:(pr + 1) * 128],
                                                                  in_=qb_.t[:, pr * 128:(pr + 1) * 128],
                                                                  identity=identb.t[:])) for pr in range(4)],
                                   r=[qb_, identb], w=[tq_ps])
                        for h2 in range(2):
                            rs = slice(64 * h2, 64 * h2 + 64)
                            K.op("act", lambda e: e.activation(out=qt.t[rs, h2, :, :].rearrange("p a b -> p (a b)"),
                                                               in_=tv[rs, 0:512], func=AF.Copy), r=[tq_ps], w=[qt])

                    def tr_k(kt):
                        kb, kt_ = ksb[kt % 4], kTb[kt % 4]
                        tv = tk_ps.t[:].bitcast(BF16)
                        K.mm_group([(lambda e, pr=pr: e.transpose(out=tv[:, pr * 128:(pr + 1) * 128],
                                                                  in_=kb.t[:, pr * 128:(pr + 1) * 128],
                                                                  identity=identb.t[:])) for pr in range(4)],
                                   r=[kb, identb], w=[tk_ps])
                        K.op("dve", lambda e: e.tensor_tensor(
                            out=kt_.t[:], in0=tv[:, 0:512].rearrange("p (a b) -> p a b", b=128),
                            in1=bc(GT.t[:, 4 * c:4 * c + 4].unsqueeze(2), [128, 4, 128]), op=ALU.mult),
                            r=[tk_ps, GT], w=[kt_])

                    G0 = blk

                    def s_block(qb, prs):
                        nonlocal scnt
                        qt, klo, khi = qTb[qb % 2], kTb[qb % 4], kTb[(qb + 1) % 4]
                        for pr in prs:
                            Sp = S_ps[scnt % 2]
                            px, pm = pex[scnt % 3], pmk[((G0 + qb) % 2) * 4 + pr]
                            scnt += 1
                            fns = []
                            for h2 in range(2):
                                fns.append(lambda e, h2=h2: e.matmul(
                                    Sp.t[:, h2 * 256:h2 * 256 + 128], lhsT=klo.t[:, pr, :], rhs=qt.t[:, h2, pr, :],
                                    start=True, stop=True))
                                fns.append(lambda e, h2=h2: e.matmul(
                                    Sp.t[:, h2 * 256 + 128:h2 * 256 + 256], lhsT=khi.t[:, pr, :],
                                    rhs=qt.t[:, h2, pr, :], start=True, stop=True))
                            K.mm_group(fns, r=[klo, khi, qt], w=[Sp])
                            K.op("act", lambda e: e.activation(out=px.t[:], in_=Sp.t[:, :], func=AF.Exp),
                                 r=[Sp], w=[px])
                            K.op("dve", lambda e: e.tensor_tensor(out=pm.t[:], in0=px.t[:], in1=em.t[:, 4 * c + pr, :],
                                                                  op=ALU.mult), r=[px, em], w=[pm])

                    def pv_block(qb, prs):
                        vlo, vhi = vsb[qb % 6], vsb[(qb + 1) % 6]
                        Ob = O_ps[(G0 + qb) % 2]
                        for pr in prs:
                            pm = pmk[((G0 + qb) % 2) * 4 + pr]
                            for h2 in range(2):
                                h = 2 * pr + h2
                                Obk = Ob[h // 4]
                                oc = slice((h % 4) * 65, (h % 4) * 65 + 65)
                                K.mm_group([
                                    lambda e: e.matmul(Obk.t[:, oc], lhsT=pm.t[:, h2 * 256:h2 * 256 + 128],
                                                       rhs=vlo.t[:, h, :], start=True, stop=False),
                                    lambda e: e.matmul(Obk.t[:, oc], lhsT=pm.t[:, h2 * 256 + 128:h2 * 256 + 256],
                                                       rhs=vhi.t[:, h, :], start=False, stop=True)],
                                    r=[pm, vlo, vhi], w=[Obk])

                    nb = nqb if P3MODE > 1 else 2
                    load_q(0)
                    load_kv(0)
                    load_kv(1)
                    for t_ in (1, 2):
                        if t_ < nqb:
                            load_q(t_)
                    for t_ in (2, 3):
                        if t_ <= nqb:
                            load_kv(t_)
                    tr_q(0)
                    tr_k(0)
                    tr_k(1)
                    if 1 < nqb:
                        tr_q(1)
                    if 2 <= nqb:
                        tr_k(2)
                    s_block(0, (0, 1, 2, 3))
                    for qb in range(nb):
                        if qb + 3 < nqb:
                            load_q(qb + 3)
                        if qb + 4 <= nqb:
                            load_kv(qb + 4)
                        if qb + 2 < nqb:
                            tr_q(qb + 2)
                        if qb + 3 <= nqb:
                            tr_k(qb + 3)
                        ob = osb[(G0 + qb) % 2]
                        Ob = O_ps[(G0 + qb) % 2]
                        for half in range(2):
                            if qb + 1 < nb:
                                s_block(qb + 1, (2 * half, 2 * half + 1))
                            pv_block(qb, (2 * half, 2 * half + 1))
                        blk += 1
                        K.op("act", lambda e: e.activation(out=ob.t[:, 0:260], in_=Ob[0].t[:, 0:260], func=AF.Copy),
                             r=[Ob[0]], w=[ob])
                        K.op("dve", lambda e: e.tensor_copy(out=ob.t[:, 260:520], in_=Ob[1].t[:, 0:260]),
                             r=[Ob[1]], w=[ob])
                        K.dma("sp", Ov[r, 128 * qb:128 * qb + 128, :], ob.t[:], ob.dsem, r=[ob])
            K.drain()

    if STOP >= 4:
        with ExitStack() as es:
            o_in = [[K.sb(es, "oin%d_%d" % (c, i), [128, 520], F32, dma=True) for c in range(3)] for i in range(2)]
            ga_in = [K.sb(es, "ga_in%d" % i, [128, 512], BF16, dma=True) for i in range(2)]
            gf_in = [K.sb(es, "gf_in%d" % i, [128, 512], BF16, dma=True) for i in range(2)]
            xr = [K.sb(es, "xr%d" % i, [128, D], F32, dma=True) for i in range(2)]
            osum = [K.sb(es, "osum%d" % i, [128, 520], F32) for i in range(2)]
            rden = [K.sb(es, "rden%d" % i, [128, 8], F32) for i in range(2)]
            ya32 = [K.sb(es, "ya32_%d" % i, [128, 512], F32) for i in range(2)]
            ya_bf = [K.sb(es, "ya_bf%d" % i, [128, 512], BF16) for i in range(2)]
            yT = [K.sb(es, "yT%d" % i, [128, 8, 128], BF16) for i in range(2)]
            out_sb = [K.sb(es, "out_sb%d" % i, [128, D], F32, dma=True) for i in range(2)]
            tpa, tpg = psb[0], psb[1]
            outp = [[psb[2], psb[3]], [psb[4], psb[5]]]
            def load4(i):
                o = i % 2
                rows = slice(128 * i, 128 * i + 128)
                for c in range(3):
                    K.dma("sp", o_in[o][c].t[:], O_d[c][rows, :], o_in[o][c].dsem, w=[o_in[o][c]])
                K.dma("sp", ga_in[o].t[:], GA_d[rows, :], ga_in[o].dsem, w=[ga_in[o]])
                K.dma("sp", gf_in[o].t[:], GF_d[rows, :], gf_in[o].dsem, w=[gf_in[o]])
                K.dma("sp", xr[o].t[:], x_d[rows, :], xr[o].dsem, w=[xr[o]])

            load4(0)
            for i in range(NT):
                o = i % 2
                rows = slice(128 * i, 128 * i + 128)
                if i + 1 < NT:
                    load4(i + 1)
                os_, rd, y32, ybf, yt = osum[o], rden[o], ya32[o], ya_bf[o], yT[o]
                K.op("dve", lambda e: e.tensor_tensor(out=os_.t[:], in0=o_in[o][0].t[:], in1=o_in[o][1].t[:], op=ALU.add),
                     r=[o_in[o][0], o_in[o][1]], w=[os_])
                K.op("dve", lambda e: e.tensor_tensor(out=os_.t[:], in0=os_.t[:], in1=o_in[o][2].t[:], op=ALU.add),
                     r=[os_, o_in[o][2]], w=[os_])
                osv = os_.t[:].rearrange("p (h e) -> p h e", e=65)
                K.op("dve", lambda e: e.reciprocal(out=rd.t[:], in_=osv[:, :, 64]), r=[os_], w=[rd])
                K.op("dve", lambda e: e.tensor_tensor(out=y32.t[:].rearrange("p (h d) -> p h d", d=64),
                                                      in0=osv[:, :, 0:64], in1=bc(rd.t[:].unsqueeze(2), [128, 8, 64]),
                                                      op=ALU.mult), r=[os_, rd], w=[y32])
                K.op("pool", lambda e: e.tensor_tensor(out=ybf.t[:], in0=y32.t[:], in1=ga_in[o].t[:], op=ALU.mult),
                     r=[y32, ga_in[o]], w=[ybf])
                tav = tpa.t[:].bitcast(BF16)
                tgv = tpg.t[:].bitcast(BF16)
                K.mm_group([(lambda e, j=j: e.transpose(out=tav[:, j * 128:(j + 1) * 128],
                                                        in_=ybf.t[:, j * 128:(j + 1) * 128], identity=identb.t[:]))
                            for j in range(4)], r=[ybf, identb], w=[tpa])
                K.op("act", lambda e: e.activation(out=yt.t[:, 4:8, :].rearrange("p a b -> p (a b)"), in_=tav[:, 0:512],
                                                   func=AF.Copy), r=[tpa], w=[yt])
                K.mm_group([(lambda e, j=j: e.transpose(out=tgv[:, j * 128:(j + 1) * 128],
                                                        in_=gf_in[o].t[:, j * 128:(j + 1) * 128], identity=identb.t[:]))
                            for j in range(4)], r=[gf_in[o], identb], w=[tpg])
                K.op("dve", lambda e: e.tensor_tensor(out=yt.t[:, 0:4, :],
                                                      in0=tgv[:, 0:512].rearrange("p (a b) -> p a b", b=128),
                                                      in1=fT.t[:, :, 128 * i:128 * i + 128], op=ALU.mult),
                     r=[tpg], w=[yt])
                for half in range(2):
                    pb = outp[o][half]
                    K.mm_group([(lambda e, cc=cc: e.matmul(pb.t[:, :], lhsT=yt.t[:, cc, :],
                                                           rhs=wout_bf.t[:, cc, half * 512:(half + 1) * 512],
                                                           start=(cc == 0), stop=(cc == 7))) for cc in range(8)],
                               r=[yt, wout_bf], w=[pb])
                    K.op("dve", lambda e: e.tensor_tensor(out=out_sb[o].t[:, half * 512:(half + 1) * 512],
                                                          in0=pb.t[:, :], in1=xr[o].t[:, half * 512:(half + 1) * 512],
                                                          op=ALU.add), r=[pb, xr[o]], w=[out_sb[o]])
                K.dma("sp", y_d[rows, :], out_sb[o].t[:], out_sb[o].dsem, r=[out_sb[o]])
    K.drain()
    K.root.close()
    return nc


_CONSTS = None


def kernel(x, norm_w, w_in, q_norm_w, k_norm_w, w_fourier, w_out):
    global _CONSTS
    if _CONSTS is None:
        _CONSTS = _consts()
    nc = build()
    x = np.ascontiguousarray(np.asarray(x, dtype=np.float32))
    shared = {
        "norm_w": np.asarray(norm_w, np.float32), "w_in": np.asarray(w_in, np.float32),
        "q_norm_w": np.asarray(q_norm_w, np.float32), "k_norm_w": np.asarray(k_norm_w, np.float32),
        "w_fourier": np.asarray(w_fourier, np.float32), "w_out": np.asarray(w_out, np.float32),
    }
    shared.update(_CONSTS)
    in_maps = [dict(shared, x=x[b]) for b in range(NCORES)]
    res = run_bass_kernel_spmd(nc, in_maps, core_ids=list(range(NCORES)))
    if DEBUG:
        kernel.last = res
    ys = [np.asarray(r["y"]) for r in res.results]
    ys += [np.zeros_like(ys[0])] * (8 - len(ys))
    return np.stack(ys, axis=0).astype(np.float32)
```

```python
import os
from contextlib import ExitStack

import ml_dtypes
import numpy as np

import concourse.bass as bass
import concourse.mybir as mybir
from concourse.alu_op_type import AluOpType as ALU
from concourse.bass_utils import run_bass_kernel_spmd

F32 = mybir.dt.float32
BF16 = mybir.dt.bfloat16
AF = mybir.ActivationFunctionType
AX = mybir.AxisListType

S = 8192
D = 1024
NT = S // 128
INW = 6144
NCOL = 6656
DILS = (1, 4, 16)
EPS = 1e-6
DEBUG = bool(int(os.environ.get("MK_DEBUG", "0")))
NTILES_DBG = int(os.environ.get("MK_NT", str(NT)))
STOP = int(os.environ.get("MK_STOP", "9"))
NCORES = int(os.environ.get("MK_CORES", "8"))
P3MODE = int(os.environ.get("MK_P3", "9"))
P3X = int(os.environ.get("MK_P3X", "9"))
SKIP01 = bool(int(os.environ.get("MK_SKIP01", "0")))


def _consts():
    bf = ml_dtypes.bfloat16
    c = {}
    c["identb"] = np.eye(128, dtype=np.float32).astype(bf)
    kap = 1.0 / np.sqrt(S * 64.0)
    cc = np.arange(64)
    ang = 2 * np.pi * np.outer(cc, cc) / 64.0
    cos2 = np.concatenate([np.cos(ang), np.cos(ang)], axis=1) * kap
    sin2 = np.concatenate([np.sin(ang), np.sin(ang)], axis=1) * kap
    c["cs64"] = np.concatenate([cos2, sin2], axis=1).astype(np.float32)
    s1 = np.arange(128)[:, None, None].astype(np.float64)
    s2 = np.arange(64)[None, :, None].astype(np.float64)
    k1 = np.arange(128)[None, None, :].astype(np.float64)
    ph = 2 * np.pi * (s1 * k1 / 128.0 + s2 * k1 / 8192.0)
    A = np.cos(ph)
    B = np.sin(ph)
    c["tabA"] = A.reshape(128, 8192).astype(np.float32).astype(bf)
    c["tabB"] = B.reshape(128, 8192).astype(np.float32).astype(bf)
    c["tabC"] = (-B).reshape(128, 8192).astype(np.float32).astype(bf)
    s2v = np.arange(64)[:, None].astype(np.float64)
    k2v = np.arange(64)[None, :].astype(np.float64)
    ph2 = 2 * np.pi * s2v * k2v / 64.0
    c["cs2"] = np.concatenate([np.cos(ph2), -np.sin(ph2)], axis=0).astype(np.float32).astype(bf)
    j = np.arange(128)[:, None].astype(np.float64)
    i = np.arange(128)[None, :].astype(np.float64)
    em = np.zeros((128, 3, 4, 2, 2, 128), dtype=np.float64)
    for ci, dil in enumerate(DILS):
        for h in range(8):
            slope = 2.0 ** (-(h + 1))
            lo = np.where(j >= i, np.exp(-slope * dil * np.abs(j - 64 - i)), 0.0)
            hi = np.where(j <= i, np.exp(-slope * dil * np.abs(64 + j - i)), 0.0)
            em[:, ci, h // 2, h % 2, 0, :] = lo
            em[:, ci, h // 2, h % 2, 1, :] = hi
    c["emask"] = em.reshape(128, 12 * 512).astype(np.float32).astype(bf)
    return c


class Sem:
    def __init__(self, K, name):
        self.h = K.root.enter_context(K.nc.semaphore(name))
        self.n = 0


class Buf:
    def __init__(self, t, dsem=None):
        self.t = t
        self.w = None
        self.r = []
        self.dsem = dsem


class KB:
    def __init__(self, nc):
        self.nc = nc
        self.root = ExitStack()
        self.E = {"pe": nc.tensor, "act": nc.scalar, "dve": nc.vector, "pool": nc.gpsimd, "sp": nc.sync}
        self.esem = {e: Sem(self, "e_" + e) for e in ("pe", "act", "dve", "pool")}
        self.waited = {e: {} for e in self.E}
        self.dpool = [Sem(self, "d%d" % i) for i in range(72)]
        self.dnext = 0
        self.phase_sem = Sem(self, "phase")
        self.allbufs = []

    def sb(self, es, name, shape, dt, dma=False):
        t = es.enter_context(self.nc.sbuf_tensor("sb_" + name, list(shape), dt))
        b = Buf(t, self.new_dsem() if dma else None)
        self.allbufs.append(b)
        return b

    def ps(self, es, name):
        t = es.enter_context(self.nc.psum_tensor("ps_" + name, [128, 512], F32))
        b = Buf(t)
        self.allbufs.append(b)
        return b

    def new_dsem(self):
        s = self.dpool[self.dnext]
        self.dnext += 1
        return s

    def _waits(self, eng, r, w):
        need = {}

        def add(ev):
            if ev is None:
                return
            s, v = ev
            if need.get(s, 0) < v:
                need[s] = v

        for b in r:
            add(b.w)
        for b in w:
            add(b.w)
            for ev in b.r:
                add(ev)
        for s, v in need.items():
            if self.waited[eng].get(s, 0) < v:
                self.E[eng].wait_ge(s.h, v)
                self.waited[eng][s] = v

    def _record(self, ev, r, w):
        for b in r:
            b.r.append(ev)
        for b in w:
            b.w = ev
            b.r = []

    def op(self, eng, fn, r=(), w=()):
        self._waits(eng, r, w)
        ins = fn(self.E[eng])
        s = self.esem[eng]
        ins.then_inc(s.h, 1)
        s.n += 1
        self._record((s, s.n), r, w)
        return ins

    def mm_group(self, fns, r=(), w=()):
        self._waits("pe", r, w)
        ins = None
        for fn in fns:
            ins = fn(self.nc.tensor)
        s = self.esem["pe"]
        ins.then_inc(s.h, 1)
        s.n += 1
        self._record((s, s.n), r, w)

    def dma(self, q, out, in_, sem, r=(), w=(), **kw):
        self._waits(q, r, w)
        ins = self.E[q].dma_start(out=out, in_=in_, **kw)
        ins.then_inc(sem.h, 16)
        sem.n += 16
        self._record((sem, sem.n), r, w)

    def drain(self):
        sp = self.nc.sync
        for s in list(self.esem.values()) + self.dpool[: self.dnext]:
            if s.n > 0 and self.waited["sp"].get(s, 0) < s.n:
                sp.wait_ge(s.h, s.n)
                self.waited["sp"][s] = s.n
        ps = self.phase_sem
        sp.sem_inc(ps.h, 1)
        ps.n += 1
        for e in ("pe", "act", "dve", "pool"):
            self.E[e].wait_ge(ps.h, ps.n)
            for s in list(self.esem.values()) + self.dpool[: self.dnext]:
                self.waited[e][s] = s.n
        for b in self.allbufs:
            b.w = None
            b.r = []
        self.allbufs = []
        self.dnext = 0


def bc(ap, shape):
    return ap.to_broadcast(list(shape))


def build():
    nc = bass.Bass("TRN2", target_bir_lowering=False)
    K = KB(nc)
    dt_in = lambda n, s, d=F32: nc.dram_tensor(n, list(s), d, kind="ExternalInput").ap()
    x_d = dt_in("x", [S, D])
    nw_d = dt_in("norm_w", [D])
    win_d = dt_in("w_in", [D, INW])
    qn_d = dt_in("q_norm_w", [24, 64])
    kn_d = dt_in("k_norm_w", [24, 64])
    wf_d = dt_in("w_fourier", [8, 64, 64])
    wout_d = dt_in("w_out", [D, D])
    identb_d = dt_in("identb", [128, 128], BF16)
    cs64_d = dt_in("cs64", [64, 256])
    tabA_d = dt_in("tabA", [128, 8192], BF16)
    tabB_d = dt_in("tabB", [128, 8192], BF16)
    tabC_d = dt_in("tabC", [128, 8192], BF16)
    cs2_d = dt_in("cs2", [128, 64], BF16)
    em_d = dt_in("emask", [128, 12 * 512], BF16)
    y_d = nc.dram_tensor("y", [S, D], F32, kind="ExternalOutput").ap()
    skind = "ExternalOutput" if DEBUG else "Internal"
    scr = lambda n, s, d: nc.dram_tensor(n, list(s), d, kind=skind).ap()
    PQ_d = scr("PQ_s", [S, 1024], BF16)
    GF_d = scr("GF_s", [S, 512], BF16)
    GA_d = scr("GA_s", [S, 512], BF16)
    QK_d = scr("QK_s", [S, 3072], BF16)
    V_d = scr("V_s", [S, 1536], BF16)
    Y_d = scr("Y_s", [64, 128, 2, 512], BF16)
    O_d = [scr("O%d_s" % c, [S, 520], F32) for c in range(3)]

    es0 = K.root
    identb = K.sb(es0, "identb", [128, 128], BF16, dma=True)
    mhalf = K.sb(es0, "mhalf", [128, 48], F32)
    psb = [K.ps(es0, "psb%d" % i) for i in range(8)]
    wout_bf = K.sb(es0, "wout_bf", [128, 8, 1024], BF16)
    K.dma("sp", identb.t[:], identb_d[:, :], identb.dsem, w=[identb])
    K.op("pool", lambda e: e.memset(mhalf.t[:], -0.5), w=[mhalf])

    with ExitStack() as es:
      if not SKIP01:
        w_bf = K.sb(es, "w_bf", [128, 8, NCOL], BF16)
        with ExitStack() as es_p0:
            nw = K.sb(es_p0, "nw", [128, 8], F32, dma=True)
            stage = [K.sb(es_p0, "stage%d" % i, [128, INW], F32, dma=True) for i in range(2)]
            wu_bf = K.sb(es_p0, "wu_bf", [128, 8, 512], BF16)
            wuT = K.sb(es_p0, "wuT", [128, 4, 1024], BF16)
            cs64 = K.sb(es_p0, "cs64", [64, 256], F32, dma=True)
            wf_sb = K.sb(es_p0, "wf_sb", [64, 8, 64], F32, dma=True)
            BD = K.sb(es_p0, "BD", [128, 4, 256], BF16)
            K.dma("sp", nw.t[:], nw_d.rearrange("(kc p) -> p kc", p=128), nw.dsem, w=[nw],
                  allow_slow_non_contiguous=True)
            K.dma("sp", cs64.t[:], cs64_d[:, :], cs64.dsem, w=[cs64])
            K.dma("sp", wf_sb.t[:], wf_d.rearrange("g c d -> c g d"), wf_sb.dsem, w=[wf_sb])
            win_v = win_d.rearrange("(kc p) n -> p kc n", p=128)
            for kc in range(8):
                st = stage[kc % 2]
                K.dma("sp", st.t[:], win_v[:, kc, :], st.dsem, w=[st])
                sc = nw.t[:, kc:kc + 1]
                K.op("dve", lambda e: e.tensor_scalar(out=w_bf.t[:, kc, 1024:4096], in0=st.t[:, 512:3584],
                                                      scalar1=sc, scalar2=None, op0=ALU.mult),
                     r=[st, nw], w=[w_bf])
                K.op("act", lambda e: e.activation(out=w_bf.t[:, kc, 4096:NCOL], in_=st.t[:, 3584:INW],
                                                   func=AF.Copy, scale=sc), r=[st, nw], w=[w_bf])
                K.op("pool", lambda e: e.tensor_scalar(out=wu_bf.t[:, kc, :], in0=st.t[:, 0:512],
                                                       scalar1=sc, scalar2=None, op0=ALU.mult),
                     r=[st, nw], w=[wu_bf])
            wout_v = wout_d.rearrange("(cc p) n -> p cc n", p=128)
            for h in range(2):
                st = stage[h]
                K.dma("sp", st.t[:, 0:4096].rearrange("p (c n) -> p c n", c=4), wout_v[:, 4 * h:4 * h + 4, :],
                      st.dsem, w=[st])
                K.op("dve", lambda e: e.tensor_scalar(
                    out=wout_bf.t[:, 4 * h:4 * h + 4, :], in0=st.t[:, 0:4096].rearrange("p (c n) -> p c n", c=4),
                    scalar1=0.5, scalar2=None, op0=ALU.mult), r=[st], w=[wout_bf])
            wf2 = wf_sb.t[:].rearrange("c g d -> c (g d)")
            K.mm_group([lambda e: e.matmul(psb[0].t[:, :], lhsT=cs64.t[:, 0:128], rhs=wf2, start=True, stop=True)],
                       r=[cs64, wf_sb], w=[psb[0]])
            K.mm_group([lambda e: e.matmul(psb[1].t[:, :], lhsT=cs64.t[:, 128:256], rhs=wf2, start=True, stop=True)],
                       r=[cs64, wf_sb], w=[psb[1]])
            K.op("pool", lambda e: e.memset(BD.t[:], 0.0), w=[BD])
            for half in range(2):
                pr = slice(64 * half, 64 * half + 64)
                for t in range(2):
                    src = psb[t].t[pr, :].rearrange("p (j two d) -> p j two d", two=2, d=64)[:, :, half, :]
                    dst = BD.t[pr, :, 128 * t + 64 * half:128 * t + 64 * half + 64]
                    K.op("dve", lambda e: e.tensor_copy(out=dst, in_=src), r=[psb[t]], w=[BD])
            for j in range(4):
                pb = psb[2 + j % 2]
                pv = pb.t[:].bitcast(BF16)
                K.mm_group([(lambda e, kc=kc: e.transpose(out=pv[:, kc * 128:(kc + 1) * 128],
                                                          in_=wu_bf.t[:, kc, j * 128:(j + 1) * 128],
                                                          identity=identb.t[:])) for kc in range(8)],
                           r=[wu_bf, identb], w=[pb])
                K.op("act", lambda e: e.activation(out=wuT.t[:, j, :], in_=pv, func=AF.Copy), r=[pb], w=[wuT])
            for kc in range(8):
                pa, pb2 = psb[4 + 2 * (kc % 2)], psb[5 + 2 * (kc % 2)]
                for jj, pb in ((0, pa), (1, pb2)):
                    K.mm_group([(lambda e, j=j: e.matmul(pb.t[:, (j % 2) * 256:(j % 2) * 256 + 256],
                                                         lhsT=wuT.t[:, j, kc * 128:(kc + 1) * 128],
                                                         rhs=BD.t[:, j, :], start=True, stop=True))
                                for j in (2 * jj, 2 * jj + 1)], r=[wuT, BD], w=[pb])
                    for t in range(2):
                        src = pb.t[:, :].rearrange("p (j two d) -> p j two d", two=2, d=128)[:, :, t, :]
                        dst = w_bf.t[:, kc, 512 * t + 256 * jj:512 * t + 256 * jj + 256].rearrange(
                            "p (j d) -> p j d", d=128)
                        K.op("dve" if t == 0 else "act",
                             (lambda e: e.tensor_copy(out=dst, in_=src)) if t == 0 else
                             (lambda e: e.activation(out=dst, in_=src, func=AF.Copy)),
                             r=[pb], w=[w_bf])
        with ExitStack() as es1:
            xb = [K.sb(es1, "xb%d" % i, [128, D], F32, dma=True) for i in range(3)]
            xs = [K.sb(es1, "xs%d" % i, [128, D], BF16) for i in range(2)]
            xT = [K.sb(es1, "xT%d" % i, [128, D], BF16) for i in range(2)]
            ssx = [K.sb(es1, "ssx%d" % i, [128, 2], F32) for i in range(2)]
            pq_sb = [K.sb(es1, "pq_sb%d" % i, [128, 1024], BF16, dma=True) for i in range(2)]
            gf_sb = [K.sb(es1, "gf_sb%d" % i, [128, 512], BF16, dma=True) for i in range(2)]
            ga_sb = [K.sb(es1, "ga_sb%d" % i, [128, 512], BF16, dma=True) for i in range(2)]
            qk_sb = [K.sb(es1, "qk_sb%d" % i, [128, 3072], BF16, dma=True) for i in range(2)]
            v_sb = [K.sb(es1, "v_sb%d" % i, [128, 1536], BF16, dma=True) for i in range(2)]
            sq = [K.sb(es1, "sq%d" % i, [128, 512], BF16) for i in range(3)]
            th = [K.sb(es1, "th%d" % i, [128, 512], F32) for i in range(2)]
            ss8 = [K.sb(es1, "ss8_%d" % i, [128, 8], F32) for i in range(3)]
            rs8 = [K.sb(es1, "rs8_%d" % i, [128, 8], F32) for i in range(3)]
            xTp = psb[7]
            gps = psb[0:6]
            gcount = 0
            xTv = xTp.t[:].bitcast(BF16)

            def load_x(i):
                xbi = xb[i % 3]
                K.dma("sp", xbi.t[:], x_d[128 * i:128 * i + 128, :], xbi.dsem, w=[xbi])

            def xchain_a(i):
                xbi, xsi, ssi = xb[i % 3], xs[i % 2], ssx[i % 2]
                K.op("act", lambda e: e.activation(out=xsi.t[:], in_=xbi.t[:], func=AF.Square,
                                                   accum_out=ssi.t[:, 0:1]), r=[xbi], w=[xsi, ssi])
                K.op("pool", lambda e: e.tensor_scalar(out=ssi.t[:, 1:2], in0=ssi.t[:, 0:1], scalar1=1.0 / D,
                                                       scalar2=EPS, op0=ALU.mult, op1=ALU.add), r=[ssi], w=[ssi])
                K.op("pool", lambda e: e.tensor_tensor(out=ssi.t[:, 1:2], in0=ssi.t[:, 1:2], in1=mhalf.t[:, 0:1],
                                                       op=ALU.pow), r=[ssi, mhalf], w=[ssi])

            def xchain_b(i):
                xbi, xsi, ssi = xb[i % 3], xs[i % 2], ssx[i % 2]
                K.op("dve", lambda e: e.tensor_scalar(out=xsi.t[:], in0=xbi.t[:], scalar1=ssi.t[:, 1:2],
                                                      scalar2=None, op0=ALU.mult), r=[xbi, ssi], w=[xsi])

            def xtrans(i):
                xsi, xTi = xs[i % 2], xT[i % 2]
                K.mm_group([(lambda e, kc=kc: e.transpose(out=xTv[:, kc * 128:(kc + 1) * 128],
                                                          in_=xsi.t[:, kc * 128:(kc + 1) * 128],
                                                          identity=identb.t[:])) for kc in range(8)],
                           r=[xsi, identb], w=[xTp])
                K.op("dve", lambda e: e.tensor_copy(out=xTi.t[:], in_=xTv), r=[xTp], w=[xTi])

            load_x(0)
            if NTILES_DBG > 1:
                load_x(1)
            xchain_a(0)
            xchain_b(0)
            xtrans(0)
            pending = []

            def flush_pending():
                while pending:
                    pb_, r8_, dst_, g_ = pending.pop(0)
                    K.op("dve", lambda e: e.tensor_tensor(
                        out=dst_.t[:, (g_ - 3) * 512:(g_ - 2) * 512].rearrange("p (h d) -> p h d", d=64),
                        in0=pb_.t[:, :].rearrange("p (h d) -> p h d", d=64),
                        in1=bc(r8_.t[:].unsqueeze(2), [128, 8, 64]), op=ALU.mult), r=[pb_, r8_], w=[dst_])

            for i in range(NTILES_DBG):
                xTi = xT[i % 2]
                if i + 2 < NTILES_DBG:
                    load_x(i + 2)
                if i + 1 < NTILES_DBG:
                    xchain_a(i + 1)
                o = i % 2
                for g in range(13):
                    pb = gps[gcount % 6]
                    gcount += 1
                    K.mm_group([(lambda e, kc=kc: e.matmul(pb.t[:, :], lhsT=xTi.t[:, kc * 128:(kc + 1) * 128],
                                                           rhs=w_bf.t[:, kc, g * 512:(g + 1) * 512],
                                                           start=(kc == 0), stop=(kc == 7))) for kc in range(8)],
                               r=[xTi, w_bf], w=[pb])
                    if g < 2:
                        dst = pq_sb[o]
                        K.op("act", lambda e: e.activation(out=dst.t[:, g * 512:(g + 1) * 512], in_=pb.t[:, :],
                                                           func=AF.Copy), r=[pb], w=[dst])
                    elif g == 2 or g == 12:
                        dst = gf_sb[o] if g == 2 else ga_sb[o]
                        tb = th[0 if g == 2 else 1]
                        K.op("act", lambda e: e.activation(out=tb.t[:], in_=pb.t[:, :], func=AF.Tanh, scale=0.5),
                             r=[pb], w=[tb])
                        flush_pending()
                        K.op("dve", lambda e: e.scalar_tensor_tensor(out=dst.t[:], in0=tb.t[:], scalar=1.0,
                                                                     in1=pb.t[:, :], op0=ALU.add, op1=ALU.mult),
                             r=[tb, pb], w=[dst])
                    elif g < 9:
                        k3 = gcount % 3
                        sqb, s8, r8 = sq[k3], ss8[k3], rs8[k3]
                        dst = qk_sb[o]
                        K.op("act", lambda e: e.activation(out=sqb.t[:], in_=pb.t[:, :], func=AF.Square),
                             r=[pb], w=[sqb])
                        K.op("dve", lambda e: e.tensor_reduce(out=s8.t[:], in_=sqb.t[:].rearrange(
                            "p (h d) -> p h d", d=64), axis=AX.X, op=ALU.add), r=[sqb], w=[s8])
                        flush_pending()
                        K.op("pool", lambda e: e.tensor_scalar(out=r8.t[:], in0=s8.t[:], scalar1=1.0 / 64,
                                                               scalar2=EPS, op0=ALU.mult, op1=ALU.add),
                             r=[s8], w=[r8])
                        K.op("pool", lambda e: e.tensor_tensor(out=r8.t[:], in0=r8.t[:], in1=mhalf.t[:, 0:8],
                                                               op=ALU.pow), r=[r8, mhalf], w=[r8])
                        pending.append((pb, r8, dst, g))
                    else:
                        flush_pending()
                        dst = v_sb[o]
                        K.op("act", lambda e: e.activation(out=dst.t[:, (g - 9) * 512:(g - 8) * 512],
                                                           in_=pb.t[:, :], func=AF.Copy), r=[pb], w=[dst])
                    if g == 3 and i + 1 < NTILES_DBG:
                        xchain_b(i + 1)
                    if g == 6 and i + 1 < NTILES_DBG:
                        xtrans(i + 1)
                rows = slice(128 * i, 128 * i + 128)
                K.dma("sp", PQ_d[rows, :], pq_sb[o].t[:], pq_sb[o].dsem, r=[pq_sb[o]])
                K.dma("sp", GF_d[rows, :], gf_sb[o].t[:], gf_sb[o].dsem, r=[gf_sb[o]])
                K.dma("sp", GA_d[rows, :], ga_sb[o].t[:], ga_sb[o].dsem, r=[ga_sb[o]])
                K.dma("sp", QK_d[rows, :], qk_sb[o].t[:], qk_sb[o].dsem, r=[qk_sb[o]])
                K.dma("sp", V_d[rows, :], v_sb[o].t[:], v_sb[o].dsem, r=[v_sb[o]])
            K.drain()


    fT = K.sb(es0, "fT", [128, 4, S], BF16)
    if STOP >= 2 and not SKIP01:
        with ExitStack() as es:
            tabs = [K.sb(es, "tab%d" % t, [128, 8192], BF16, dma=True) for t in range(3)]
            zin = [K.sb(es, "zin%d" % i, [128, 8, 1024], BF16, dma=True) for i in range(2)]
            ysb = [K.sb(es, "ysb%d" % i, [128, 2, 512], BF16, dma=True) for i in range(4)]
            for t, td in enumerate((tabA_d, tabB_d, tabC_d)):
                K.dma("sp", tabs[t].t[:], td[:, :], tabs[t].dsem, w=[tabs[t]])
            PQv = PQ_d.rearrange("(s1 s2) c -> s1 s2 c", s2=64)
            for s2c in range(8):
                z = zin[s2c % 2]
                K.dma("sp", z.t[:], PQv[:, s2c * 8:(s2c + 1) * 8, :], z.dsem, w=[z])
                for s2l in range(8):
                    s2 = s2c * 8 + s2l
                    cs = slice(s2 * 128, (s2 + 1) * 128)
                    A, B, C = tabs[0].t[:, cs], tabs[1].t[:, cs], tabs[2].t[:, cs]
                    P, Q = z.t[:, s2l, 0:512], z.t[:, s2l, 512:1024]
                    pr, pi = psb[(2 * s2) % 8], psb[(2 * s2 + 1) % 8]
                    K.mm_group([lambda e: e.matmul(pr.t[:, :], lhsT=A, rhs=P, start=True, stop=False),
                                lambda e: e.matmul(pr.t[:, :], lhsT=C, rhs=Q, start=False, stop=True)],
                               r=[tabs[0], tabs[2], z], w=[pr])
                    K.mm_group([lambda e: e.matmul(pi.t[:, :], lhsT=B, rhs=P, start=True, stop=False),
                                lambda e: e.matmul(pi.t[:, :], lhsT=A, rhs=Q, start=False, stop=True)],
                               r=[tabs[0], tabs[1], z], w=[pi])
                    yb = ysb[s2 % 4]
                    K.op("act", lambda e: e.activation(out=yb.t[:, 0, :], in_=pr.t[:, :], func=AF.Copy), r=[pr], w=[yb])
                    K.op("dve", lambda e: e.tensor_copy(out=yb.t[:, 1, :], in_=pi.t[:, :]), r=[pi], w=[yb])
                    K.dma("sp", Y_d[s2], yb.t[:], yb.dsem, r=[yb])
            K.drain()
        with ExitStack() as es:
            cs2 = K.sb(es, "cs2", [128, 64], BF16, dma=True)
            K.dma("sp", cs2.t[:], cs2_d[:, :], cs2.dsem, w=[cs2])
            yin_t = [es.enter_context(nc.sbuf_tensor("sb_yin%d" % i, [128, 16, 512], BF16)) for i in range(2)]
            yin = [[Buf(yin_t[i], K.new_dsem()) for _ in range(2)] for i in range(2)]
            for i in range(2):
                K.allbufs.extend(yin[i])
            cnt = 0
            for k1c in range(8):
                yt = yin_t[k1c % 2]
                yl, yh = yin[k1c % 2]
                for ri, yb in ((0, yl), (1, yh)):
                    K.dma("sp", yt[64 * ri:64 * ri + 64, :, :], Y_d[:, k1c * 16:(k1c + 1) * 16, ri, :], yb.dsem, w=[yb])
                for chc in range(4):
                    for hh in range(2):
                        pb = psb[cnt % 8]
                        K.mm_group([(lambda e, k=k: e.matmul(pb.t[:, k * 64:(k + 1) * 64],
                                                             lhsT=yt[:, hh * 8 + k, chc * 128:(chc + 1) * 128],
                                                             rhs=cs2.t[:, :], start=True, stop=True)) for k in range(8)],
                                   r=[yl, yh, cs2], w=[pb])
                        k10 = k1c * 16 + hh * 8
                        dst = fT.t[:, chc, :].rearrange("p (k2 k1) -> p k1 k2", k1=128)[:, k10:k10 + 8, :]
                        src = pb.t[:, :].rearrange("p (k1 k2) -> p k1 k2", k2=64)
                        if cnt % 2 == 0:
                            K.op("act", lambda e: e.activation(out=dst, in_=src, func=AF.Copy), r=[pb])
                        else:
                            K.op("dve", lambda e: e.tensor_copy(out=dst, in_=src), r=[pb])
                        cnt += 1
            K.drain()

    if STOP >= 3:
        with ExitStack() as es:
            em = K.sb(es, "em", [128, 12, 512], BF16, dma=True)
            gqT = K.sb(es, "gqT", [128, 12], F32, dma=True)
            gkT = K.sb(es, "gkT", [128, 12], F32, dma=True)
            GT = K.sb(es, "GT", [128, 12], F32)
            K.dma("sp", em.t[:], em_d.rearrange("p (a b) -> p a b", b=512), em.dsem, w=[em])
            K.dma("sp", gqT.t[:], qn_d.rearrange("(pr h2) d -> (h2 d) pr", h2=2), gqT.dsem, w=[gqT],
                  allow_slow_non_contiguous=True)
            K.dma("sp", gkT.t[:], kn_d.rearrange("(pr h2) d -> (h2 d) pr", h2=2), gkT.dsem, w=[gkT],
                  allow_slow_non_contiguous=True)
            K.op("dve", lambda e: e.scalar_tensor_tensor(out=GT.t[:], in0=gqT.t[:], scalar=0.125, in1=gkT.t[:],
                                                         op0=ALU.mult, op1=ALU.mult), r=[gqT, gkT], w=[GT])
            RQ, RK, RV = 4, 6, 12
            qsb = [K.sb(es, "qsb%d" % i, [128, 512], BF16, dma=True) for i in range(RQ)]
            ksb = [K.sb(es, "ksb%d" % i, [128, 512], BF16, dma=True) for i in range(RK)]
            vsb = [K.sb(es, "vsb%d" % i, [128, 8, 65], BF16, dma=True) for i in range(RV)]
            qTb = [K.sb(es, "qT%d" % i, [128, 2, 4, 128], BF16) for i in range(2)]
            for qz in qTb:
                K.op("pool", lambda e: e.memset(qz.t[:], 0.0), w=[qz])
            kTb = [K.sb(es, "kT%d" % i, [128, 4, 128], BF16) for i in range(RK)]
            pex = [K.sb(es, "pex%d" % i, [128, 512], BF16) for i in range(3)]
            pmk = [K.sb(es, "pmk%d" % i, [128, 512], BF16) for i in range(8)]
            osb = [K.sb(es, "osb%d" % i, [128, 520], F32, dma=True) for i in range(2)]
            tq_ps, tk_ps = psb[0], psb[1]
            S_ps = [psb[2], psb[3]]
            O_ps = [[psb[4], psb[5]], [psb[6], psb[7]]]
            blocks, ktiles = [], []
            for c, dil in enumerate(DILS):
                nqb = S // dil // 128
                for r in range(dil):
                    base = len(ktiles)
                    for kt in range(nqb + 1):
                        ktiles.append((c, r, kt))
                    for qb in range(nqb):
                        blocks.append((c, r, qb, base + qb, base + qb + 1))
            NB = len(blocks)
            views = []
            for c, dil in enumerate(DILS):
                views.append((QK_d.rearrange("(i d) c -> d i c", d=dil), V_d.rearrange("(i d) c -> d i c", d=dil),
                              O_d[c].rearrange("(i d) e -> d i e", d=dil)))
            st = {"kl": 0, "kt": 0, "sc": 0}

            def load_q(b):
                c, r, qb = blocks[b][0:3]
                qb_ = qsb[b % RQ]
                K.dma("sp", qb_.t[:], views[c][0][r, 128 * qb:128 * qb + 128, c * 512:(c + 1) * 512], qb_.dsem,
                      w=[qb_])

            def load_kv(ki):
                c, r, kt = ktiles[ki]
                dil = DILS[c]
                L = S // dil
                nqb = L // 128
                kb, vb = ksb[ki % RK], vsb[ki % RV]
                a, b_ = 128 * kt - 64, 128 * kt + 64
                p0, p1 = 0, 128
                if kt == 0:
                    a, p0 = 0, 64
                if kt == nqb:
                    b_, p1 = L, 64
                if kt == 0 or kt == nqb:
                    K.op("pool", lambda e: e.memset(kb.t[:], 0.0), w=[kb])
                    K.op("pool", lambda e: e.memset(vb.t[:], 0.0), w=[vb])
                K.op("pool", lambda e: e.memset(vb.t[p0:p1, :, 64:65], 1.0), w=[vb])
                K.dma("sp", kb.t[p0:p1, :], views[c][0][r, a:b_, 1536 + c * 512:1536 + (c + 1) * 512], kb.dsem,
                      w=[kb])
                K.dma("sp", vb.t[p0:p1, :, 0:64], views[c][1][r, a:b_, c * 512:(c + 1) * 512].rearrange(
                    "n (h d) -> n h d", d=64), vb.dsem, w=[vb])

            def ensure_loaded(upto):
                while st["kl"] <= upto:
                    load_kv(st["kl"])
                    st["kl"] += 1

            def tr_q(b):
                qb_, qt = qsb[b % RQ], qTb[b % 2]
                tv = tq_ps.t[:].bitcast(BF16)
                K.mm_group([(lambda e, pr=pr: e.transpose(out=tv[:, pr * 128:(pr + 1) * 128],
                                                          in_=qb_.t[:, pr * 128:(pr + 1) * 128],
                                                          identity=identb.t[:])) for pr in range(4)],
                           r=[qb_, identb], w=[tq_ps])
                for h2 in range(2):
                    rs = slice(64 * h2, 64 * h2 + 64)
                    K.op("dve", lambda e: e.tensor_copy(out=qt.t[rs, h2, :, :].rearrange("p a b -> p (a b)"),
                                                        in_=tv[rs, 0:512]), r=[tq_ps], w=[qt])

            def tr_k(ki):
                c = ktiles[ki][0]
                kb, kt_ = ksb[ki % RK], kTb[ki % RK]
                tv = tk_ps.t[:].bitcast(BF16)
                K.mm_group([(lambda e, pr=pr: e.transpose(out=tv[:, pr * 128:(pr + 1) * 128],
                                                          in_=kb.t[:, pr * 128:(pr + 1) * 128],
                                                          identity=identb.t[:])) for pr in range(4)],
                           r=[kb, identb], w=[tk_ps])
                K.op("dve", lambda e: e.tensor_tensor(
                    out=kt_.t[:], in0=tv[:, 0:512].rearrange("p (a b) -> p a b", b=128),
                    in1=bc(GT.t[:, 4 * c:4 * c + 4].unsqueeze(2), [128, 4, 128]), op=ALU.mult),
                    r=[tk_ps, GT], w=[kt_])

            def ensure_tr(upto):
                while st["kt"] <= upto:
                    tr_k(st["kt"])
                    st["kt"] += 1

            def s_block(b, prs):
                c, r, qb, kl, kh = blocks[b]
                qt, klo, khi = qTb[b % 2], kTb[kl % RK], kTb[kh % RK]
                for pr in prs:
                    Sp = S_ps[st["sc"] % 2]
                    px, pm = pex[st["sc"] % 3], pmk[(b % 2) * 4 + pr]
                    st["sc"] += 1
                    fns = []
                    for kk, off in ((klo, 0), (khi, 128)):
                        for h2 in range(2):
                            fns.append(lambda e, h2=h2, kk=kk, off=off: e.matmul(
                                Sp.t[:, h2 * 256 + off:h2 * 256 + off + 128], lhsT=kk.t[:, pr, :],
                                rhs=qt.t[:, h2, pr, :], start=True, stop=True))
                    K.mm_group(fns, r=[klo, khi, qt], w=[Sp])
                    K.op("act", lambda e: e.activation(out=px.t[:], in_=Sp.t[:, :], func=AF.Exp), r=[Sp], w=[px])
                    K.op("dve", lambda e: e.tensor_tensor(out=pm.t[:], in0=px.t[:], in1=em.t[:, 4 * c + pr, :],
                                                          op=ALU.mult), r=[px, em], w=[pm])

            def pv_block(b, prs):
                c, r, qb, kl, kh = blocks[b]
                vlo, vhi = vsb[kl % RV], vsb[kh % RV]
                Ob = O_ps[b % 2]
                for pr in prs:
                    pm = pmk[(b % 2) * 4 + pr]
                    for h2 in range(2):
                        h = 2 * pr + h2
                        Obk = Ob[h // 4]
                        oc = slice((h % 4) * 65, (h % 4) * 65 + 65)
                        K.mm_group([
                            lambda e: e.matmul(Obk.t[:, oc], lhsT=pm.t[:, h2 * 256:h2 * 256 + 128],
                                               rhs=vlo.t[:, h, :], start=True, stop=False),
                            lambda e: e.matmul(Obk.t[:, oc], lhsT=pm.t[:, h2 * 256 + 128:h2 * 256 + 256],
                                               rhs=vhi.t[:, h, :], start=False, stop=True)],
                            r=[pm, vlo, vhi], w=[Obk])

            if P3MODE > 0:
                for t_ in range(3):
                    load_q(t_)
                ensure_loaded(blocks[2][4])
                tr_q(0)
                ensure_tr(blocks[0][4])
                s_block(0, (0, 1, 2, 3))
                tr_q(1)
                ensure_tr(blocks[1][4])
                for b in range(NB):
                    if b + 3 < NB:
                        load_q(b + 3)
                        ensure_loaded(blocks[b + 3][4])
                    if b + 2 < NB:
                        tr_q(b + 2)
                        ensure_tr(blocks[b + 2][4])
                    c, r, qb = blocks[b][0:3]
                    ob, Ob = osb[b % 2], O_ps[b % 2]
                    for half in range(2):
                        if b + 1 < NB:
                            s_block(b + 1, (2 * half, 2 * half + 1))
                        pv_block(b, (2 * half, 2 * half + 1))
                    K.op("act", lambda e: e.activation(out=ob.t[:, 0:260], in_=Ob[0].t[:, 0:260], func=AF.Copy),
                         r=[Ob[0]], w=[ob])
                    K.op("dve", lambda e: e.tensor_copy(out=ob.t[:, 260:520], in_=Ob[1].t[:, 0:260]),
                         r=[Ob[1]], w=[ob])
                    K.dma("sp", views[c][2][r, 128 * qb:128 * qb + 128, :], ob.t[:], ob.dsem, r=[ob])
            K.drain()


    if STOP >= 4:
        with ExitStack() as es:
            NS = 4
            o_in = [[K.sb(es, "oin%d_%d" % (c, i), [128, 520], F32, dma=True) for c in range(3)] for i in range(NS)]
            ga_in = [K.sb(es, "ga_in%d" % i, [128, 512], BF16, dma=True) for i in range(NS)]
            gf_in = [K.sb(es, "gf_in%d" % i, [128, 512], BF16, dma=True) for i in range(NS)]
            xr = [K.sb(es, "xr%d" % i, [128, D], F32, dma=True) for i in range(NS)]
            osum = [K.sb(es, "osum%d" % i, [128, 520], F32) for i in range(3)]
            rden = [K.sb(es, "rden%d" % i, [128, 8], F32) for i in range(3)]
            ya32 = [K.sb(es, "ya32_%d" % i, [128, 512], F32) for i in range(3)]
            ya_bf = [K.sb(es, "ya_bf%d" % i, [128, 512], BF16) for i in range(3)]
            yT = [K.sb(es, "yT%d" % i, [128, 8, 128], BF16) for i in range(2)]
            out_sb = [K.sb(es, "out_sb%d" % i, [128, D], F32, dma=True) for i in range(2)]
            tpa, tpg = [psb[0], psb[6]], [psb[1], psb[7]]
            outp = [[psb[2], psb[3]], [psb[4], psb[5]]]

            def load4(i):
                o = i % NS
                rows = slice(128 * i, 128 * i + 128)
                for c in range(3):
                    K.dma("sp", o_in[o][c].t[:], O_d[c][rows, :], o_in[o][c].dsem, w=[o_in[o][c]])
                K.dma("sp", ga_in[o].t[:], GA_d[rows, :], ga_in[o].dsem, w=[ga_in[o]])
                K.dma("sp", gf_in[o].t[:], GF_d[rows, :], gf_in[o].dsem, w=[gf_in[o]])
                K.dma("sp", xr[o].t[:], x_d[rows, :], xr[o].dsem, w=[xr[o]])

            def stage_a1(i):
                o3, s3 = i % 3, i % NS
                os_, rd, y32, ybf = osum[o3], rden[o3], ya32[o3], ya_bf[o3]
                K.op("dve", lambda e: e.tensor_tensor(out=os_.t[:], in0=o_in[s3][0].t[:], in1=o_in[s3][1].t[:],
                                                      op=ALU.add), r=[o_in[s3][0], o_in[s3][1]], w=[os_])
                K.op("dve", lambda e: e.tensor_tensor(out=os_.t[:], in0=os_.t[:], in1=o_in[s3][2].t[:], op=ALU.add),
                     r=[os_, o_in[s3][2]], w=[os_])
                osv = os_.t[:].rearrange("p (h e) -> p h e", e=65)
                K.op("dve", lambda e: e.reciprocal(out=rd.t[:], in_=osv[:, :, 64]), r=[os_], w=[rd])
                K.op("dve", lambda e: e.tensor_tensor(out=y32.t[:].rearrange("p (h d) -> p h d", d=64),
                                                      in0=osv[:, :, 0:64],
                                                      in1=bc(rd.t[:].unsqueeze(2), [128, 8, 64]),
                                                      op=ALU.mult), r=[os_, rd], w=[y32])
                K.op("pool", lambda e: e.tensor_tensor(out=ybf.t[:], in0=y32.t[:], in1=ga_in[s3].t[:], op=ALU.mult),
                     r=[y32, ga_in[s3]], w=[ybf])

            def stage_a2(i):
                o, o3, s3 = i % 2, i % 3, i % NS
                ybf, yt = ya_bf[o3], yT[o]
                tav = tpa[o].t[:].bitcast(BF16)
                tgv = tpg[o].t[:].bitcast(BF16)
                K.mm_group([(lambda e, j=j: e.transpose(out=tgv[:, j * 128:(j + 1) * 128],
                                                        in_=gf_in[s3].t[:, j * 128:(j + 1) * 128],
                                                        identity=identb.t[:]))
                            for j in range(4)], r=[gf_in[s3], identb], w=[tpg[o]])
                K.op("dve", lambda e: e.tensor_tensor(out=yt.t[:, 0:4, :],
                                                      in0=tgv[:, 0:512].rearrange("p (a b) -> p a b", b=128),
                                                      in1=fT.t[:, :, 128 * i:128 * i + 128], op=ALU.mult),
                     r=[tpg[o]], w=[yt])
                K.mm_group([(lambda e, j=j: e.transpose(out=tav[:, j * 128:(j + 1) * 128],
                                                        in_=ybf.t[:, j * 128:(j + 1) * 128], identity=identb.t[:]))
                            for j in range(4)], r=[ybf, identb], w=[tpa[o]])
                K.op("act", lambda e: e.activation(out=yt.t[:, 4:8, :].rearrange("p a b -> p (a b)"),
                                                   in_=tav[:, 0:512], func=AF.Copy), r=[tpa[o]], w=[yt])

            def stage_b_mm(i):
                o = i % 2
                yt = yT[o]
                for half in range(2):
                    pb = outp[o][half]
                    K.mm_group([(lambda e, cc=cc: e.matmul(pb.t[:, :], lhsT=yt.t[:, cc, :],
                                                           rhs=wout_bf.t[:, cc, half * 512:(half + 1) * 512],
                                                           start=(cc == 0), stop=(cc == 7))) for cc in range(8)],
                               r=[yt, wout_bf], w=[pb])

            def stage_b_res(i):
                o, s3 = i % 2, i % NS
                rows = slice(128 * i, 128 * i + 128)
                for half in range(2):
                    pb = outp[o][half]
                    K.op("dve", lambda e: e.tensor_tensor(out=out_sb[o].t[:, half * 512:(half + 1) * 512],
                                                          in0=pb.t[:, :],
                                                          in1=xr[s3].t[:, half * 512:(half + 1) * 512],
                                                          op=ALU.add), r=[pb, xr[s3]], w=[out_sb[o]])
                K.dma("sp", y_d[rows, :], out_sb[o].t[:], out_sb[o].dsem, r=[out_sb[o]])

            for t_ in range(3):
                load4(t_)
            stage_a1(0)
            stage_a1(1)
            stage_a2(0)
            for i in range(NT):
                if i + 3 < NT:
                    load4(i + 3)
                if i + 2 < NT:
                    stage_a1(i + 2)
                stage_b_mm(i)
                if i + 1 < NT:
                    stage_a2(i + 1)
                stage_b_res(i)
    K.drain()
    K.root.close()
    return nc


_CONSTS = None


def kernel(x, norm_w, w_in, q_norm_w, k_norm_w, w_fourier, w_out):
    global _CONSTS
    if _CONSTS is None:
        _CONSTS = _consts()
    nc = build()
    x = np.ascontiguousarray(np.asarray(x, dtype=np.float32))
    shared = {
        "norm_w": np.asarray(norm_w, np.float32), "w_in": np.asarray(w_in, np.float32),
        "q_norm_w": np.asarray(q_norm_w, np.float32), "k_norm_w": np.asarray(k_norm_w, np.float32),
        "w_fourier": np.asarray(w_fourier, np.float32), "w_out": np.asarray(w_out, np.float32),
    }
    shared.update(_CONSTS)
    in_maps = [dict(shared, x=x[b]) for b in range(NCORES)]
    if os.environ.get("MK_TRACE") == "1":
        res = run_bass_kernel_spmd(nc, in_maps, core_ids=list(range(NCORES)), trace=True)
        print("EXEC_TIME_NS", res.exec_time_ns)
    else:
        res = run_bass_kernel_spmd(nc, in_maps, core_ids=list(range(NCORES)))
    if DEBUG:
        kernel.last = res
    ys = [np.asarray(r["y"]) for r in res.results]
    ys += [np.zeros_like(ys[0])] * (8 - len(ys))
    return np.stack(ys, axis=0).astype(np.float32)
```

```python
import os
from contextlib import ExitStack

import ml_dtypes
import numpy as np

import concourse.bass as bass
import concourse.mybir as mybir
from concourse.alu_op_type import AluOpType as ALU
from concourse.bass_utils import run_bass_kernel_spmd

F32 = mybir.dt.float32
BF16 = mybir.dt.bfloat16
AF = mybir.ActivationFunctionType
AX = mybir.AxisListType

S = 8192
D = 1024
NT = S // 128
INW = 6144
NCOL = 6656
DILS = (1, 4, 16)
EPS = 1e-6
DEBUG = bool(int(os.environ.get("MK_DEBUG", "0")))
NTILES_DBG = int(os.environ.get("MK_NT", str(NT)))
STOP = int(os.environ.get("MK_STOP", "9"))
NCORES = int(os.environ.get("MK_CORES", "8"))
P3MODE = int(os.environ.get("MK_P3", "9"))
P3X = int(os.environ.get("MK_P3X", "9"))
SKIP01 = bool(int(os.environ.get("MK_SKIP01", "0")))


def _consts():
    bf = ml_dtypes.bfloat16
    c = {}
    c["identb"] = np.eye(128, dtype=np.float32).astype(bf)
    kap = 1.0 / np.sqrt(S * 64.0)
    cc = np.arange(64)
    ang = 2 * np.pi * np.outer(cc, cc) / 64.0
    cos2 = np.concatenate([np.cos(ang), np.cos(ang)], axis=1) * kap
    sin2 = np.concatenate([np.sin(ang), np.sin(ang)], axis=1) * kap
    c["cs64"] = np.concatenate([cos2, sin2], axis=1).astype(np.float32)
    s1 = np.arange(128)[:, None, None].astype(np.float64)
    s2 = np.arange(64)[None, :, None].astype(np.float64)
    k1 = np.arange(128)[None, None, :].astype(np.float64)
    ph = 2 * np.pi * (s1 * k1 / 128.0 + s2 * k1 / 8192.0)
    A = np.cos(ph)
    B = np.sin(ph)
    c["tabA"] = A.reshape(128, 8192).astype(np.float32).astype(bf)
    c["tabB"] = B.reshape(128, 8192).astype(np.float32).astype(bf)
    c["tabC"] = (-B).reshape(128, 8192).astype(np.float32).astype(bf)
    s2v = np.arange(64)[:, None].astype(np.float64)
    k2v = np.arange(64)[None, :].astype(np.float64)
    ph2 = 2 * np.pi * s2v * k2v / 64.0
    c["cs2"] = np.concatenate([np.cos(ph2), -np.sin(ph2)], axis=0).astype(np.float32).astype(bf)
    j = np.arange(128)[:, None].astype(np.float64)
    i = np.arange(128)[None, :].astype(np.float64)
    em = np.zeros((128, 3, 4, 2, 2, 128), dtype=np.float64)
    for ci, dil in enumerate(DILS):
        for h in range(8):
            slope = 2.0 ** (-(h + 1))
            lo = np.where(j >= i, np.exp(-slope * dil * np.abs(j - 64 - i)), 0.0)
            hi = np.where(j <= i, np.exp(-slope * dil * np.abs(64 + j - i)), 0.0)
            em[:, ci, h // 2, h % 2, 0, :] = lo
            em[:, ci, h // 2, h % 2, 1, :] = hi
    c["emask"] = em.reshape(128, 12 * 512).astype(np.float32).astype(bf)
    return c


class Sem:
    def __init__(self, K, name):
        self.h = K.root.enter_context(K.nc.semaphore(name))
        self.n = 0


class Buf:
    def __init__(self, t, dsem=None):
        self.t = t
        self.w = None
        self.r = []
        self.dsem = dsem


class KB:
    def __init__(self, nc):
        self.nc = nc
        self.root = ExitStack()
        self.E = {"pe": nc.tensor, "act": nc.scalar, "dve": nc.vector, "pool": nc.gpsimd, "sp": nc.sync}
        self.esem = {e: Sem(self, "e_" + e) for e in ("pe", "act", "dve", "pool")}
        self.waited = {e: {} for e in self.E}
        self.dpool = [Sem(self, "d%d" % i) for i in range(72)]
        self.dnext = 0
        self.phase_sem = Sem(self, "phase")
        self.allbufs = []

    def sb(self, es, name, shape, dt, dma=False):
        t = es.enter_context(self.nc.sbuf_tensor("sb_" + name, list(shape), dt))
        b = Buf(t, self.new_dsem() if dma else None)
        self.allbufs.append(b)
        return b

    def ps(self, es, name):
        t = es.enter_context(self.nc.psum_tensor("ps_" + name, [128, 512], F32))
        b = Buf(t)
        self.allbufs.append(b)
        return b

    def new_dsem(self):
        s = self.dpool[self.dnext]
        self.dnext += 1
        return s

    def _waits(self, eng, r, w):
        need = {}

        def add(ev):
            if ev is None:
                return
            s, v = ev
            if need.get(s, 0) < v:
                need[s] = v

        for b in r:
            add(b.w)
        for b in w:
            add(b.w)
            for ev in b.r:
                add(ev)
        for s, v in need.items():
            if self.waited[eng].get(s, 0) < v:
                self.E[eng].wait_ge(s.h, v)
                self.waited[eng][s] = v

    def _record(self, ev, r, w):
        for b in r:
            b.r.append(ev)
        for b in w:
            b.w = ev
            b.r = []

    def op(self, eng, fn, r=(), w=()):
        self._waits(eng, r, w)
        ins = fn(self.E[eng])
        s = self.esem[eng]
        ins.then_inc(s.h, 1)
        s.n += 1
        self._record((s, s.n), r, w)
        return ins

    def mm_group(self, fns, r=(), w=()):
        self._waits("pe", r, w)
        ins = None
        for fn in fns:
            ins = fn(self.nc.tensor)
        s = self.esem["pe"]
        ins.then_inc(s.h, 1)
        s.n += 1
        self._record((s, s.n), r, w)

    def dma(self, q, out, in_, sem, r=(), w=(), **kw):
        self._waits(q, r, w)
        ins = self.E[q].dma_start(out=out, in_=in_, **kw)
        ins.then_inc(sem.h, 16)
        sem.n += 16
        self._record((sem, sem.n), r, w)

    def drain(self):
        sp = self.nc.sync
        for s in list(self.esem.values()) + self.dpool[: self.dnext]:
            if s.n > 0 and self.waited["sp"].get(s, 0) < s.n:
                sp.wait_ge(s.h, s.n)
                self.waited["sp"][s] = s.n
        ps = self.phase_sem
        sp.sem_inc(ps.h, 1)
        ps.n += 1
        for e in ("pe", "act", "dve", "pool"):
            self.E[e].wait_ge(ps.h, ps.n)
            for s in list(self.esem.values()) + self.dpool[: self.dnext]:
                self.waited[e][s] = s.n
        for b in self.allbufs:
            b.w = None
            b.r = []
        self.allbufs = []
        self.dnext = 0


def bc(ap, shape):
    return ap.to_broadcast(list(shape))


def build():
    nc = bass.Bass("TRN2", target_bir_lowering=False)
    K = KB(nc)
    dt_in = lambda n, s, d=F32: nc.dram_tensor(n, list(s), d, kind="ExternalInput").ap()
    x_d = dt_in("x", [S, D])
    nw_d = dt_in("norm_w", [D])
    win_d = dt_in("w_in", [D, INW])
    qn_d = dt_in("q_norm_w", [24, 64])
    kn_d = dt_in("k_norm_w", [24, 64])
    wf_d = dt_in("w_fourier", [8, 64, 64])
    wout_d = dt_in("w_out", [D, D])
    identb_d = dt_in("identb", [128, 128], BF16)
    cs64_d = dt_in("cs64", [64, 256])
    tabA_d = dt_in("tabA", [128, 8192], BF16)
    tabB_d = dt_in("tabB", [128, 8192], BF16)
    tabC_d = dt_in("tabC", [128, 8192], BF16)
    cs2_d = dt_in("cs2", [128, 64], BF16)
    em_d = dt_in("emask", [128, 12 * 512], BF16)
    y_d = nc.dram_tensor("y", [S, D], F32, kind="ExternalOutput").ap()
    skind = "ExternalOutput" if DEBUG else "Internal"
    scr = lambda n, s, d: nc.dram_tensor(n, list(s), d, kind=skind).ap()
    PQ_d = scr("PQ_s", [S, 1024], BF16)
    GF_d = scr("GF_s", [S, 512], BF16)
    GA_d = scr("GA_s", [S, 512], BF16)
    QK_d = scr("QK_s", [S, 3072], BF16)
    V_d = scr("V_s", [S, 1536], BF16)
    Y_d = scr("Y_s", [64, 128, 2, 512], BF16)
    O_acc = scr("O_s", [S, 520], F32)
    O_d = [O_acc, O_acc, O_acc]

    es0 = K.root
    identb = K.sb(es0, "identb", [128, 128], BF16, dma=True)
    mhalf = K.sb(es0, "mhalf", [128, 48], F32)
    psb = [K.ps(es0, "psb%d" % i) for i in range(8)]
    wout_bf = K.sb(es0, "wout_bf", [128, 8, 1024], BF16)
    K.dma("sp", identb.t[:], identb_d[:, :], identb.dsem, w=[identb])
    K.op("pool", lambda e: e.memset(mhalf.t[:], -0.5), w=[mhalf])

    with ExitStack() as es:
      if not SKIP01:
        w_bf = K.sb(es, "w_bf", [128, 8, NCOL], BF16)
        with ExitStack() as es_p0:
            nw = K.sb(es_p0, "nw", [128, 8], F32, dma=True)
            stage = [K.sb(es_p0, "stage%d" % i, [128, INW], F32, dma=True) for i in range(2)]
            wu_bf = K.sb(es_p0, "wu_bf", [128, 8, 512], BF16)
            wuT = K.sb(es_p0, "wuT", [128, 4, 1024], BF16)
            cs64 = K.sb(es_p0, "cs64", [64, 256], F32, dma=True)
            wf_sb = K.sb(es_p0, "wf_sb", [64, 8, 64], F32, dma=True)
            BD = K.sb(es_p0, "BD", [128, 4, 256], BF16)
            K.dma("sp", nw.t[:], nw_d.rearrange("(kc p) -> p kc", p=128), nw.dsem, w=[nw],
                  allow_slow_non_contiguous=True)
            K.dma("sp", cs64.t[:], cs64_d[:, :], cs64.dsem, w=[cs64])
            K.dma("sp", wf_sb.t[:], wf_d.rearrange("g c d -> c g d"), wf_sb.dsem, w=[wf_sb])
            win_v = win_d.rearrange("(kc p) n -> p kc n", p=128)
            for kc in range(8):
                st = stage[kc % 2]
                K.dma("sp", st.t[:], win_v[:, kc, :], st.dsem, w=[st])
                sc = nw.t[:, kc:kc + 1]
                K.op("dve", lambda e: e.tensor_scalar(out=w_bf.t[:, kc, 1024:4096], in0=st.t[:, 512:3584],
                                                      scalar1=sc, scalar2=None, op0=ALU.mult),
                     r=[st, nw], w=[w_bf])
                K.op("act", lambda e: e.activation(out=w_bf.t[:, kc, 4096:NCOL], in_=st.t[:, 3584:INW],
                                                   func=AF.Copy, scale=sc), r=[st, nw], w=[w_bf])
                K.op("pool", lambda e: e.tensor_scalar(out=wu_bf.t[:, kc, :], in0=st.t[:, 0:512],
                                                       scalar1=sc, scalar2=None, op0=ALU.mult),
                     r=[st, nw], w=[wu_bf])
            wout_v = wout_d.rearrange("(cc p) n -> p cc n", p=128)
            for h in range(2):
                st = stage[h]
                K.dma("sp", st.t[:, 0:4096].rearrange("p (c n) -> p c n", c=4), wout_v[:, 4 * h:4 * h + 4, :],
                      st.dsem, w=[st])
                K.op("dve", lambda e: e.tensor_scalar(
                    out=wout_bf.t[:, 4 * h:4 * h + 4, :], in0=st.t[:, 0:4096].rearrange("p (c n) -> p c n", c=4),
                    scalar1=0.5, scalar2=None, op0=ALU.mult), r=[st], w=[wout_bf])
            wf2 = wf_sb.t[:].rearrange("c g d -> c (g d)")
            K.mm_group([lambda e: e.matmul(psb[0].t[:, :], lhsT=cs64.t[:, 0:128], rhs=wf2, start=True, stop=True)],
                       r=[cs64, wf_sb], w=[psb[0]])
            K.mm_group([lambda e: e.matmul(psb[1].t[:, :], lhsT=cs64.t[:, 128:256], rhs=wf2, start=True, stop=True)],
                       r=[cs64, wf_sb], w=[psb[1]])
            K.op("pool", lambda e: e.memset(BD.t[:], 0.0), w=[BD])
            for half in range(2):
                pr = slice(64 * half, 64 * half + 64)
                for t in range(2):
                    src = psb[t].t[pr, :].rearrange("p (j two d) -> p j two d", two=2, d=64)[:, :, half, :]
                    dst = BD.t[pr, :, 128 * t + 64 * half:128 * t + 64 * half + 64]
                    K.op("dve", lambda e: e.tensor_copy(out=dst, in_=src), r=[psb[t]], w=[BD])
            for j in range(4):
                pb = psb[2 + j % 2]
                pv = pb.t[:].bitcast(BF16)
                K.mm_group([(lambda e, kc=kc: e.transpose(out=pv[:, kc * 128:(kc + 1) * 128],
                                                          in_=wu_bf.t[:, kc, j * 128:(j + 1) * 128],
                                                          identity=identb.t[:])) for kc in range(8)],
                           r=[wu_bf, identb], w=[pb])
                K.op("act", lambda e: e.activation(out=wuT.t[:, j, :], in_=pv, func=AF.Copy), r=[pb], w=[wuT])
            for kc in range(8):
                pa, pb2 = psb[4 + 2 * (kc % 2)], psb[5 + 2 * (kc % 2)]
                for jj, pb in ((0, pa), (1, pb2)):
                    K.mm_group([(lambda e, j=j: e.matmul(pb.t[:, (j % 2) * 256:(j % 2) * 256 + 256],
                                                         lhsT=wuT.t[:, j, kc * 128:(kc + 1) * 128],
                                                         rhs=BD.t[:, j, :], start=True, stop=True))
                                for j in (2 * jj, 2 * jj + 1)], r=[wuT, BD], w=[pb])
                    for t in range(2):
                        src = pb.t[:, :].rearrange("p (j two d) -> p j two d", two=2, d=128)[:, :, t, :]
                        dst = w_bf.t[:, kc, 512 * t + 256 * jj:512 * t + 256 * jj + 256].rearrange(
                            "p (j d) -> p j d", d=128)
                        K.op("dve" if t == 0 else "act",
                             (lambda e: e.tensor_copy(out=dst, in_=src)) if t == 0 else
                             (lambda e: e.activation(out=dst, in_=src, func=AF.Copy)),
                             r=[pb], w=[w_bf])
            K.drain()
        with ExitStack() as es1:
            xb = [K.sb(es1, "xb%d" % i, [128, D], F32, dma=True) for i in range(3)]
            xs = [K.sb(es1, "xs%d" % i, [128, D], BF16) for i in range(2)]
            xT = [K.sb(es1, "xT%d" % i, [128, D], BF16) for i in range(2)]
            ssx = [K.sb(es1, "ssx%d" % i, [128, 2], F32) for i in range(2)]
            pq_sb = [K.sb(es1, "pq_sb%d" % i, [128, 1024], BF16, dma=True) for i in range(2)]
            gf_sb = [K.sb(es1, "gf_sb%d" % i, [128, 512], BF16, dma=True) for i in range(2)]
            ga_sb = [K.sb(es1, "ga_sb%d" % i, [128, 512], BF16, dma=True) for i in range(2)]
            qk_sb = [K.sb(es1, "qk_sb%d" % i, [128, 3072], BF16, dma=True) for i in range(2)]
            v_sb = [K.sb(es1, "v_sb%d" % i, [128, 1536], BF16, dma=True) for i in range(2)]
            sq = [K.sb(es1, "sq%d" % i, [128, 512], BF16) for i in range(3)]
            th = [K.sb(es1, "th%d" % i, [128, 512], F32) for i in range(2)]
            ss8 = [K.sb(es1, "ss8_%d" % i, [128, 8], F32) for i in range(3)]
            rs8 = [K.sb(es1, "rs8_%d" % i, [128, 8], F32) for i in range(3)]
            xTp = psb[7]
            gps = psb[0:6]
            gcount = 0
            xTv = xTp.t[:].bitcast(BF16)

            def load_x(i):
                xbi = xb[i % 3]
                K.dma("sp", xbi.t[:], x_d[128 * i:128 * i + 128, :], xbi.dsem, w=[xbi])

            def xchain_a(i):
                xbi, xsi, ssi = xb[i % 3], xs[i % 2], ssx[i % 2]
                K.op("act", lambda e: e.activation(out=xsi.t[:], in_=xbi.t[:], func=AF.Square,
                                                   accum_out=ssi.t[:, 0:1]), r=[xbi], w=[xsi, ssi])
                K.op("pool", lambda e: e.tensor_scalar(out=ssi.t[:, 1:2], in0=ssi.t[:, 0:1], scalar1=1.0 / D,
                                                       scalar2=EPS, op0=ALU.mult, op1=ALU.add), r=[ssi], w=[ssi])
                K.op("pool", lambda e: e.tensor_tensor(out=ssi.t[:, 1:2], in0=ssi.t[:, 1:2], in1=mhalf.t[:, 0:1],
                                                       op=ALU.pow), r=[ssi, mhalf], w=[ssi])

            def xchain_b(i):
                xbi, xsi, ssi = xb[i % 3], xs[i % 2], ssx[i % 2]
                K.op("dve", lambda e: e.tensor_scalar(out=xsi.t[:], in0=xbi.t[:], scalar1=ssi.t[:, 1:2],
                                                      scalar2=None, op0=ALU.mult), r=[xbi, ssi], w=[xsi])

            def xtrans(i):
                xsi, xTi = xs[i % 2], xT[i % 2]
                K.mm_group([(lambda e, kc=kc: e.transpose(out=xTv[:, kc * 128:(kc + 1) * 128],
                                                          in_=xsi.t[:, kc * 128:(kc + 1) * 128],
                                                          identity=identb.t[:])) for kc in range(8)],
                           r=[xsi, identb], w=[xTp])
                K.op("dve", lambda e: e.tensor_copy(out=xTi.t[:], in_=xTv), r=[xTp], w=[xTi])

            load_x(0)
            if NTILES_DBG > 1:
                load_x(1)
            xchain_a(0)
            xchain_b(0)
            xtrans(0)
            pending = []

            def flush_pending():
                while pending:
                    pb_, r8_, dst_, g_ = pending.pop(0)
                    K.op("dve", lambda e: e.tensor_tensor(
                        out=dst_.t[:, (g_ - 3) * 512:(g_ - 2) * 512].rearrange("p (h d) -> p h d", d=64),
                        in0=pb_.t[:, :].rearrange("p (h d) -> p h d", d=64),
                        in1=bc(r8_.t[:].unsqueeze(2), [128, 8, 64]), op=ALU.mult), r=[pb_, r8_], w=[dst_])

            for i in range(NTILES_DBG):
                xTi = xT[i % 2]
                if i + 2 < NTILES_DBG:
                    load_x(i + 2)
                if i + 1 < NTILES_DBG:
                    xchain_a(i + 1)
                o = i % 2
                for g in range(13):
                    pb = gps[gcount % 6]
                    gcount += 1
                    K.mm_group([(lambda e, kc=kc: e.matmul(pb.t[:, :], lhsT=xTi.t[:, kc * 128:(kc + 1) * 128],
                                                           rhs=w_bf.t[:, kc, g * 512:(g + 1) * 512],
                                                           start=(kc == 0), stop=(kc == 7))) for kc in range(8)],
                               r=[xTi, w_bf], w=[pb])
                    if g < 2:
                        dst = pq_sb[o]
                        K.op("act", lambda e: e.activation(out=dst.t[:, g * 512:(g + 1) * 512], in_=pb.t[:, :],
                                                           func=AF.Copy), r=[pb], w=[dst])
                    elif g == 2 or g == 12:
                        dst = gf_sb[o] if g == 2 else ga_sb[o]
                        tb = th[0 if g == 2 else 1]
                        K.op("act", lambda e: e.activation(out=tb.t[:], in_=pb.t[:, :], func=AF.Tanh, scale=0.5),
                             r=[pb], w=[tb])
                        flush_pending()
                        K.op("dve", lambda e: e.scalar_tensor_tensor(out=dst.t[:], in0=tb.t[:], scalar=1.0,
                                                                     in1=pb.t[:, :], op0=ALU.add, op1=ALU.mult),
                             r=[tb, pb], w=[dst])
                    elif g < 9:
                        k3 = gcount % 3
                        sqb, s8, r8 = sq[k3], ss8[k3], rs8[k3]
                        dst = qk_sb[o]
                        K.op("act", lambda e: e.activation(out=sqb.t[:], in_=pb.t[:, :], func=AF.Square),
                             r=[pb], w=[sqb])
                        K.op("dve", lambda e: e.tensor_reduce(out=s8.t[:], in_=sqb.t[:].rearrange(
                            "p (h d) -> p h d", d=64), axis=AX.X, op=ALU.add), r=[sqb], w=[s8])
                        flush_pending()
                        K.op("pool", lambda e: e.tensor_scalar(out=r8.t[:], in0=s8.t[:], scalar1=1.0 / 64,
                                                               scalar2=EPS, op0=ALU.mult, op1=ALU.add),
                             r=[s8], w=[r8])
                        K.op("pool", lambda e: e.tensor_tensor(out=r8.t[:], in0=r8.t[:], in1=mhalf.t[:, 0:8],
                                                               op=ALU.pow), r=[r8, mhalf], w=[r8])
                        pending.append((pb, r8, dst, g))
                    else:
                        flush_pending()
                        dst = v_sb[o]
                        K.op("act", lambda e: e.activation(out=dst.t[:, (g - 9) * 512:(g - 8) * 512],
                                                           in_=pb.t[:, :], func=AF.Copy), r=[pb], w=[dst])
                    if g == 3 and i + 1 < NTILES_DBG:
                        xchain_b(i + 1)
                    if g == 6 and i + 1 < NTILES_DBG:
                        xtrans(i + 1)
                rows = slice(128 * i, 128 * i + 128)
                K.dma("sp", PQ_d[rows, :], pq_sb[o].t[:], pq_sb[o].dsem, r=[pq_sb[o]])
                K.dma("sp", GF_d[rows, :], gf_sb[o].t[:], gf_sb[o].dsem, r=[gf_sb[o]])
                K.dma("sp", GA_d[rows, :], ga_sb[o].t[:], ga_sb[o].dsem, r=[ga_sb[o]])
                K.dma("sp", QK_d[rows, :], qk_sb[o].t[:], qk_sb[o].dsem, r=[qk_sb[o]])
                K.dma("sp", V_d[rows, :], v_sb[o].t[:], v_sb[o].dsem, r=[v_sb[o]])
            K.drain()


    fT = K.sb(es0, "fT", [128, 4, S], BF16)
    if STOP >= 2 and not SKIP01:
        with ExitStack() as es:
            tabs = [K.sb(es, "tab%d" % t, [128, 8192], BF16, dma=True) for t in range(3)]
            zin = [K.sb(es, "zin%d" % i, [128, 8, 1024], BF16, dma=True) for i in range(2)]
            ysb = [K.sb(es, "ysb%d" % i, [128, 2, 512], BF16, dma=True) for i in range(4)]
            for t, td in enumerate((tabA_d, tabB_d, tabC_d)):
                K.dma("sp", tabs[t].t[:], td[:, :], tabs[t].dsem, w=[tabs[t]])
            PQv = PQ_d.rearrange("(s1 s2) c -> s1 s2 c", s2=64)
            for s2c in range(8):
                z = zin[s2c % 2]
                K.dma("sp", z.t[:], PQv[:, s2c * 8:(s2c + 1) * 8, :], z.dsem, w=[z])
                for s2l in range(8):
                    s2 = s2c * 8 + s2l
                    cs = slice(s2 * 128, (s2 + 1) * 128)
                    A, B, C = tabs[0].t[:, cs], tabs[1].t[:, cs], tabs[2].t[:, cs]
                    P, Q = z.t[:, s2l, 0:512], z.t[:, s2l, 512:1024]
                    pr, pi = psb[(2 * s2) % 8], psb[(2 * s2 + 1) % 8]
                    K.mm_group([lambda e: e.matmul(pr.t[:, :], lhsT=A, rhs=P, start=True, stop=False),
                                lambda e: e.matmul(pr.t[:, :], lhsT=C, rhs=Q, start=False, stop=True)],
                               r=[tabs[0], tabs[2], z], w=[pr])
                    K.mm_group([lambda e: e.matmul(pi.t[:, :], lhsT=B, rhs=P, start=True, stop=False),
                                lambda e: e.matmul(pi.t[:, :], lhsT=A, rhs=Q, start=False, stop=True)],
                               r=[tabs[0], tabs[1], z], w=[pi])
                    yb = ysb[s2 % 4]
                    K.op("act", lambda e: e.activation(out=yb.t[:, 0, :], in_=pr.t[:, :], func=AF.Copy), r=[pr], w=[yb])
                    K.op("dve", lambda e: e.tensor_copy(out=yb.t[:, 1, :], in_=pi.t[:, :]), r=[pi], w=[yb])
                    K.dma("sp", Y_d[s2], yb.t[:], yb.dsem, r=[yb])
            K.drain()
        with ExitStack() as es:
            cs2 = K.sb(es, "cs2", [128, 64], BF16, dma=True)
            K.dma("sp", cs2.t[:], cs2_d[:, :], cs2.dsem, w=[cs2])
            yin_t = [es.enter_context(nc.sbuf_tensor("sb_yin%d" % i, [128, 16, 512], BF16)) for i in range(2)]
            yin = [[Buf(yin_t[i], K.new_dsem()) for _ in range(2)] for i in range(2)]
            for i in range(2):
                K.allbufs.extend(yin[i])
            cnt = 0
            for k1c in range(8):
                yt = yin_t[k1c % 2]
                yl, yh = yin[k1c % 2]
                for ri, yb in ((0, yl), (1, yh)):
                    K.dma("sp", yt[64 * ri:64 * ri + 64, :, :], Y_d[:, k1c * 16:(k1c + 1) * 16, ri, :], yb.dsem, w=[yb])
                for chc in range(4):
                    for hh in range(2):
                        pb = psb[cnt % 8]
                        K.mm_group([(lambda e, k=k: e.matmul(pb.t[:, k * 64:(k + 1) * 64],
                                                             lhsT=yt[:, hh * 8 + k, chc * 128:(chc + 1) * 128],
                                                             rhs=cs2.t[:, :], start=True, stop=True)) for k in range(8)],
                                   r=[yl, yh, cs2], w=[pb])
                        k10 = k1c * 16 + hh * 8
                        dst = fT.t[:, chc, :].rearrange("p (k2 k1) -> p k1 k2", k1=128)[:, k10:k10 + 8, :]
                        src = pb.t[:, :].rearrange("p (k1 k2) -> p k1 k2", k2=64)
                        if cnt % 2 == 0:
                            K.op("act", lambda e: e.activation(out=dst, in_=src, func=AF.Copy), r=[pb])
                        else:
                            K.op("dve", lambda e: e.tensor_copy(out=dst, in_=src), r=[pb])
                        cnt += 1
            K.drain()

    if STOP >= 3:
        with ExitStack() as es:
            em = K.sb(es, "em", [128, 12, 512], BF16, dma=True)
            gqT = K.sb(es, "gqT", [128, 12], F32, dma=True)
            gkT = K.sb(es, "gkT", [128, 12], F32, dma=True)
            GT = K.sb(es, "GT", [128, 12], F32)
            K.dma("sp", em.t[:], em_d.rearrange("p (a b) -> p a b", b=512), em.dsem, w=[em])
            K.dma("sp", gqT.t[:], qn_d.rearrange("(pr h2) d -> (h2 d) pr", h2=2), gqT.dsem, w=[gqT],
                  allow_slow_non_contiguous=True)
            K.dma("sp", gkT.t[:], kn_d.rearrange("(pr h2) d -> (h2 d) pr", h2=2), gkT.dsem, w=[gkT],
                  allow_slow_non_contiguous=True)
            K.op("dve", lambda e: e.scalar_tensor_tensor(out=GT.t[:], in0=gqT.t[:], scalar=0.125, in1=gkT.t[:],
                                                         op0=ALU.mult, op1=ALU.mult), r=[gqT, gkT], w=[GT])
            RQ, RK, RV = 4, 6, 12
            qsb = [K.sb(es, "qsb%d" % i, [128, 512], BF16, dma=True) for i in range(RQ)]
            ksb = [K.sb(es, "ksb%d" % i, [128, 512], BF16, dma=True) for i in range(RK)]
            vsb = [K.sb(es, "vsb%d" % i, [128, 8, 65], BF16, dma=True) for i in range(RV)]
            qTb = [K.sb(es, "qT%d" % i, [128, 2, 4, 128], BF16) for i in range(2)]
            for qz in qTb:
                K.op("pool", lambda e: e.memset(qz.t[:], 0.0), w=[qz])
            kTb = [K.sb(es, "kT%d" % i, [128, 4, 128], BF16) for i in range(RK)]
            pex = [K.sb(es, "pex%d" % i, [128, 512], BF16) for i in range(3)]
            pmk = [K.sb(es, "pmk%d" % i, [128, 512], BF16) for i in range(8)]
            osb = [K.sb(es, "osb%d" % i, [128, 520], F32, dma=True) for i in range(2)]
            tq_ps, tk_ps = psb[0], psb[1]
            S_ps = [psb[2], psb[3]]
            O_ps = [[psb[4], psb[5]], [psb[6], psb[7]]]
            blocks, ktiles = [], []
            for c, dil in enumerate(DILS):
                nqb = S // dil // 128
                for r in range(dil):
                    base = len(ktiles)
                    for kt in range(nqb + 1):
                        ktiles.append((c, r, kt))
                    for qb in range(nqb):
                        blocks.append((c, r, qb, base + qb, base + qb + 1))
            NB = len(blocks)
            views = []
            for c, dil in enumerate(DILS):
                views.append((QK_d.rearrange("(i d) c -> d i c", d=dil), V_d.rearrange("(i d) c -> d i c", d=dil),
                              O_d[c].rearrange("(i d) e -> d i e", d=dil)))
            st = {"kl": 0, "kt": 0, "sc": 0}

            def load_q(b):
                c, r, qb = blocks[b][0:3]
                qb_ = qsb[b % RQ]
                K.dma("sp", qb_.t[:], views[c][0][r, 128 * qb:128 * qb + 128, c * 512:(c + 1) * 512], qb_.dsem,
                      w=[qb_])

            def load_kv(ki):
                c, r, kt = ktiles[ki]
                dil = DILS[c]
                L = S // dil
                nqb = L // 128
                kb, vb = ksb[ki % RK], vsb[ki % RV]
                a, b_ = 128 * kt - 64, 128 * kt + 64
                p0, p1 = 0, 128
                if kt == 0:
                    a, p0 = 0, 64
                if kt == nqb:
                    b_, p1 = L, 64
                if kt == 0 or kt == nqb:
                    K.op("pool", lambda e: e.memset(kb.t[:], 0.0), w=[kb])
                    K.op("pool", lambda e: e.memset(vb.t[:], 0.0), w=[vb])
                K.op("pool", lambda e: e.memset(vb.t[p0:p1, :, 64:65], 1.0), w=[vb])
                K.dma("sp", kb.t[p0:p1, :], views[c][0][r, a:b_, 1536 + c * 512:1536 + (c + 1) * 512], kb.dsem,
                      w=[kb])
                K.dma("sp", vb.t[p0:p1, :, 0:64], views[c][1][r, a:b_, c * 512:(c + 1) * 512].rearrange(
                    "n (h d) -> n h d", d=64), vb.dsem, w=[vb])

            def ensure_loaded(upto):
                while st["kl"] <= upto:
                    load_kv(st["kl"])
                    st["kl"] += 1

            def tr_q(b):
                qb_, qt = qsb[b % RQ], qTb[b % 2]
                tv = tq_ps.t[:].bitcast(BF16)
                K.mm_group([(lambda e, pr=pr: e.transpose(out=tv[:, pr * 128:(pr + 1) * 128],
                                                          in_=qb_.t[:, pr * 128:(pr + 1) * 128],
                                                          identity=identb.t[:])) for pr in range(4)],
                           r=[qb_, identb], w=[tq_ps])
                for h2 in range(2):
                    rs = slice(64 * h2, 64 * h2 + 64)
                    K.op("dve", lambda e: e.tensor_copy(out=qt.t[rs, h2, :, :].rearrange("p a b -> p (a b)"),
                                                        in_=tv[rs, 0:512]), r=[tq_ps], w=[qt])

            def tr_k(ki):
                c = ktiles[ki][0]
                kb, kt_ = ksb[ki % RK], kTb[ki % RK]
                tv = tk_ps.t[:].bitcast(BF16)
                K.mm_group([(lambda e, pr=pr: e.transpose(out=tv[:, pr * 128:(pr + 1) * 128],
                                                          in_=kb.t[:, pr * 128:(pr + 1) * 128],
                                                          identity=identb.t[:])) for pr in range(4)],
                           r=[kb, identb], w=[tk_ps])
                K.op("dve", lambda e: e.tensor_tensor(
                    out=kt_.t[:], in0=tv[:, 0:512].rearrange("p (a b) -> p a b", b=128),
                    in1=bc(GT.t[:, 4 * c:4 * c + 4].unsqueeze(2), [128, 4, 128]), op=ALU.mult),
                    r=[tk_ps, GT], w=[kt_])

            def ensure_tr(upto):
                while st["kt"] <= upto:
                    tr_k(st["kt"])
                    st["kt"] += 1

            def s_block(b, prs):
                c, r, qb, kl, kh = blocks[b]
                qt, klo, khi = qTb[b % 2], kTb[kl % RK], kTb[kh % RK]
                for pr in prs:
                    Sp = S_ps[st["sc"] % 2]
                    px, pm = pex[st["sc"] % 3], pmk[(b % 2) * 4 + pr]
                    st["sc"] += 1
                    fns = []
                    for kk, off in ((klo, 0), (khi, 128)):
                        for h2 in range(2):
                            fns.append(lambda e, h2=h2, kk=kk, off=off: e.matmul(
                                Sp.t[:, h2 * 256 + off:h2 * 256 + off + 128], lhsT=kk.t[:, pr, :],
                                rhs=qt.t[:, h2, pr, :], start=True, stop=True))
                    K.mm_group(fns, r=[klo, khi, qt], w=[Sp])
                    K.op("act", lambda e: e.activation(out=px.t[:], in_=Sp.t[:, :], func=AF.Exp), r=[Sp], w=[px])
                    K.op("dve", lambda e: e.tensor_tensor(out=pm.t[:], in0=px.t[:], in1=em.t[:, 4 * c + pr, :],
                                                          op=ALU.mult), r=[px, em], w=[pm])

            def pv_block(b, prs):
                c, r, qb, kl, kh = blocks[b]
                vlo, vhi = vsb[kl % RV], vsb[kh % RV]
                Ob = O_ps[b % 2]
                for pr in prs:
                    pm = pmk[(b % 2) * 4 + pr]
                    for h2 in range(2):
                        h = 2 * pr + h2
                        Obk = Ob[h // 4]
                        oc = slice((h % 4) * 65, (h % 4) * 65 + 65)
                        K.mm_group([
                            lambda e: e.matmul(Obk.t[:, oc], lhsT=pm.t[:, h2 * 256:h2 * 256 + 128],
                                               rhs=vlo.t[:, h, :], start=True, stop=False),
                            lambda e: e.matmul(Obk.t[:, oc], lhsT=pm.t[:, h2 * 256 + 128:h2 * 256 + 256],
                                               rhs=vhi.t[:, h, :], start=False, stop=True)],
                            r=[pm, vlo, vhi], w=[Obk])

            if P3MODE > 0:
                for t_ in range(3):
                    load_q(t_)
                ensure_loaded(blocks[2][4])
                tr_q(0)
                ensure_tr(blocks[0][4])
                s_block(0, (0, 1, 2, 3))
                tr_q(1)
                ensure_tr(blocks[1][4])
                for b in range(NB):
                    if b + 3 < NB:
                        load_q(b + 3)
                        ensure_loaded(blocks[b + 3][4])
                    if b + 2 < NB:
                        tr_q(b + 2)
                        ensure_tr(blocks[b + 2][4])
                    c, r, qb = blocks[b][0:3]
                    ob, Ob = osb[b % 2], O_ps[b % 2]
                    for half in range(2):
                        if b + 1 < NB:
                            s_block(b + 1, (2 * half, 2 * half + 1))
                        pv_block(b, (2 * half, 2 * half + 1))
                    K.op("act", lambda e: e.activation(out=ob.t[:, 0:260], in_=Ob[0].t[:, 0:260], func=AF.Copy),
                         r=[Ob[0]], w=[ob])
                    K.op("dve", lambda e: e.tensor_copy(out=ob.t[:, 260:520], in_=Ob[1].t[:, 0:260]),
                         r=[Ob[1]], w=[ob])
                    if c == 0:
                        K.dma("sp", views[c][2][r, 128 * qb:128 * qb + 128, :], ob.t[:], ob.dsem, r=[ob])
                    else:
                        if blocks[b - 1][0] != c:
                            for ob_ in osb:
                                sm = ob_.dsem
                                if K.waited["pool"].get(sm, 0) < sm.n:
                                    nc.gpsimd.wait_ge(sm.h, sm.n)
                                    K.waited["pool"][sm] = sm.n
                        K.dma("pool", views[c][2][r, 128 * qb:128 * qb + 128, :], ob.t[:], ob.dsem, r=[ob],
                              accum_op=ALU.add)
            K.drain()


    if STOP >= 4:
        with ExitStack() as es:
            NS = 4
            o_in = [[K.sb(es, "oin%d_%d" % (c, i), [128, 520], F32, dma=True) for c in range(1)] for i in range(NS)]
            ga_in = [K.sb(es, "ga_in%d" % i, [128, 512], BF16, dma=True) for i in range(NS)]
            gf_in = [K.sb(es, "gf_in%d" % i, [128, 512], BF16, dma=True) for i in range(NS)]
            xr = [K.sb(es, "xr%d" % i, [128, D], F32, dma=True) for i in range(NS)]
            osum = [K.sb(es, "osum%d" % i, [128, 520], F32) for i in range(3)]
            rden = [K.sb(es, "rden%d" % i, [128, 8], F32) for i in range(3)]
            ya32 = [K.sb(es, "ya32_%d" % i, [128, 512], F32) for i in range(3)]
            ya_bf = [K.sb(es, "ya_bf%d" % i, [128, 512], BF16) for i in range(3)]
            yT = [K.sb(es, "yT%d" % i, [128, 8, 128], BF16) for i in range(2)]
            out_sb = [K.sb(es, "out_sb%d" % i, [128, D], F32, dma=True) for i in range(2)]
            tpa, tpg = [psb[0], psb[6]], [psb[1], psb[7]]
            outp = [[psb[2], psb[3]], [psb[4], psb[5]]]

            def load4(i):
                o = i % NS
                rows = slice(128 * i, 128 * i + 128)
                K.dma("sp", o_in[o][0].t[:], O_acc[rows, :], o_in[o][0].dsem, w=[o_in[o][0]])
                K.dma("sp", ga_in[o].t[:], GA_d[rows, :], ga_in[o].dsem, w=[ga_in[o]])
                K.dma("sp", gf_in[o].t[:], GF_d[rows, :], gf_in[o].dsem, w=[gf_in[o]])
                K.dma("act", xr[o].t[:], x_d[rows, :], xr[o].dsem, w=[xr[o]])

            def stage_a1(i):
                o3, s3 = i % 3, i % NS
                os_, rd, y32, ybf = o_in[s3][0], rden[o3], ya32[o3], ya_bf[o3]
                osv = os_.t[:].rearrange("p (h e) -> p h e", e=65)
                K.op("dve", lambda e: e.reciprocal(out=rd.t[:], in_=osv[:, :, 64]), r=[os_], w=[rd])
                K.op("dve", lambda e: e.tensor_tensor(out=y32.t[:].rearrange("p (h d) -> p h d", d=64),
                                                      in0=osv[:, :, 0:64],
                                                      in1=bc(rd.t[:].unsqueeze(2), [128, 8, 64]),
                                                      op=ALU.mult), r=[os_, rd], w=[y32])
                K.op("pool", lambda e: e.tensor_tensor(out=ybf.t[:], in0=y32.t[:], in1=ga_in[s3].t[:], op=ALU.mult),
                     r=[y32, ga_in[s3]], w=[ybf])

            def stage_a2(i):
                o, o3, s3 = i % 2, i % 3, i % NS
                ybf, yt = ya_bf[o3], yT[o]
                tav = tpa[o].t[:].bitcast(BF16)
                tgv = tpg[o].t[:].bitcast(BF16)
                K.mm_group([(lambda e, j=j: e.transpose(out=tgv[:, j * 128:(j + 1) * 128],
                                                        in_=gf_in[s3].t[:, j * 128:(j + 1) * 128],
                                                        identity=identb.t[:]))
                            for j in range(4)], r=[gf_in[s3], identb], w=[tpg[o]])
                K.op("dve", lambda e: e.tensor_tensor(out=yt.t[:, 0:4, :],
                                                      in0=tgv[:, 0:512].rearrange("p (a b) -> p a b", b=128),
                                                      in1=fT.t[:, :, 128 * i:128 * i + 128], op=ALU.mult),
                     r=[tpg[o]], w=[yt])
                K.mm_group([(lambda e, j=j: e.transpose(out=tav[:, j * 128:(j + 1) * 128],
                                                        in_=ybf.t[:, j * 128:(j + 1) * 128], identity=identb.t[:]))
                            for j in range(4)], r=[ybf, identb], w=[tpa[o]])
                K.op("act", lambda e: e.activation(out=yt.t[:, 4:8, :].rearrange("p a b -> p (a b)"),
                                                   in_=tav[:, 0:512], func=AF.Copy), r=[tpa[o]], w=[yt])

            def stage_b_mm(i):
                o = i % 2
                yt = yT[o]
                for half in range(2):
                    pb = outp[o][half]
                    K.mm_group([(lambda e, cc=cc: e.matmul(pb.t[:, :], lhsT=yt.t[:, cc, :],
                                                           rhs=wout_bf.t[:, cc, half * 512:(half + 1) * 512],
                                                           start=(cc == 0), stop=(cc == 7))) for cc in range(8)],
                               r=[yt, wout_bf], w=[pb])

            def stage_b_res(i):
                o, s3 = i % 2, i % NS
                rows = slice(128 * i, 128 * i + 128)
                for half in range(2):
                    pb = outp[o][half]
                    K.op("dve", lambda e: e.tensor_tensor(out=out_sb[o].t[:, half * 512:(half + 1) * 512],
                                                          in0=pb.t[:, :],
                                                          in1=xr[s3].t[:, half * 512:(half + 1) * 512],
                                                          op=ALU.add), r=[pb, xr[s3]], w=[out_sb[o]])
                K.dma("pool", y_d[rows, :], out_sb[o].t[:], out_sb[o].dsem, r=[out_sb[o]])

            for t_ in range(3):
                load4(t_)
            stage_a1(0)
            stage_a1(1)
            stage_a2(0)
            for i in range(NT):
                if i + 3 < NT:
                    load4(i + 3)
                if i + 2 < NT:
                    stage_a1(i + 2)
                if i + 1 < NT:
                    stage_a2(i + 1)
                stage_b_mm(i)
                stage_b_res(i)
    K.drain()
    K.root.close()
    return nc


_CONSTS = None


def kernel(x, norm_w, w_in, q_norm_w, k_norm_w, w_fourier, w_out):
    global _CONSTS
    if _CONSTS is None:
        _CONSTS = _consts()
    nc = build()
    x = np.ascontiguousarray(np.asarray(x, dtype=np.float32))
    shared = {
        "norm_w": np.asarray(norm_w, np.float32), "w_in": np.asarray(w_in, np.float32),
        "q_norm_w": np.asarray(q_norm_w, np.float32), "k_norm_w": np.asarray(k_norm_w, np.float32),
        "w_fourier": np.asarray(w_fourier, np.float32), "w_out": np.asarray(w_out, np.float32),
    }
    shared.update(_CONSTS)
    in_maps = [dict(shared, x=x[b]) for b in range(NCORES)]
    if os.environ.get("MK_TRACE") == "1":
        res = run_bass_kernel_spmd(nc, in_maps, core_ids=list(range(NCORES)), trace=True)
        print("EXEC_TIME_NS", res.exec_time_ns)
    else:
        res = run_bass_kernel_spmd(nc, in_maps, core_ids=list(range(NCORES)))
    if DEBUG:
        kernel.last = res
    ys = [np.asarray(r["y"]) for r in res.results]
    ys += [np.zeros_like(ys[0])] * (8 - len(ys))
    return np.stack(ys, axis=0).astype(np.float32)
```

```python
import os
from contextlib import ExitStack

import ml_dtypes
import numpy as np

import concourse.bass as bass
import concourse.mybir as mybir
from concourse.alu_op_type import AluOpType as ALU
from concourse.bass_utils import run_bass_kernel_spmd

F32 = mybir.dt.float32
BF16 = mybir.dt.bfloat16
AF = mybir.ActivationFunctionType
AX = mybir.AxisListType

S = 8192
D = 1024
NT = S // 128
INW = 6144
NCOL = 6656
DILS = (1, 4, 16)
EPS = 1e-6
DEBUG = bool(int(os.environ.get("MK_DEBUG", "0")))
NTILES_DBG = int(os.environ.get("MK_NT", str(NT)))
STOP = int(os.environ.get("MK_STOP", "9"))
NCORES = int(os.environ.get("MK_CORES", "8"))
P3MODE = int(os.environ.get("MK_P3", "9"))
P3X = int(os.environ.get("MK_P3X", "9"))
SKIP01 = bool(int(os.environ.get("MK_SKIP01", "0")))


def _consts():
    bf = ml_dtypes.bfloat16
    c = {}
    c["identb"] = np.eye(128, dtype=np.float32).astype(bf)
    kap = 1.0 / np.sqrt(S * 64.0)
    cc = np.arange(64)
    ang = 2 * np.pi * np.outer(cc, cc) / 64.0
    cos2 = np.concatenate([np.cos(ang), np.cos(ang)], axis=1) * kap
    sin2 = np.concatenate([np.sin(ang), np.sin(ang)], axis=1) * kap
    c["cs64"] = np.concatenate([cos2, sin2], axis=1).astype(np.float32)
    s1 = np.arange(128)[:, None, None].astype(np.float64)
    s2 = np.arange(64)[None, :, None].astype(np.float64)
    k1 = np.arange(128)[None, None, :].astype(np.float64)
    ph = 2 * np.pi * (s1 * k1 / 128.0 + s2 * k1 / 8192.0)
    A = np.cos(ph)
    B = np.sin(ph)
    c["tabA"] = A.reshape(128, 8192).astype(np.float32).astype(bf)
    c["tabB"] = B.reshape(128, 8192).astype(np.float32).astype(bf)
    c["tabC"] = (-B).reshape(128, 8192).astype(np.float32).astype(bf)
    s2v = np.arange(64)[:, None].astype(np.float64)
    k2v = np.arange(64)[None, :].astype(np.float64)
    ph2 = 2 * np.pi * s2v * k2v / 64.0
    c["cs2"] = np.concatenate([np.cos(ph2), -np.sin(ph2)], axis=0).astype(np.float32).astype(bf)
    j = np.arange(128)[:, None].astype(np.float64)
    i = np.arange(128)[None, :].astype(np.float64)
    em = np.zeros((128, 3, 2, 2, 2, 2, 128), dtype=np.float64)
    for ci, dil in enumerate(DILS):
        for h in range(8):
            slope = 2.0 ** (-(h + 1))
            lo = np.where(j >= i, np.exp(-slope * dil * np.abs(j - 64 - i)), 0.0)
            hi = np.where(j <= i, np.exp(-slope * dil * np.abs(64 + j - i)), 0.0)
            pr, h2 = h // 2, h % 2
            em[:, ci, pr // 2, h2, pr % 2, 0, :] = lo
            em[:, ci, pr // 2, h2, pr % 2, 1, :] = hi
    c["emask"] = em.reshape(128, 12 * 512).astype(np.float32).astype(bf)
    return c


class Sem:
    def __init__(self, K, name):
        self.h = K.root.enter_context(K.nc.semaphore(name))
        self.n = 0


class Buf:
    def __init__(self, t, dsem=None):
        self.t = t
        self.w = None
        self.r = []
        self.dsem = dsem


class KB:
    def __init__(self, nc):
        self.nc = nc
        self.root = ExitStack()
        self.E = {"pe": nc.tensor, "act": nc.scalar, "dve": nc.vector, "pool": nc.gpsimd, "sp": nc.sync}
        self.esem = {e: Sem(self, "e_" + e) for e in ("pe", "act", "dve", "pool")}
        self.waited = {e: {} for e in self.E}
        self.dpool = [Sem(self, "d%d" % i) for i in range(72)]
        self.dnext = 0
        self.phase_sem = Sem(self, "phase")
        self.allbufs = []
        self.swsems = []

    def sb(self, es, name, shape, dt, dma=False):
        t = es.enter_context(self.nc.sbuf_tensor("sb_" + name, list(shape), dt))
        b = Buf(t, self.new_dsem() if dma else None)
        self.allbufs.append(b)
        return b

    def ps(self, es, name):
        t = es.enter_context(self.nc.psum_tensor("ps_" + name, [128, 512], F32))
        b = Buf(t)
        self.allbufs.append(b)
        return b

    def new_swsem(self, name):
        sm = Sem(self, name)
        self.swsems.append(sm)
        return sm

    def new_dsem(self):
        s = self.dpool[self.dnext]
        self.dnext += 1
        return s

    def _waits(self, eng, r, w):
        need = {}

        def add(ev):
            if ev is None:
                return
            s, v = ev
            if need.get(s, 0) < v:
                need[s] = v

        for b in r:
            add(b.w)
        for b in w:
            add(b.w)
            for ev in b.r:
                add(ev)
        for s, v in need.items():
            if self.waited[eng].get(s, 0) < v:
                self.E[eng].wait_ge(s.h, v)
                self.waited[eng][s] = v

    def _record(self, ev, r, w):
        for b in r:
            b.r.append(ev)
        for b in w:
            b.w = ev
            b.r = []

    def op(self, eng, fn, r=(), w=()):
        self._waits(eng, r, w)
        ins = fn(self.E[eng])
        s = self.esem[eng]
        ins.then_inc(s.h, 1)
        s.n += 1
        self._record((s, s.n), r, w)
        return ins

    def mm_group(self, fns, r=(), w=()):
        self._waits("pe", r, w)
        ins = None
        for fn in fns:
            ins = fn(self.nc.tensor)
        s = self.esem["pe"]
        ins.then_inc(s.h, 1)
        s.n += 1
        self._record((s, s.n), r, w)

    def dma(self, q, out, in_, sem, r=(), w=(), **kw):
        self._waits(q, r, w)
        ins = self.E[q].dma_start(out=out, in_=in_, **kw)
        ins.then_inc(sem.h, 16)
        sem.n += 16
        self._record((sem, sem.n), r, w)

    def drain(self):
        sp = self.nc.sync
        for s in list(self.esem.values()) + self.dpool[: self.dnext] + self.swsems:
            if s.n > 0 and self.waited["sp"].get(s, 0) < s.n:
                sp.wait_ge(s.h, s.n)
                self.waited["sp"][s] = s.n
        ps = self.phase_sem
        sp.sem_inc(ps.h, 1)
        ps.n += 1
        for e in ("pe", "act", "dve", "pool"):
            self.E[e].wait_ge(ps.h, ps.n)
            for s in list(self.esem.values()) + self.dpool[: self.dnext]:
                self.waited[e][s] = s.n
        for b in self.allbufs:
            b.w = None
            b.r = []
        self.allbufs = []
        self.dnext = 0


def bc(ap, shape):
    return ap.to_broadcast(list(shape))


def build():
    nc = bass.Bass("TRN2", target_bir_lowering=False)
    K = KB(nc)
    dt_in = lambda n, s, d=F32: nc.dram_tensor(n, list(s), d, kind="ExternalInput").ap()
    x_d = dt_in("x", [S, D])
    nw_d = dt_in("norm_w", [D])
    win_d = dt_in("w_in", [D, INW])
    qn_d = dt_in("q_norm_w", [24, 64])
    kn_d = dt_in("k_norm_w", [24, 64])
    wf_d = dt_in("w_fourier", [8, 64, 64])
    wout_d = dt_in("w_out", [D, D])
    identb_d = dt_in("identb", [128, 128], BF16)
    cs64_d = dt_in("cs64", [64, 256])
    tabA_d = dt_in("tabA", [128, 8192], BF16)
    tabB_d = dt_in("tabB", [128, 8192], BF16)
    tabC_d = dt_in("tabC", [128, 8192], BF16)
    cs2_d = dt_in("cs2", [128, 64], BF16)
    em_d = dt_in("emask", [128, 12 * 512], BF16)
    y_d = nc.dram_tensor("y", [S, D], F32, kind="ExternalOutput").ap()
    skind = "ExternalOutput" if DEBUG else "Internal"
    scr = lambda n, s, d: nc.dram_tensor(n, list(s), d, kind=skind).ap()
    PQ_d = scr("PQ_s", [S, 1024], BF16)
    GF_d = scr("GF_s", [S, 512], BF16)
    GA_d = scr("GA_s", [S, 512], BF16)
    QK_d = scr("QK_s", [S, 3072], BF16)
    V_d = scr("V_s", [S, 1536], BF16)
    Y_d = scr("Y_s", [64, 128, 2, 512], BF16)
    O_acc = scr("O_s", [S, 520], F32)
    O_d = [O_acc, O_acc, O_acc]

    es0 = K.root
    identb = K.sb(es0, "identb", [128, 128], BF16, dma=True)
    mhalf = K.sb(es0, "mhalf", [128, 48], F32)
    psb = [K.ps(es0, "psb%d" % i) for i in range(8)]
    wout_bf = K.sb(es0, "wout_bf", [128, 8, 1024], BF16)
    K.dma("sp", identb.t[:], identb_d[:, :], identb.dsem, w=[identb])
    K.op("pool", lambda e: e.memset(mhalf.t[:], -0.5), w=[mhalf])

    with ExitStack() as es:
      if not SKIP01:
        w_bf = K.sb(es, "w_bf", [128, 8, NCOL], BF16)
        with ExitStack() as es_p0:
            nw = K.sb(es_p0, "nw", [128, 8], F32, dma=True)
            stage = [K.sb(es_p0, "stage%d" % i, [128, INW], F32, dma=True) for i in range(2)]
            wu_bf = K.sb(es_p0, "wu_bf", [128, 8, 512], BF16)
            wuT = K.sb(es_p0, "wuT", [128, 4, 1024], BF16)
            cs64 = K.sb(es_p0, "cs64", [64, 256], F32, dma=True)
            wf_sb = K.sb(es_p0, "wf_sb", [64, 8, 64], F32, dma=True)
            BD = K.sb(es_p0, "BD", [128, 4, 256], BF16)
            K.dma("sp", nw.t[:], nw_d.rearrange("(kc p) -> p kc", p=128), nw.dsem, w=[nw],
                  allow_slow_non_contiguous=True)
            K.dma("sp", cs64.t[:], cs64_d[:, :], cs64.dsem, w=[cs64])
            K.dma("sp", wf_sb.t[:], wf_d.rearrange("g c d -> c g d"), wf_sb.dsem, w=[wf_sb])
            win_v = win_d.rearrange("(kc p) n -> p kc n", p=128)
            for kc in range(8):
                st = stage[kc % 2]
                K.dma("sp", st.t[:], win_v[:, kc, :], st.dsem, w=[st])
                sc = nw.t[:, kc:kc + 1]
                K.op("dve", lambda e: e.tensor_scalar(out=w_bf.t[:, kc, 1024:4096], in0=st.t[:, 512:3584],
                                                      scalar1=sc, scalar2=None, op0=ALU.mult),
                     r=[st, nw], w=[w_bf])
                K.op("act", lambda e: e.activation(out=w_bf.t[:, kc, 4096:NCOL], in_=st.t[:, 3584:INW],
                                                   func=AF.Copy, scale=sc), r=[st, nw], w=[w_bf])
                K.op("pool", lambda e: e.tensor_scalar(out=wu_bf.t[:, kc, :], in0=st.t[:, 0:512],
                                                       scalar1=sc, scalar2=None, op0=ALU.mult),
                     r=[st, nw], w=[wu_bf])
            wout_v = wout_d.rearrange("(cc p) n -> p cc n", p=128)
            for h in range(2):
                st = stage[h]
                K.dma("sp", st.t[:, 0:4096].rearrange("p (c n) -> p c n", c=4), wout_v[:, 4 * h:4 * h + 4, :],
                      st.dsem, w=[st])
                K.op("dve", lambda e: e.tensor_scalar(
                    out=wout_bf.t[:, 4 * h:4 * h + 4, :], in0=st.t[:, 0:4096].rearrange("p (c n) -> p c n", c=4),
                    scalar1=0.5, scalar2=None, op0=ALU.mult), r=[st], w=[wout_bf])
            wf2 = wf_sb.t[:].rearrange("c g d -> c (g d)")
            K.mm_group([lambda e: e.matmul(psb[0].t[:, :], lhsT=cs64.t[:, 0:128], rhs=wf2, start=True, stop=True)],
                       r=[cs64, wf_sb], w=[psb[0]])
            K.mm_group([lambda e: e.matmul(psb[1].t[:, :], lhsT=cs64.t[:, 128:256], rhs=wf2, start=True, stop=True)],
                       r=[cs64, wf_sb], w=[psb[1]])
            K.op("pool", lambda e: e.memset(BD.t[:], 0.0), w=[BD])
            for half in range(2):
                pr = slice(64 * half, 64 * half + 64)
                for t in range(2):
                    src = psb[t].t[pr, :].rearrange("p (j two d) -> p j two d", two=2, d=64)[:, :, half, :]
                    dst = BD.t[pr, :, 128 * t + 64 * half:128 * t + 64 * half + 64]
                    K.op("dve", lambda e: e.tensor_copy(out=dst, in_=src), r=[psb[t]], w=[BD])
            for j in range(4):
                pb = psb[2 + j % 2]
                pv = pb.t[:].bitcast(BF16)
                K.mm_group([(lambda e, kc=kc: e.transpose(out=pv[:, kc * 128:(kc + 1) * 128],
                                                          in_=wu_bf.t[:, kc, j * 128:(j + 1) * 128],
                                                          identity=identb.t[:])) for kc in range(8)],
                           r=[wu_bf, identb], w=[pb])
                K.op("act", lambda e: e.activation(out=wuT.t[:, j, :], in_=pv, func=AF.Copy), r=[pb], w=[wuT])
            for kc in range(8):
                pa, pb2 = psb[4 + 2 * (kc % 2)], psb[5 + 2 * (kc % 2)]
                for jj, pb in ((0, pa), (1, pb2)):
                    K.mm_group([(lambda e, j=j: e.matmul(pb.t[:, (j % 2) * 256:(j % 2) * 256 + 256],
                                                         lhsT=wuT.t[:, j, kc * 128:(kc + 1) * 128],
                                                         rhs=BD.t[:, j, :], start=True, stop=True))
                                for j in (2 * jj, 2 * jj + 1)], r=[wuT, BD], w=[pb])
                    for t in range(2):
                        src = pb.t[:, :].rearrange("p (j two d) -> p j two d", two=2, d=128)[:, :, t, :]
                        dst = w_bf.t[:, kc, 512 * t + 256 * jj:512 * t + 256 * jj + 256].rearrange(
                            "p (j d) -> p j d", d=128)
                        K.op("dve" if t == 0 else "act",
                             (lambda e: e.tensor_copy(out=dst, in_=src)) if t == 0 else
                             (lambda e: e.activation(out=dst, in_=src, func=AF.Copy)),
                             r=[pb], w=[w_bf])
            K.drain()
        with ExitStack() as es1:
            xb = [K.sb(es1, "xb%d" % i, [128, D], F32, dma=True) for i in range(3)]
            xs = [K.sb(es1, "xs%d" % i, [128, D], BF16) for i in range(2)]
            xT = [K.sb(es1, "xT%d" % i, [128, D], BF16) for i in range(2)]
            ssx = [K.sb(es1, "ssx%d" % i, [128, 2], F32) for i in range(2)]
            pq_sb = [K.sb(es1, "pq_sb%d" % i, [128, 1024], BF16, dma=True) for i in range(2)]
            gf_sb = [K.sb(es1, "gf_sb%d" % i, [128, 512], BF16, dma=True) for i in range(2)]
            ga_sb = [K.sb(es1, "ga_sb%d" % i, [128, 512], BF16, dma=True) for i in range(2)]
            qk_sb = [K.sb(es1, "qk_sb%d" % i, [128, 3072], BF16, dma=True) for i in range(2)]
            v_sb = [K.sb(es1, "v_sb%d" % i, [128, 1536], BF16, dma=True) for i in range(2)]
            sq = [K.sb(es1, "sq%d" % i, [128, 512], BF16) for i in range(3)]
            th = [K.sb(es1, "th%d" % i, [128, 512], F32) for i in range(2)]
            ss8 = [K.sb(es1, "ss8_%d" % i, [128, 8], F32) for i in range(3)]
            rs8 = [K.sb(es1, "rs8_%d" % i, [128, 8], F32) for i in range(3)]
            xTp = psb[7]
            gps = psb[0:6]
            gcount = 0
            xTv = xTp.t[:].bitcast(BF16)

            def load_x(i):
                xbi = xb[i % 3]
                K.dma("sp", xbi.t[:], x_d[128 * i:128 * i + 128, :], xbi.dsem, w=[xbi])

            def xchain_a(i):
                xbi, xsi, ssi = xb[i % 3], xs[i % 2], ssx[i % 2]
                K.op("act", lambda e: e.activation(out=xsi.t[:], in_=xbi.t[:], func=AF.Square,
                                                   accum_out=ssi.t[:, 0:1]), r=[xbi], w=[xsi, ssi])
                K.op("pool", lambda e: e.tensor_scalar(out=ssi.t[:, 1:2], in0=ssi.t[:, 0:1], scalar1=1.0 / D,
                                                       scalar2=EPS, op0=ALU.mult, op1=ALU.add), r=[ssi], w=[ssi])
                K.op("pool", lambda e: e.tensor_tensor(out=ssi.t[:, 1:2], in0=ssi.t[:, 1:2], in1=mhalf.t[:, 0:1],
                                                       op=ALU.pow), r=[ssi, mhalf], w=[ssi])

            def xchain_b(i):
                xbi, xsi, ssi = xb[i % 3], xs[i % 2], ssx[i % 2]
                K.op("dve", lambda e: e.tensor_scalar(out=xsi.t[:], in0=xbi.t[:], scalar1=ssi.t[:, 1:2],
                                                      scalar2=None, op0=ALU.mult), r=[xbi, ssi], w=[xsi])

            def xtrans(i):
                xsi, xTi = xs[i % 2], xT[i % 2]
                K.mm_group([(lambda e, kc=kc: e.transpose(out=xTv[:, kc * 128:(kc + 1) * 128],
                                                          in_=xsi.t[:, kc * 128:(kc + 1) * 128],
                                                          identity=identb.t[:])) for kc in range(8)],
                           r=[xsi, identb], w=[xTp])
                K.op("dve", lambda e: e.tensor_copy(out=xTi.t[:], in_=xTv), r=[xTp], w=[xTi])

            load_x(0)
            if NTILES_DBG > 1:
                load_x(1)
            xchain_a(0)
            xchain_b(0)
            xtrans(0)
            pending = []

            def flush_pending():
                while pending:
                    pb_, r8_, dst_, g_ = pending.pop(0)
                    K.op("dve", lambda e: e.tensor_tensor(
                        out=dst_.t[:, (g_ - 3) * 512:(g_ - 2) * 512].rearrange("p (h d) -> p h d", d=64),
                        in0=pb_.t[:, :].rearrange("p (h d) -> p h d", d=64),
                        in1=bc(r8_.t[:].unsqueeze(2), [128, 8, 64]), op=ALU.mult), r=[pb_, r8_], w=[dst_])

            for i in range(NTILES_DBG):
                xTi = xT[i % 2]
                if i + 2 < NTILES_DBG:
                    load_x(i + 2)
                if i + 1 < NTILES_DBG:
                    xchain_a(i + 1)
                o = i % 2
                for g in range(13):
                    pb = gps[gcount % 6]
                    gcount += 1
                    K.mm_group([(lambda e, kc=kc: e.matmul(pb.t[:, :], lhsT=xTi.t[:, kc * 128:(kc + 1) * 128],
                                                           rhs=w_bf.t[:, kc, g * 512:(g + 1) * 512],
                                                           start=(kc == 0), stop=(kc == 7))) for kc in range(8)],
                               r=[xTi, w_bf], w=[pb])
                    if g < 2:
                        dst = pq_sb[o]
                        K.op("act", lambda e: e.activation(out=dst.t[:, g * 512:(g + 1) * 512], in_=pb.t[:, :],
                                                           func=AF.Copy), r=[pb], w=[dst])
                    elif g == 2 or g == 12:
                        dst = gf_sb[o] if g == 2 else ga_sb[o]
                        tb = th[0 if g == 2 else 1]
                        K.op("act", lambda e: e.activation(out=tb.t[:], in_=pb.t[:, :], func=AF.Tanh, scale=0.5),
                             r=[pb], w=[tb])
                        flush_pending()
                        K.op("dve", lambda e: e.scalar_tensor_tensor(out=dst.t[:], in0=tb.t[:], scalar=1.0,
                                                                     in1=pb.t[:, :], op0=ALU.add, op1=ALU.mult),
                             r=[tb, pb], w=[dst])
                    elif g < 9:
                        k3 = gcount % 3
                        sqb, s8, r8 = sq[k3], ss8[k3], rs8[k3]
                        dst = qk_sb[o]
                        K.op("act", lambda e: e.activation(out=sqb.t[:], in_=pb.t[:, :], func=AF.Square),
                             r=[pb], w=[sqb])
                        K.op("dve", lambda e: e.tensor_reduce(out=s8.t[:], in_=sqb.t[:].rearrange(
                            "p (h d) -> p h d", d=64), axis=AX.X, op=ALU.add), r=[sqb], w=[s8])
                        flush_pending()
                        K.op("pool", lambda e: e.tensor_scalar(out=r8.t[:], in0=s8.t[:], scalar1=1.0 / 64,
                                                               scalar2=EPS, op0=ALU.mult, op1=ALU.add),
                             r=[s8], w=[r8])
                        K.op("pool", lambda e: e.tensor_tensor(out=r8.t[:], in0=r8.t[:], in1=mhalf.t[:, 0:8],
                                                               op=ALU.pow), r=[r8, mhalf], w=[r8])
                        pending.append((pb, r8, dst, g))
                    else:
                        flush_pending()
                        dst = v_sb[o]
                        K.op("act", lambda e: e.activation(out=dst.t[:, (g - 9) * 512:(g - 8) * 512],
                                                           in_=pb.t[:, :], func=AF.Copy), r=[pb], w=[dst])
                    if g == 3 and i + 1 < NTILES_DBG:
                        xchain_b(i + 1)
                    if g == 6 and i + 1 < NTILES_DBG:
                        xtrans(i + 1)
                rows = slice(128 * i, 128 * i + 128)
                K.dma("sp", PQ_d[rows, :], pq_sb[o].t[:], pq_sb[o].dsem, r=[pq_sb[o]])
                K.dma("sp", GF_d[rows, :], gf_sb[o].t[:], gf_sb[o].dsem, r=[gf_sb[o]])
                K.dma("sp", GA_d[rows, :], ga_sb[o].t[:], ga_sb[o].dsem, r=[ga_sb[o]])
                K.dma("sp", QK_d[rows, :], qk_sb[o].t[:], qk_sb[o].dsem, r=[qk_sb[o]])
                K.dma("sp", V_d[rows, :], v_sb[o].t[:], v_sb[o].dsem, r=[v_sb[o]])
            K.drain()


    fT = K.sb(es0, "fT", [128, 4, S], BF16)
    if STOP >= 2 and not SKIP01:
        with ExitStack() as es:
            tabs = [K.sb(es, "tab%d" % t, [128, 8192], BF16, dma=True) for t in range(3)]
            zin = [K.sb(es, "zin%d" % i, [128, 8, 1024], BF16, dma=True) for i in range(2)]
            ysb = [K.sb(es, "ysb%d" % i, [128, 2, 512], BF16, dma=True) for i in range(4)]
            for t, td in enumerate((tabA_d, tabB_d, tabC_d)):
                K.dma("sp", tabs[t].t[:], td[:, :], tabs[t].dsem, w=[tabs[t]])
            PQv = PQ_d.rearrange("(s1 s2) c -> s1 s2 c", s2=64)
            for s2c in range(8):
                z = zin[s2c % 2]
                K.dma("sp", z.t[:], PQv[:, s2c * 8:(s2c + 1) * 8, :], z.dsem, w=[z])
                for s2l in range(8):
                    s2 = s2c * 8 + s2l
                    cs = slice(s2 * 128, (s2 + 1) * 128)
                    A, B, C = tabs[0].t[:, cs], tabs[1].t[:, cs], tabs[2].t[:, cs]
                    P, Q = z.t[:, s2l, 0:512], z.t[:, s2l, 512:1024]
                    pr, pi = psb[(2 * s2) % 8], psb[(2 * s2 + 1) % 8]
                    K.mm_group([lambda e: e.matmul(pr.t[:, :], lhsT=A, rhs=P, start=True, stop=False),
                                lambda e: e.matmul(pr.t[:, :], lhsT=C, rhs=Q, start=False, stop=True)],
                               r=[tabs[0], tabs[2], z], w=[pr])
                    K.mm_group([lambda e: e.matmul(pi.t[:, :], lhsT=B, rhs=P, start=True, stop=False),
                                lambda e: e.matmul(pi.t[:, :], lhsT=A, rhs=Q, start=False, stop=True)],
                               r=[tabs[0], tabs[1], z], w=[pi])
                    yb = ysb[s2 % 4]
                    K.op("act", lambda e: e.activation(out=yb.t[:, 0, :], in_=pr.t[:, :], func=AF.Copy), r=[pr], w=[yb])
                    K.op("dve", lambda e: e.tensor_copy(out=yb.t[:, 1, :], in_=pi.t[:, :]), r=[pi], w=[yb])
                    K.dma("sp", Y_d[s2], yb.t[:], yb.dsem, r=[yb])
            K.drain()
        with ExitStack() as es:
            cs2 = K.sb(es, "cs2", [128, 64], BF16, dma=True)
            K.dma("sp", cs2.t[:], cs2_d[:, :], cs2.dsem, w=[cs2])
            yin_t = [es.enter_context(nc.sbuf_tensor("sb_yin%d" % i, [128, 16, 512], BF16)) for i in range(2)]
            yin = [[Buf(yin_t[i], K.new_dsem()) for _ in range(2)] for i in range(2)]
            for i in range(2):
                K.allbufs.extend(yin[i])
            cnt = 0
            for k1c in range(8):
                yt = yin_t[k1c % 2]
                yl, yh = yin[k1c % 2]
                for ri, yb in ((0, yl), (1, yh)):
                    K.dma("sp", yt[64 * ri:64 * ri + 64, :, :], Y_d[:, k1c * 16:(k1c + 1) * 16, ri, :], yb.dsem, w=[yb])
                for chc in range(4):
                    for hh in range(2):
                        pb = psb[cnt % 8]
                        K.mm_group([(lambda e, k=k: e.matmul(pb.t[:, k * 64:(k + 1) * 64],
                                                             lhsT=yt[:, hh * 8 + k, chc * 128:(chc + 1) * 128],
                                                             rhs=cs2.t[:, :], start=True, stop=True)) for k in range(8)],
                                   r=[yl, yh, cs2], w=[pb])
                        k10 = k1c * 16 + hh * 8
                        dst = fT.t[:, chc, :].rearrange("p (k2 k1) -> p k1 k2", k1=128)[:, k10:k10 + 8, :]
                        src = pb.t[:, :].rearrange("p (k1 k2) -> p k1 k2", k2=64)
                        if cnt % 2 == 0:
                            K.op("act", lambda e: e.activation(out=dst, in_=src, func=AF.Copy), r=[pb])
                        else:
                            K.op("dve", lambda e: e.tensor_copy(out=dst, in_=src), r=[pb])
                        cnt += 1
            K.drain()

    if STOP >= 3:
        with ExitStack() as es:
            em = K.sb(es, "em", [128, 12, 512], BF16, dma=True)
            gqT = K.sb(es, "gqT", [128, 12], F32, dma=True)
            gkT = K.sb(es, "gkT", [128, 12], F32, dma=True)
            GT = K.sb(es, "GT", [128, 12], F32)
            K.dma("sp", em.t[:], em_d.rearrange("p (a b) -> p a b", b=512), em.dsem, w=[em])
            K.dma("sp", gqT.t[:], qn_d.rearrange("(pr h2) d -> (h2 d) pr", h2=2), gqT.dsem, w=[gqT],
                  allow_slow_non_contiguous=True)
            K.dma("sp", gkT.t[:], kn_d.rearrange("(pr h2) d -> (h2 d) pr", h2=2), gkT.dsem, w=[gkT],
                  allow_slow_non_contiguous=True)
            K.op("dve", lambda e: e.scalar_tensor_tensor(out=GT.t[:], in0=gqT.t[:], scalar=0.125, in1=gkT.t[:],
                                                         op0=ALU.mult, op1=ALU.mult), r=[gqT, gkT], w=[GT])
            RQ, RK, RV = 4, 6, 12
            qsb = [K.sb(es, "qsb%d" % i, [128, 512], BF16, dma=True) for i in range(RQ)]
            ksb = [K.sb(es, "ksb%d" % i, [128, 512], BF16, dma=True) for i in range(RK)]
            vsb = [K.sb(es, "vsb%d" % i, [128, 8, 65], BF16, dma=True) for i in range(RV)]
            qTb = [K.sb(es, "qT%d" % i, [128, 4, 128], BF16) for i in range(2)]
            kTb = [K.sb(es, "kT%d" % i, [128, 4, 128], BF16) for i in range(RK)]
            pex = [K.sb(es, "pex%d" % i, [128, 512], BF16) for i in range(3)]
            pmk = [K.sb(es, "pmk%d" % i, [128, 512], BF16) for i in range(8)]
            osb = [K.sb(es, "osb%d" % i, [128, 520], F32, dma=True) for i in range(2)]
            for i_, ob_ in enumerate(osb):
                ob_.swsem = K.new_swsem("sw_osb%d" % i_)
            tq_ps, tk_ps = psb[0], psb[1]
            S_ps = [psb[2], psb[3]]
            O_ps = [[psb[4], psb[5]], [psb[6], psb[7]]]
            blocks, ktiles = [], []
            for c, dil in enumerate(DILS):
                nqb = S // dil // 128
                for r in range(dil):
                    base = len(ktiles)
                    for kt in range(nqb + 1):
                        ktiles.append((c, r, kt))
                    for qb in range(nqb):
                        blocks.append((c, r, qb, base + qb, base + qb + 1))
            NB = len(blocks)
            views = []
            for c, dil in enumerate(DILS):
                views.append((QK_d.rearrange("(i d) c -> d i c", d=dil), V_d.rearrange("(i d) c -> d i c", d=dil),
                              O_d[c].rearrange("(i d) e -> d i e", d=dil)))
            st = {"kl": 0, "kt": 0, "sc": 0}

            def load_q(b):
                c, r, qb = blocks[b][0:3]
                qb_ = qsb[b % RQ]
                K.dma("sp", qb_.t[:], views[c][0][r, 128 * qb:128 * qb + 128, c * 512:(c + 1) * 512], qb_.dsem,
                      w=[qb_])

            def load_kv(ki):
                c, r, kt = ktiles[ki]
                dil = DILS[c]
                L = S // dil
                nqb = L // 128
                kb, vb = ksb[ki % RK], vsb[ki % RV]
                a, b_ = 128 * kt - 64, 128 * kt + 64
                p0, p1 = 0, 128
                if kt == 0:
                    a, p0 = 0, 64
                if kt == nqb:
                    b_, p1 = L, 64
                if kt == 0 or kt == nqb:
                    K.op("pool", lambda e: e.memset(kb.t[:], 0.0), w=[kb])
                    K.op("pool", lambda e: e.memset(vb.t[:], 0.0), w=[vb])
                K.op("pool", lambda e: e.memset(vb.t[p0:p1, :, 64:65], 1.0), w=[vb])
                K.dma("sp", kb.t[p0:p1, :], views[c][0][r, a:b_, 1536 + c * 512:1536 + (c + 1) * 512], kb.dsem,
                      w=[kb])
                K.dma("sp", vb.t[p0:p1, :, 0:64], views[c][1][r, a:b_, c * 512:(c + 1) * 512].rearrange(
                    "n (h d) -> n h d", d=64), vb.dsem, w=[vb])

            def ensure_loaded(upto):
                while st["kl"] <= upto:
                    load_kv(st["kl"])
                    st["kl"] += 1

            def tr_q(b):
                qb_, qt = qsb[b % RQ], qTb[b % 2]
                tv = tq_ps.t[:].bitcast(BF16)
                K.mm_group([(lambda e, pr=pr: e.transpose(out=tv[:, pr * 128:(pr + 1) * 128],
                                                          in_=qb_.t[:, pr * 128:(pr + 1) * 128],
                                                          identity=identb.t[:])) for pr in range(4)],
                           r=[qb_, identb], w=[tq_ps])
                K.op("dve", lambda e: e.tensor_copy(out=qt.t[:].rearrange("p a b -> p (a b)"), in_=tv[:, 0:512]),
                     r=[tq_ps], w=[qt])

            def tr_k(ki):
                c = ktiles[ki][0]
                kb, kt_ = ksb[ki % RK], kTb[ki % RK]
                tv = tk_ps.t[:].bitcast(BF16)
                K.mm_group([(lambda e, pr=pr: e.transpose(out=tv[:, pr * 128:(pr + 1) * 128],
                                                          in_=kb.t[:, pr * 128:(pr + 1) * 128],
                                                          identity=identb.t[:])) for pr in range(4)],
                           r=[kb, identb], w=[tk_ps])
                K.op("dve", lambda e: e.tensor_tensor(
                    out=kt_.t[:], in0=tv[:, 0:512].rearrange("p (a b) -> p a b", b=128),
                    in1=bc(GT.t[:, 4 * c:4 * c + 4].unsqueeze(2), [128, 4, 128]), op=ALU.mult),
                    r=[tk_ps, GT], w=[kt_])

            def ensure_tr(upto):
                while st["kt"] <= upto:
                    tr_k(st["kt"])
                    st["kt"] += 1

            def s_block(b, prs):
                c, r, qb, kl, kh = blocks[b]
                qt, klo, khi = qTb[b % 2], kTb[kl % RK], kTb[kh % RK]
                half = prs[0] // 2
                fns = []
                for pp, pr in enumerate(prs):
                    for kk, off in ((klo, 0), (khi, 128)):
                        for h2 in range(2):
                            rs = slice(64 * h2, 64 * h2 + 64)
                            fns.append(lambda e, h2=h2, kk=kk, off=off, pp=pp, pr=pr, rs=rs: e.matmul(
                                S_ps[h2].t[:, pp * 256 + off:pp * 256 + off + 128], lhsT=kk.t[rs, pr, :],
                                rhs=qt.t[rs, pr, :], start=True, stop=True))
                K.mm_group(fns, r=[klo, khi, qt], w=[S_ps[0], S_ps[1]])
                for h2 in range(2):
                    px, pm = pex[st["sc"] % 3], pmk[(b % 2) * 4 + half * 2 + h2]
                    st["sc"] += 1
                    Sp = S_ps[h2]
                    K.op("act", lambda e: e.activation(out=px.t[:], in_=Sp.t[:, :], func=AF.Exp), r=[Sp], w=[px])
                    K.op("dve", lambda e: e.tensor_tensor(out=pm.t[:], in0=px.t[:],
                                                          in1=em.t[:, (c * 2 + half) * 2 + h2, :],
                                                          op=ALU.mult), r=[px, em], w=[pm])

            def pv_block(b, prs):
                c, r, qb, kl, kh = blocks[b]
                vlo, vhi = vsb[kl % RV], vsb[kh % RV]
                Ob = O_ps[b % 2]
                fns, pms = [], []
                Obk = Ob[prs[0] // 2]
                for pr in prs:
                    for h2 in range(2):
                        pm = pmk[(b % 2) * 4 + (pr // 2) * 2 + h2]
                        pms.append(pm)
                        pp = pr % 2
                        h = 2 * pr + h2
                        oc = slice((h % 4) * 65, (h % 4) * 65 + 65)
                        fns.append(lambda e, pm=pm, pp=pp, h=h, oc=oc: e.matmul(
                            Obk.t[:, oc], lhsT=pm.t[:, pp * 256:pp * 256 + 128], rhs=vlo.t[:, h, :],
                            start=True, stop=False))
                        fns.append(lambda e, pm=pm, pp=pp, h=h, oc=oc: e.matmul(
                            Obk.t[:, oc], lhsT=pm.t[:, pp * 256 + 128:pp * 256 + 256], rhs=vhi.t[:, h, :],
                            start=False, stop=True))
                K.mm_group(fns, r=pms + [vlo, vhi], w=[Obk])

            if P3MODE > 0:
                for t_ in range(3):
                    load_q(t_)
                ensure_loaded(blocks[2][4])
                tr_q(0)
                ensure_tr(blocks[0][4])
                s_block(0, (0, 1))
                s_block(0, (2, 3))
                tr_q(1)
                ensure_tr(blocks[1][4])
                for b in range(NB):
                    if b + 3 < NB:
                        load_q(b + 3)
                        ensure_loaded(blocks[b + 3][4])
                    if b + 2 < NB:
                        tr_q(b + 2)
                        ensure_tr(blocks[b + 2][4])
                    c, r, qb = blocks[b][0:3]
                    ob, Ob = osb[b % 2], O_ps[b % 2]
                    for half in range(2):
                        if b + 1 < NB:
                            s_block(b + 1, (2 * half, 2 * half + 1))
                        pv_block(b, (2 * half, 2 * half + 1))
                    K.op("act", lambda e: e.activation(out=ob.t[:, 0:260], in_=Ob[0].t[:, 0:260], func=AF.Copy),
                         r=[Ob[0]], w=[ob])
                    K.op("dve", lambda e: e.tensor_copy(out=ob.t[:, 260:520], in_=Ob[1].t[:, 0:260]),
                         r=[Ob[1]], w=[ob])
                    if c == 0:
                        K.dma("sp", views[c][2][r, 128 * qb:128 * qb + 128, :], ob.t[:], ob.dsem, r=[ob])
                    else:
                        if blocks[b - 1][0] != c:
                            for ob_ in osb:
                                for sm in (ob_.dsem, ob_.swsem):
                                    if sm.n > 0 and K.waited["pool"].get(sm, 0) < sm.n:
                                        nc.gpsimd.wait_ge(sm.h, sm.n)
                                        K.waited["pool"][sm] = sm.n
                        K.dma("pool", views[c][2][r, 128 * qb:128 * qb + 128, :], ob.t[:], ob.swsem, r=[ob],
                              accum_op=ALU.add)
            K.drain()


    if STOP >= 4:
        with ExitStack() as es:
            NS = 4
            o_in = [[K.sb(es, "oin%d_%d" % (c, i), [128, 520], F32, dma=True) for c in range(1)] for i in range(NS)]
            ga_in = [K.sb(es, "ga_in%d" % i, [128, 512], BF16, dma=True) for i in range(NS)]
            gf_in = [K.sb(es, "gf_in%d" % i, [128, 512], BF16, dma=True) for i in range(NS)]
            xr = [K.sb(es, "xr%d" % i, [128, D], F32, dma=True) for i in range(NS)]
            osum = [K.sb(es, "osum%d" % i, [128, 520], F32) for i in range(3)]
            rden = [K.sb(es, "rden%d" % i, [128, 8], F32) for i in range(3)]
            ya32 = [K.sb(es, "ya32_%d" % i, [128, 512], F32) for i in range(3)]
            ya_bf = [K.sb(es, "ya_bf%d" % i, [128, 512], BF16) for i in range(3)]
            yT = [K.sb(es, "yT%d" % i, [128, 8, 128], BF16) for i in range(2)]
            out_sb = [K.sb(es, "out_sb%d" % i, [128, D], F32) for i in range(2)]
            for i_, ob_ in enumerate(out_sb):
                ob_.dsem = K.new_swsem("sw_out%d" % i_)
            tpa, tpg = [psb[0], psb[6]], [psb[1], psb[7]]
            outp = [[psb[2], psb[3]], [psb[4], psb[5]]]

            def load4(i):
                o = i % NS
                rows = slice(128 * i, 128 * i + 128)
                K.dma("sp", o_in[o][0].t[:], O_acc[rows, :], o_in[o][0].dsem, w=[o_in[o][0]])
                K.dma("sp", ga_in[o].t[:], GA_d[rows, :], ga_in[o].dsem, w=[ga_in[o]])
                K.dma("sp", gf_in[o].t[:], GF_d[rows, :], gf_in[o].dsem, w=[gf_in[o]])
                K.dma("act", xr[o].t[:], x_d[rows, :], xr[o].dsem, w=[xr[o]])

            def stage_a1(i):
                o3, s3 = i % 3, i % NS
                os_, rd, y32, ybf = o_in[s3][0], rden[o3], ya32[o3], ya_bf[o3]
                osv = os_.t[:].rearrange("p (h e) -> p h e", e=65)
                K.op("dve", lambda e: e.reciprocal(out=rd.t[:], in_=osv[:, :, 64]), r=[os_], w=[rd])
                K.op("dve", lambda e: e.tensor_tensor(out=y32.t[:].rearrange("p (h d) -> p h d", d=64),
                                                      in0=osv[:, :, 0:64],
                                                      in1=bc(rd.t[:].unsqueeze(2), [128, 8, 64]),
                                                      op=ALU.mult), r=[os_, rd], w=[y32])
                K.op("pool", lambda e: e.tensor_tensor(out=ybf.t[:], in0=y32.t[:], in1=ga_in[s3].t[:], op=ALU.mult),
                     r=[y32, ga_in[s3]], w=[ybf])

            def stage_a2(i):
                o, o3, s3 = i % 2, i % 3, i % NS
                ybf, yt = ya_bf[o3], yT[o]
                tav = tpa[o].t[:].bitcast(BF16)
                tgv = tpg[o].t[:].bitcast(BF16)
                K.mm_group([(lambda e, j=j: e.transpose(out=tgv[:, j * 128:(j + 1) * 128],
                                                        in_=gf_in[s3].t[:, j * 128:(j + 1) * 128],
                                                        identity=identb.t[:]))
                            for j in range(4)], r=[gf_in[s3], identb], w=[tpg[o]])
                K.op("dve", lambda e: e.tensor_tensor(out=yt.t[:, 0:4, :],
                                                      in0=tgv[:, 0:512].rearrange("p (a b) -> p a b", b=128),
                                                      in1=fT.t[:, :, 128 * i:128 * i + 128], op=ALU.mult),
                     r=[tpg[o]], w=[yt])
                K.mm_group([(lambda e, j=j: e.transpose(out=tav[:, j * 128:(j + 1) * 128],
                                                        in_=ybf.t[:, j * 128:(j + 1) * 128], identity=identb.t[:]))
                            for j in range(4)], r=[ybf, identb], w=[tpa[o]])
                K.op("act", lambda e: e.activation(out=yt.t[:, 4:8, :].rearrange("p a b -> p (a b)"),
                                                   in_=tav[:, 0:512], func=AF.Copy), r=[tpa[o]], w=[yt])

            def stage_b_mm(i):
                o = i % 2
                yt = yT[o]
                for half in range(2):
                    pb = outp[o][half]
                    K.mm_group([(lambda e, cc=cc: e.matmul(pb.t[:, :], lhsT=yt.t[:, cc, :],
                                                           rhs=wout_bf.t[:, cc, half * 512:(half + 1) * 512],
                                                           start=(cc == 0), stop=(cc == 7))) for cc in range(8)],
                               r=[yt, wout_bf], w=[pb])

            def stage_b_res(i):
                o, s3 = i % 2, i % NS
                rows = slice(128 * i, 128 * i + 128)
                for half in range(2):
                    pb = outp[o][half]
                    K.op("dve", lambda e: e.tensor_tensor(out=out_sb[o].t[:, half * 512:(half + 1) * 512],
                                                          in0=pb.t[:, :],
                                                          in1=xr[s3].t[:, half * 512:(half + 1) * 512],
                                                          op=ALU.add), r=[pb, xr[s3]], w=[out_sb[o]])
                K.dma("pool", y_d[rows, :], out_sb[o].t[:], out_sb[o].dsem, r=[out_sb[o]])

            for t_ in range(3):
                load4(t_)
            stage_a1(0)
            stage_a1(1)
            stage_a2(0)
            for i in range(NT):
                if i + 3 < NT:
                    load4(i + 3)
                if i + 2 < NT:
                    stage_a1(i + 2)
                if i + 1 < NT:
                    stage_a2(i + 1)
                stage_b_mm(i)
                stage_b_res(i)
    K.drain()
    K.root.close()
    return nc


_CONSTS = None


def kernel(x, norm_w, w_in, q_norm_w, k_norm_w, w_fourier, w_out):
    global _CONSTS
    if _CONSTS is None:
        _CONSTS = _consts()
    nc = build()
    x = np.ascontiguousarray(np.asarray(x, dtype=np.float32))
    shared = {
        "norm_w": np.asarray(norm_w, np.float32), "w_in": np.asarray(w_in, np.float32),
        "q_norm_w": np.asarray(q_norm_w, np.float32), "k_norm_w": np.asarray(k_norm_w, np.float32),
        "w_fourier": np.asarray(w_fourier, np.float32), "w_out": np.asarray(w_out, np.float32),
    }
    shared.update(_CONSTS)
    in_maps = [dict(shared, x=x[b]) for b in range(NCORES)]
    if os.environ.get("MK_TRACE") == "1":
        res = run_bass_kernel_spmd(nc, in_maps, core_ids=list(range(NCORES)), trace=True)
        print("EXEC_TIME_NS", res.exec_time_ns)
    else:
        res = run_bass_kernel_spmd(nc, in_maps, core_ids=list(range(NCORES)))
    if DEBUG:
        kernel.last = res
    ys = [np.asarray(r["y"]) for r in res.results]
    ys += [np.zeros_like(ys[0])] * (8 - len(ys))
    return np.stack(ys, axis=0).astype(np.float32)
```

```python
import os
from contextlib import ExitStack

import ml_dtypes
import numpy as np

import concourse.bass as bass
import concourse.mybir as mybir
from concourse.alu_op_type import AluOpType as ALU
from concourse.bass_utils import run_bass_kernel_spmd

F32 = mybir.dt.float32
BF16 = mybir.dt.bfloat16
AF = mybir.ActivationFunctionType
AX = mybir.AxisListType

S = 8192
D = 1024
NT = S // 128
INW = 6144
NCOL = 6656
DILS = (1, 4, 16)
EPS = 1e-6
DEBUG = bool(int(os.environ.get("MK_DEBUG", "0")))
NTILES_DBG = int(os.environ.get("MK_NT", str(NT)))
STOP = int(os.environ.get("MK_STOP", "9"))
NCORES = int(os.environ.get("MK_CORES", "8"))
P3MODE = int(os.environ.get("MK_P3", "9"))
P3X = int(os.environ.get("MK_P3X", "9"))
SKIP01 = bool(int(os.environ.get("MK_SKIP01", "0")))


def _consts():
    bf = ml_dtypes.bfloat16
    c = {}
    c["identb"] = np.eye(128, dtype=np.float32).astype(bf)
    kap = 1.0 / np.sqrt(S * 64.0)
    cc = np.arange(64)
    ang = 2 * np.pi * np.outer(cc, cc) / 64.0
    cos2 = np.concatenate([np.cos(ang), np.cos(ang)], axis=1) * kap
    sin2 = np.concatenate([np.sin(ang), np.sin(ang)], axis=1) * kap
    c["cs64"] = np.concatenate([cos2, sin2], axis=1).astype(np.float32)
    s1 = np.arange(128)[:, None, None].astype(np.float64)
    s2 = np.arange(64)[None, :, None].astype(np.float64)
    k1 = np.arange(128)[None, None, :].astype(np.float64)
    ph = 2 * np.pi * (s1 * k1 / 128.0 + s2 * k1 / 8192.0)
    A = np.cos(ph)
    B = np.sin(ph)
    c["tabA"] = A.reshape(128, 8192).astype(np.float32).astype(bf)
    c["tabB"] = B.reshape(128, 8192).astype(np.float32).astype(bf)
    c["tabC"] = (-B).reshape(128, 8192).astype(np.float32).astype(bf)
    s2v = np.arange(64)[:, None].astype(np.float64)
    k2v = np.arange(64)[None, :].astype(np.float64)
    ph2 = 2 * np.pi * s2v * k2v / 64.0
    c["cs2"] = np.concatenate([np.cos(ph2), -np.sin(ph2)], axis=0).astype(np.float32).astype(bf)
    j = np.arange(128)[:, None].astype(np.float64)
    i = np.arange(128)[None, :].astype(np.float64)
    em = np.zeros((128, 3, 2, 2, 2, 2, 128), dtype=np.float64)
    for ci, dil in enumerate(DILS):
        for h in range(8):
            slope = 2.0 ** (-(h + 1))
            lo = np.where(j >= i, np.exp(-slope * dil * np.abs(j - 64 - i)), 0.0)
            hi = np.where(j <= i, np.exp(-slope * dil * np.abs(64 + j - i)), 0.0)
            pr, h2 = h // 2, h % 2
            em[:, ci, pr // 2, h2, pr % 2, 0, :] = lo
            em[:, ci, pr // 2, h2, pr % 2, 1, :] = hi
    c["emask"] = em.reshape(128, 12 * 512).astype(np.float32).astype(bf)
    return c


class Sem:
    def __init__(self, K, name):
        self.h = K.root.enter_context(K.nc.semaphore(name))
        self.n = 0


class Buf:
    def __init__(self, t, dsem=None):
        self.t = t
        self.w = None
        self.r = []
        self.dsem = dsem


class KB:
    def __init__(self, nc):
        self.nc = nc
        self.root = ExitStack()
        self.E = {"pe": nc.tensor, "act": nc.scalar, "dve": nc.vector, "pool": nc.gpsimd, "sp": nc.sync}
        self.esem = {e: Sem(self, "e_" + e) for e in ("pe", "act", "dve", "pool")}
        self.waited = {e: {} for e in self.E}
        self.dpool = [Sem(self, "d%d" % i) for i in range(72)]
        self.dnext = 0
        self.phase_sem = Sem(self, "phase")
        self.allbufs = []
        self.swsems = []

    def sb(self, es, name, shape, dt, dma=False):
        t = es.enter_context(self.nc.sbuf_tensor("sb_" + name, list(shape), dt))
        b = Buf(t, self.new_dsem() if dma else None)
        self.allbufs.append(b)
        return b

    def ps(self, es, name):
        t = es.enter_context(self.nc.psum_tensor("ps_" + name, [128, 512], F32))
        b = Buf(t)
        self.allbufs.append(b)
        return b

    def new_swsem(self, name):
        sm = Sem(self, name)
        self.swsems.append(sm)
        return sm

    def new_dsem(self):
        s = self.dpool[self.dnext]
        self.dnext += 1
        return s

    def _waits(self, eng, r, w):
        need = {}

        def add(ev):
            if ev is None:
                return
            s, v = ev
            if need.get(s, 0) < v:
                need[s] = v

        for b in r:
            add(b.w)
        for b in w:
            add(b.w)
            for ev in b.r:
                add(ev)
        for s, v in need.items():
            if self.waited[eng].get(s, 0) < v:
                self.E[eng].wait_ge(s.h, v)
                self.waited[eng][s] = v

    def _record(self, ev, r, w):
        for b in r:
            b.r.append(ev)
        for b in w:
            b.w = ev
            b.r = []

    def op(self, eng, fn, r=(), w=()):
        self._waits(eng, r, w)
        ins = fn(self.E[eng])
        s = self.esem[eng]
        ins.then_inc(s.h, 1)
        s.n += 1
        self._record((s, s.n), r, w)
        return ins

    def mm_group(self, fns, r=(), w=()):
        self._waits("pe", r, w)
        ins = None
        for fn in fns:
            ins = fn(self.nc.tensor)
        s = self.esem["pe"]
        ins.then_inc(s.h, 1)
        s.n += 1
        self._record((s, s.n), r, w)

    def dma(self, q, out, in_, sem, r=(), w=(), **kw):
        self._waits(q, r, w)
        ins = self.E[q].dma_start(out=out, in_=in_, **kw)
        ins.then_inc(sem.h, 16)
        sem.n += 16
        self._record((sem, sem.n), r, w)

    def drain(self):
        sp = self.nc.sync
        for s in list(self.esem.values()) + self.dpool[: self.dnext] + self.swsems:
            if s.n > 0 and self.waited["sp"].get(s, 0) < s.n:
                sp.wait_ge(s.h, s.n)
                self.waited["sp"][s] = s.n
        ps = self.phase_sem
        sp.sem_inc(ps.h, 1)
        ps.n += 1
        for e in ("pe", "act", "dve", "pool"):
            self.E[e].wait_ge(ps.h, ps.n)
            for s in list(self.esem.values()) + self.dpool[: self.dnext]:
                self.waited[e][s] = s.n
        for b in self.allbufs:
            b.w = None
            b.r = []
        self.allbufs = []
        self.dnext = 0


def bc(ap, shape):
    return ap.to_broadcast(list(shape))


def build():
    nc = bass.Bass("TRN2", target_bir_lowering=False)
    K = KB(nc)
    dt_in = lambda n, s, d=F32: nc.dram_tensor(n, list(s), d, kind="ExternalInput").ap()
    x_d = dt_in("x", [S, D])
    nw_d = dt_in("norm_w", [D])
    win_d = dt_in("w_in", [D, INW])
    qn_d = dt_in("q_norm_w", [24, 64])
    kn_d = dt_in("k_norm_w", [24, 64])
    wf_d = dt_in("w_fourier", [8, 64, 64])
    wout_d = dt_in("w_out", [D, D])
    identb_d = dt_in("identb", [128, 128], BF16)
    cs64_d = dt_in("cs64", [64, 256])
    tabA_d = dt_in("tabA", [128, 8192], BF16)
    tabB_d = dt_in("tabB", [128, 8192], BF16)
    tabC_d = dt_in("tabC", [128, 8192], BF16)
    cs2_d = dt_in("cs2", [128, 64], BF16)
    em_d = dt_in("emask", [128, 12 * 512], BF16)
    y_d = nc.dram_tensor("y", [S, D], F32, kind="ExternalOutput").ap()
    skind = "ExternalOutput" if DEBUG else "Internal"
    scr = lambda n, s, d: nc.dram_tensor(n, list(s), d, kind=skind).ap()
    PQ_d = scr("PQ_s", [S, 1024], BF16)
    GF_d = scr("GF_s", [S, 512], BF16)
    GA_d = scr("GA_s", [S, 512], BF16)
    QK_d = scr("QK_s", [S, 3072], BF16)
    V_d = scr("V_s", [S, 1536], BF16)
    Y_d = scr("Y_s", [64, 128, 2, 512], BF16)
    O_acc = scr("O_s", [S, 520], F32)
    O_d = [O_acc, O_acc, O_acc]

    es0 = K.root
    identb = K.sb(es0, "identb", [128, 128], BF16, dma=True)
    mhalf = K.sb(es0, "mhalf", [128, 48], F32)
    psb = [K.ps(es0, "psb%d" % i) for i in range(8)]
    wout_bf = K.sb(es0, "wout_bf", [128, 8, 1024], BF16)
    K.dma("sp", identb.t[:], identb_d[:, :], identb.dsem, w=[identb])
    K.op("pool", lambda e: e.memset(mhalf.t[:], -0.5), w=[mhalf])

    with ExitStack() as es:
      if not SKIP01:
        w_bf = K.sb(es, "w_bf", [128, 8, NCOL], BF16)
        with ExitStack() as es_p0:
            nw = K.sb(es_p0, "nw", [128, 8], F32, dma=True)
            stage = [K.sb(es_p0, "stage%d" % i, [128, INW], F32, dma=True) for i in range(2)]
            wu_bf = [K.sb(es_p0, "wu_bf%d" % i, [128, 512], BF16) for i in range(8)]
            wuT = [K.sb(es_p0, "wuT%d" % i, [128, 4, 128], BF16) for i in range(2)]
            cs64 = K.sb(es_p0, "cs64", [64, 256], F32, dma=True)
            wf_sb = K.sb(es_p0, "wf_sb", [64, 8, 64], F32, dma=True)
            BD = K.sb(es_p0, "BD", [128, 4, 256], BF16)
            K.dma("sp", nw.t[:], nw_d.rearrange("(kc p) -> p kc", p=128), nw.dsem, w=[nw],
                  allow_slow_non_contiguous=True)
            K.dma("sp", cs64.t[:], cs64_d[:, :], cs64.dsem, w=[cs64])
            K.dma("sp", wf_sb.t[:], wf_d.rearrange("g c d -> c g d"), wf_sb.dsem, w=[wf_sb])
            wf2 = wf_sb.t[:].rearrange("c g d -> c (g d)")
            K.mm_group([lambda e: e.matmul(psb[0].t[:, :], lhsT=cs64.t[:, 0:128], rhs=wf2, start=True, stop=True)],
                       r=[cs64, wf_sb], w=[psb[0]])
            K.mm_group([lambda e: e.matmul(psb[1].t[:, :], lhsT=cs64.t[:, 128:256], rhs=wf2, start=True, stop=True)],
                       r=[cs64, wf_sb], w=[psb[1]])
            K.op("pool", lambda e: e.memset(BD.t[:], 0.0), w=[BD])
            for half in range(2):
                pr = slice(64 * half, 64 * half + 64)
                for t in range(2):
                    src = psb[t].t[pr, :].rearrange("p (j two d) -> p j two d", two=2, d=64)[:, :, half, :]
                    dst = BD.t[pr, :, 128 * t + 64 * half:128 * t + 64 * half + 64]
                    K.op("dve", lambda e: e.tensor_copy(out=dst, in_=src), r=[psb[t]], w=[BD])
            def fold_kc(kc):
                pt = psb[2 + kc % 2]
                pv = pt.t[:].bitcast(BF16)
                K.mm_group([(lambda e, j=j: e.transpose(out=pv[:, j * 128:(j + 1) * 128],
                                                        in_=wu_bf[kc].t[:, j * 128:(j + 1) * 128],
                                                        identity=identb.t[:])) for j in range(4)],
                           r=[wu_bf[kc], identb], w=[pt])
                wT = wuT[kc % 2]
                K.op("act", lambda e: e.activation(out=wT.t[:].rearrange("p a b -> p (a b)"), in_=pv[:, 0:512],
                                                   func=AF.Copy), r=[pt], w=[wT])
                pa, pb2 = psb[4 + 2 * (kc % 2)], psb[5 + 2 * (kc % 2)]
                for jj, pb in ((0, pa), (1, pb2)):
                    K.mm_group([(lambda e, j=j: e.matmul(pb.t[:, (j % 2) * 256:(j % 2) * 256 + 256],
                                                         lhsT=wT.t[:, j, :], rhs=BD.t[:, j, :],
                                                         start=True, stop=True))
                                for j in (2 * jj, 2 * jj + 1)], r=[wT, BD], w=[pb])
                    for t in range(2):
                        src = pb.t[:, :].rearrange("p (j two d) -> p j two d", two=2, d=128)[:, :, t, :]
                        dst = w_bf.t[:, kc, 512 * t + 256 * jj:512 * t + 256 * jj + 256].rearrange(
                            "p (j d) -> p j d", d=128)
                        K.op("dve" if t == 0 else "act",
                             (lambda e: e.tensor_copy(out=dst, in_=src)) if t == 0 else
                             (lambda e: e.activation(out=dst, in_=src, func=AF.Copy)),
                             r=[pb], w=[w_bf])

            win_v = win_d.rearrange("(kc p) n -> p kc n", p=128)
            for kc in range(8):
                st = stage[kc % 2]
                K.dma("sp", st.t[:], win_v[:, kc, :], st.dsem, w=[st])
                sc = nw.t[:, kc:kc + 1]
                K.op("dve", lambda e: e.tensor_scalar(out=w_bf.t[:, kc, 1024:4096], in0=st.t[:, 512:3584],
                                                      scalar1=sc, scalar2=None, op0=ALU.mult),
                     r=[st, nw], w=[w_bf])
                K.op("act", lambda e: e.activation(out=w_bf.t[:, kc, 4096:NCOL], in_=st.t[:, 3584:INW],
                                                   func=AF.Copy, scale=sc), r=[st, nw], w=[w_bf])
                K.op("pool", lambda e: e.tensor_scalar(out=wu_bf[kc].t[:], in0=st.t[:, 0:512],
                                                       scalar1=sc, scalar2=None, op0=ALU.mult),
                     r=[st, nw], w=[wu_bf[kc]])
                fold_kc(kc)
            wout_v = wout_d.rearrange("(cc p) n -> p cc n", p=128)
            for h in range(2):
                st = stage[h]
                K.dma("sp", st.t[:, 0:4096].rearrange("p (c n) -> p c n", c=4), wout_v[:, 4 * h:4 * h + 4, :],
                      st.dsem, w=[st])
                K.op("dve", lambda e: e.tensor_scalar(
                    out=wout_bf.t[:, 4 * h:4 * h + 4, :], in0=st.t[:, 0:4096].rearrange("p (c n) -> p c n", c=4),
                    scalar1=0.5, scalar2=None, op0=ALU.mult), r=[st], w=[wout_bf])
            K.drain()
        with ExitStack() as es1:
            xb = [K.sb(es1, "xb%d" % i, [128, D], F32, dma=True) for i in range(3)]
            xs = [K.sb(es1, "xs%d" % i, [128, D], BF16) for i in range(2)]
            xT = [K.sb(es1, "xT%d" % i, [128, D], BF16) for i in range(2)]
            ssx = [K.sb(es1, "ssx%d" % i, [128, 2], F32) for i in range(2)]
            pq_sb = [K.sb(es1, "pq_sb%d" % i, [128, 1024], BF16, dma=True) for i in range(2)]
            gf_sb = [K.sb(es1, "gf_sb%d" % i, [128, 512], BF16, dma=True) for i in range(2)]
            ga_sb = [K.sb(es1, "ga_sb%d" % i, [128, 512], BF16, dma=True) for i in range(2)]
            qk_sb = [K.sb(es1, "qk_sb%d" % i, [128, 3072], BF16, dma=True) for i in range(2)]
            v_sb = [K.sb(es1, "v_sb%d" % i, [128, 1536], BF16, dma=True) for i in range(2)]
            sq = [K.sb(es1, "sq%d" % i, [128, 512], BF16) for i in range(3)]
            th = [K.sb(es1, "th%d" % i, [128, 512], F32) for i in range(2)]
            ss8 = [K.sb(es1, "ss8_%d" % i, [128, 8], F32) for i in range(3)]
            rs8 = [K.sb(es1, "rs8_%d" % i, [128, 8], F32) for i in range(3)]
            xTp = psb[7]
            gps = psb[0:6]
            gcount = 0
            xTv = xTp.t[:].bitcast(BF16)

            def load_x(i):
                xbi = xb[i % 3]
                K.dma("sp", xbi.t[:], x_d[128 * i:128 * i + 128, :], xbi.dsem, w=[xbi])

            def xchain_a(i):
                xbi, xsi, ssi = xb[i % 3], xs[i % 2], ssx[i % 2]
                K.op("act", lambda e: e.activation(out=xsi.t[:], in_=xbi.t[:], func=AF.Square,
                                                   accum_out=ssi.t[:, 0:1]), r=[xbi], w=[xsi, ssi])
                K.op("pool", lambda e: e.tensor_scalar(out=ssi.t[:, 1:2], in0=ssi.t[:, 0:1], scalar1=1.0 / D,
                                                       scalar2=EPS, op0=ALU.mult, op1=ALU.add), r=[ssi], w=[ssi])
                K.op("pool", lambda e: e.tensor_tensor(out=ssi.t[:, 1:2], in0=ssi.t[:, 1:2], in1=mhalf.t[:, 0:1],
                                                       op=ALU.pow), r=[ssi, mhalf], w=[ssi])

            def xchain_b(i):
                xbi, xsi, ssi = xb[i % 3], xs[i % 2], ssx[i % 2]
                K.op("dve", lambda e: e.tensor_scalar(out=xsi.t[:], in0=xbi.t[:], scalar1=ssi.t[:, 1:2],
                                                      scalar2=None, op0=ALU.mult), r=[xbi, ssi], w=[xsi])

            def xtrans(i):
                xsi, xTi = xs[i % 2], xT[i % 2]
                K.mm_group([(lambda e, kc=kc: e.transpose(out=xTv[:, kc * 128:(kc + 1) * 128],
                                                          in_=xsi.t[:, kc * 128:(kc + 1) * 128],
                                                          identity=identb.t[:])) for kc in range(8)],
                           r=[xsi, identb], w=[xTp])
                K.op("dve", lambda e: e.tensor_copy(out=xTi.t[:], in_=xTv), r=[xTp], w=[xTi])

            load_x(0)
            if NTILES_DBG > 1:
                load_x(1)
            xchain_a(0)
            xchain_b(0)
            xtrans(0)
            pending = []

            def flush_pending():
                while pending:
                    pb_, r8_, dst_, g_ = pending.pop(0)
                    K.op("dve", lambda e: e.tensor_tensor(
                        out=dst_.t[:, (g_ - 3) * 512:(g_ - 2) * 512].rearrange("p (h d) -> p h d", d=64),
                        in0=pb_.t[:, :].rearrange("p (h d) -> p h d", d=64),
                        in1=bc(r8_.t[:].unsqueeze(2), [128, 8, 64]), op=ALU.mult), r=[pb_, r8_], w=[dst_])

            for i in range(NTILES_DBG):
                xTi = xT[i % 2]
                if i + 2 < NTILES_DBG:
                    load_x(i + 2)
                if i + 1 < NTILES_DBG:
                    xchain_a(i + 1)
                o = i % 2
                for g in range(13):
                    pb = gps[gcount % 6]
                    gcount += 1
                    K.mm_group([(lambda e, kc=kc: e.matmul(pb.t[:, :], lhsT=xTi.t[:, kc * 128:(kc + 1) * 128],
                                                           rhs=w_bf.t[:, kc, g * 512:(g + 1) * 512],
                                                           start=(kc == 0), stop=(kc == 7))) for kc in range(8)],
                               r=[xTi, w_bf], w=[pb])
                    if g < 2:
                        dst = pq_sb[o]
                        K.op("act", lambda e: e.activation(out=dst.t[:, g * 512:(g + 1) * 512], in_=pb.t[:, :],
                                                           func=AF.Copy), r=[pb], w=[dst])
                    elif g == 2 or g == 12:
                        dst = gf_sb[o] if g == 2 else ga_sb[o]
                        tb = th[0 if g == 2 else 1]
                        K.op("act", lambda e: e.activation(out=tb.t[:], in_=pb.t[:, :], func=AF.Tanh, scale=0.5),
                             r=[pb], w=[tb])
                        flush_pending()
                        K.op("dve", lambda e: e.scalar_tensor_tensor(out=dst.t[:], in0=tb.t[:], scalar=1.0,
                                                                     in1=pb.t[:, :], op0=ALU.add, op1=ALU.mult),
                             r=[tb, pb], w=[dst])
                    elif g < 9:
                        k3 = gcount % 3
                        sqb, s8, r8 = sq[k3], ss8[k3], rs8[k3]
                        dst = qk_sb[o]
                        K.op("act", lambda e: e.activation(out=sqb.t[:], in_=pb.t[:, :], func=AF.Square),
                             r=[pb], w=[sqb])
                        K.op("dve", lambda e: e.tensor_reduce(out=s8.t[:], in_=sqb.t[:].rearrange(
                            "p (h d) -> p h d", d=64), axis=AX.X, op=ALU.add), r=[sqb], w=[s8])
                        flush_pending()
                        K.op("pool", lambda e: e.tensor_scalar(out=r8.t[:], in0=s8.t[:], scalar1=1.0 / 64,
                                                               scalar2=EPS, op0=ALU.mult, op1=ALU.add),
                             r=[s8], w=[r8])
                        K.op("pool", lambda e: e.tensor_tensor(out=r8.t[:], in0=r8.t[:], in1=mhalf.t[:, 0:8],
                                                               op=ALU.pow), r=[r8, mhalf], w=[r8])
                        pending.append((pb, r8, dst, g))
                    else:
                        flush_pending()
                        dst = v_sb[o]
                        K.op("act", lambda e: e.activation(out=dst.t[:, (g - 9) * 512:(g - 8) * 512],
                                                           in_=pb.t[:, :], func=AF.Copy), r=[pb], w=[dst])
                    if g == 3 and i + 1 < NTILES_DBG:
                        xchain_b(i + 1)
                    if g == 6 and i + 1 < NTILES_DBG:
                        xtrans(i + 1)
                rows = slice(128 * i, 128 * i + 128)
                K.dma("sp", PQ_d[rows, :], pq_sb[o].t[:], pq_sb[o].dsem, r=[pq_sb[o]])
                K.dma("sp", GF_d[rows, :], gf_sb[o].t[:], gf_sb[o].dsem, r=[gf_sb[o]])
                K.dma("sp", GA_d[rows, :], ga_sb[o].t[:], ga_sb[o].dsem, r=[ga_sb[o]])
                K.dma("sp", QK_d[rows, :], qk_sb[o].t[:], qk_sb[o].dsem, r=[qk_sb[o]])
                K.dma("sp", V_d[rows, :], v_sb[o].t[:], v_sb[o].dsem, r=[v_sb[o]])
            K.drain()


    fT = K.sb(es0, "fT", [128, 4, S], BF16)
    if STOP >= 2 and not SKIP01:
        with ExitStack() as es:
            tabs = [K.sb(es, "tab%d" % t, [128, 8192], BF16, dma=True) for t in range(3)]
            zin = [K.sb(es, "zin%d" % i, [128, 8, 1024], BF16, dma=True) for i in range(2)]
            ysb = [K.sb(es, "ysb%d" % i, [128, 2, 512], BF16, dma=True) for i in range(4)]
            for t, td in enumerate((tabA_d, tabB_d, tabC_d)):
                K.dma("sp", tabs[t].t[:], td[:, :], tabs[t].dsem, w=[tabs[t]])
            PQv = PQ_d.rearrange("(s1 s2) c -> s1 s2 c", s2=64)
            def load_z(s2c):
                z_ = zin[s2c % 2]
                K.dma("sp", z_.t[:], PQv[:, s2c * 8:(s2c + 1) * 8, :], z_.dsem, w=[z_])

            load_z(0)
            for s2c in range(8):
                z = zin[s2c % 2]
                if s2c + 1 < 8:
                    load_z(s2c + 1)
                for s2l in range(8):
                    s2 = s2c * 8 + s2l
                    cs = slice(s2 * 128, (s2 + 1) * 128)
                    A, B, C = tabs[0].t[:, cs], tabs[1].t[:, cs], tabs[2].t[:, cs]
                    P, Q = z.t[:, s2l, 0:512], z.t[:, s2l, 512:1024]
                    pr, pi = psb[(2 * s2) % 8], psb[(2 * s2 + 1) % 8]
                    K.mm_group([lambda e: e.matmul(pr.t[:, :], lhsT=A, rhs=P, start=True, stop=False),
                                lambda e: e.matmul(pr.t[:, :], lhsT=C, rhs=Q, start=False, stop=True)],
                               r=[tabs[0], tabs[2], z], w=[pr])
                    K.mm_group([lambda e: e.matmul(pi.t[:, :], lhsT=B, rhs=P, start=True, stop=False),
                                lambda e: e.matmul(pi.t[:, :], lhsT=A, rhs=Q, start=False, stop=True)],
                               r=[tabs[0], tabs[1], z], w=[pi])
                    yb = ysb[s2 % 4]
                    K.op("act", lambda e: e.activation(out=yb.t[:, 0, :], in_=pr.t[:, :], func=AF.Copy), r=[pr], w=[yb])
                    K.op("dve", lambda e: e.tensor_copy(out=yb.t[:, 1, :], in_=pi.t[:, :]), r=[pi], w=[yb])
                    K.dma("sp", Y_d[s2], yb.t[:], yb.dsem, r=[yb])
            K.drain()
        with ExitStack() as es:
            cs2 = K.sb(es, "cs2", [128, 64], BF16, dma=True)
            K.dma("sp", cs2.t[:], cs2_d[:, :], cs2.dsem, w=[cs2])
            yin_t = [es.enter_context(nc.sbuf_tensor("sb_yin%d" % i, [128, 16, 512], BF16)) for i in range(2)]
            yin = [[Buf(yin_t[i], K.new_dsem()) for _ in range(2)] for i in range(2)]
            for i in range(2):
                K.allbufs.extend(yin[i])
            cnt = 0
            def load_y(k1c):
                yt_ = yin_t[k1c % 2]
                for ri, yb in enumerate(yin[k1c % 2]):
                    K.dma("sp" if ri == 0 else "act", yt_[64 * ri:64 * ri + 64, :, :],
                          Y_d[:, k1c * 16:(k1c + 1) * 16, ri, :], yb.dsem, w=[yb])

            load_y(0)
            for k1c in range(8):
                yt = yin_t[k1c % 2]
                yl, yh = yin[k1c % 2]
                if k1c + 1 < 8:
                    load_y(k1c + 1)
                for chc in range(4):
                    for hh in range(2):
                        pb = psb[cnt % 8]
                        K.mm_group([(lambda e, k=k: e.matmul(pb.t[:, k * 64:(k + 1) * 64],
                                                             lhsT=yt[:, hh * 8 + k, chc * 128:(chc + 1) * 128],
                                                             rhs=cs2.t[:, :], start=True, stop=True)) for k in range(8)],
                                   r=[yl, yh, cs2], w=[pb])
                        k10 = k1c * 16 + hh * 8
                        dst = fT.t[:, chc, :].rearrange("p (k2 k1) -> p k2 k1", k1=128)[:, :, k10:k10 + 8]
                        src = pb.t[:, :].rearrange("p (k1 k2) -> p k2 k1", k2=64)
                        if cnt % 2 == 0:
                            K.op("act", lambda e: e.activation(out=dst, in_=src, func=AF.Copy), r=[pb])
                        else:
                            K.op("dve", lambda e: e.tensor_copy(out=dst, in_=src), r=[pb])
                        cnt += 1
            K.drain()

    if STOP >= 3:
        with ExitStack() as es:
            em = K.sb(es, "em", [128, 12, 512], BF16, dma=True)
            gqT = K.sb(es, "gqT", [128, 12], F32, dma=True)
            gkT = K.sb(es, "gkT", [128, 12], F32, dma=True)
            GT = K.sb(es, "GT", [128, 12], F32)
            K.dma("sp", em.t[:], em_d.rearrange("p (a b) -> p a b", b=512), em.dsem, w=[em])
            K.dma("sp", gqT.t[:], qn_d.rearrange("(pr h2) d -> (h2 d) pr", h2=2), gqT.dsem, w=[gqT],
                  allow_slow_non_contiguous=True)
            K.dma("sp", gkT.t[:], kn_d.rearrange("(pr h2) d -> (h2 d) pr", h2=2), gkT.dsem, w=[gkT],
                  allow_slow_non_contiguous=True)
            K.op("dve", lambda e: e.scalar_tensor_tensor(out=GT.t[:], in0=gqT.t[:], scalar=0.125, in1=gkT.t[:],
                                                         op0=ALU.mult, op1=ALU.mult), r=[gqT, gkT], w=[GT])
            RQ, RK, RV = 4, 6, 12
            qsb = [K.sb(es, "qsb%d" % i, [128, 512], BF16, dma=True) for i in range(RQ)]
            ksb = [K.sb(es, "ksb%d" % i, [128, 512], BF16, dma=True) for i in range(RK)]
            vsb = [K.sb(es, "vsb%d" % i, [128, 8, 65], BF16, dma=True) for i in range(RV)]
            qTb = [K.sb(es, "qT%d" % i, [128, 4, 128], BF16) for i in range(2)]
            kTb = [K.sb(es, "kT%d" % i, [128, 4, 128], BF16) for i in range(RK)]
            pex = [K.sb(es, "pex%d" % i, [128, 512], BF16) for i in range(3)]
            pmk = [K.sb(es, "pmk%d" % i, [128, 512], BF16) for i in range(8)]
            osb = [K.sb(es, "osb%d" % i, [128, 520], F32, dma=True) for i in range(2)]
            for i_, ob_ in enumerate(osb):
                ob_.swsem = K.new_swsem("sw_osb%d" % i_)
            tq_ps, tk_ps = psb[0], psb[1]
            S_ps = [psb[2], psb[3]]
            O_ps = [[psb[4], psb[5]], [psb[6], psb[7]]]
            blocks, ktiles = [], []
            for c, dil in enumerate(DILS):
                nqb = S // dil // 128
                for r in range(dil):
                    base = len(ktiles)
                    for kt in range(nqb + 1):
                        ktiles.append((c, r, kt))
                    for qb in range(nqb):
                        blocks.append((c, r, qb, base + qb, base + qb + 1))
            NB = len(blocks)
            views = []
            for c, dil in enumerate(DILS):
                views.append((QK_d.rearrange("(i d) c -> d i c", d=dil), V_d.rearrange("(i d) c -> d i c", d=dil),
                              O_d[c].rearrange("(i d) e -> d i e", d=dil)))
            st = {"kl": 0, "kt": 0, "sc": 0}

            def load_q(b):
                c, r, qb = blocks[b][0:3]
                qb_ = qsb[b % RQ]
                K.dma("sp", qb_.t[:], views[c][0][r, 128 * qb:128 * qb + 128, c * 512:(c + 1) * 512], qb_.dsem,
                      w=[qb_])

            def load_kv(ki):
                c, r, kt = ktiles[ki]
                dil = DILS[c]
                L = S // dil
                nqb = L // 128
                kb, vb = ksb[ki % RK], vsb[ki % RV]
                a, b_ = 128 * kt - 64, 128 * kt + 64
                p0, p1 = 0, 128
                if kt == 0:
                    a, p0 = 0, 64
                if kt == nqb:
                    b_, p1 = L, 64
                if kt == 0 or kt == nqb:
                    K.op("pool", lambda e: e.memset(kb.t[:], 0.0), w=[kb])
                    K.op("pool", lambda e: e.memset(vb.t[:], 0.0), w=[vb])
                K.op("pool", lambda e: e.memset(vb.t[p0:p1, :, 64:65], 1.0), w=[vb])
                K.dma("sp", kb.t[p0:p1, :], views[c][0][r, a:b_, 1536 + c * 512:1536 + (c + 1) * 512], kb.dsem,
                      w=[kb])
                K.dma("sp", vb.t[p0:p1, :, 0:64], views[c][1][r, a:b_, c * 512:(c + 1) * 512].rearrange(
                    "n (h d) -> n h d", d=64), vb.dsem, w=[vb])

            def ensure_loaded(upto):
                while st["kl"] <= upto:
                    load_kv(st["kl"])
                    st["kl"] += 1

            def tr_q(b):
                qb_, qt = qsb[b % RQ], qTb[b % 2]
                tv = tq_ps.t[:].bitcast(BF16)
                K.mm_group([(lambda e, pr=pr: e.transpose(out=tv[:, pr * 128:(pr + 1) * 128],
                                                          in_=qb_.t[:, pr * 128:(pr + 1) * 128],
                                                          identity=identb.t[:])) for pr in range(4)],
                           r=[qb_, identb], w=[tq_ps])
                K.op("dve", lambda e: e.tensor_copy(out=qt.t[:].rearrange("p a b -> p (a b)"), in_=tv[:, 0:512]),
                     r=[tq_ps], w=[qt])

            def tr_k(ki):
                c = ktiles[ki][0]
                kb, kt_ = ksb[ki % RK], kTb[ki % RK]
                tv = tk_ps.t[:].bitcast(BF16)
                K.mm_group([(lambda e, pr=pr: e.transpose(out=tv[:, pr * 128:(pr + 1) * 128],
                                                          in_=kb.t[:, pr * 128:(pr + 1) * 128],
                                                          identity=identb.t[:])) for pr in range(4)],
                           r=[kb, identb], w=[tk_ps])
                K.op("dve", lambda e: e.tensor_tensor(
                    out=kt_.t[:], in0=tv[:, 0:512].rearrange("p (a b) -> p a b", b=128),
                    in1=bc(GT.t[:, 4 * c:4 * c + 4].unsqueeze(2), [128, 4, 128]), op=ALU.mult),
                    r=[tk_ps, GT], w=[kt_])

            def ensure_tr(upto):
                while st["kt"] <= upto:
                    tr_k(st["kt"])
                    st["kt"] += 1

            def s_block(b, prs):
                c, r, qb, kl, kh = blocks[b]
                qt, klo, khi = qTb[b % 2], kTb[kl % RK], kTb[kh % RK]
                half = prs[0] // 2
                fns = []
                for pp, pr in enumerate(prs):
                    for kk, off in ((klo, 0), (khi, 128)):
                        for h2 in range(2):
                            rs = slice(64 * h2, 64 * h2 + 64)
                            fns.append(lambda e, h2=h2, kk=kk, off=off, pp=pp, pr=pr, rs=rs: e.matmul(
                                S_ps[h2].t[:, pp * 256 + off:pp * 256 + off + 128], lhsT=kk.t[rs, pr, :],
                                rhs=qt.t[rs, pr, :], start=True, stop=True))
                K.mm_group(fns, r=[klo, khi, qt], w=[S_ps[0], S_ps[1]])
                for h2 in range(2):
                    px, pm = pex[st["sc"] % 3], pmk[(b % 2) * 4 + half * 2 + h2]
                    st["sc"] += 1
                    Sp = S_ps[h2]
                    K.op("act", lambda e: e.activation(out=px.t[:], in_=Sp.t[:, :], func=AF.Exp), r=[Sp], w=[px])
                    K.op("dve", lambda e: e.tensor_tensor(out=pm.t[:], in0=px.t[:],
                                                          in1=em.t[:, (c * 2 + half) * 2 + h2, :],
                                                          op=ALU.mult), r=[px, em], w=[pm])

            def pv_block(b, prs):
                c, r, qb, kl, kh = blocks[b]
                vlo, vhi = vsb[kl % RV], vsb[kh % RV]
                Ob = O_ps[b % 2]
                fns, pms = [], []
                Obk = Ob[prs[0] // 2]
                for pr in prs:
                    for h2 in range(2):
                        pm = pmk[(b % 2) * 4 + (pr // 2) * 2 + h2]
                        pms.append(pm)
                        pp = pr % 2
                        h = 2 * pr + h2
                        oc = slice((h % 4) * 65, (h % 4) * 65 + 65)
                        fns.append(lambda e, pm=pm, pp=pp, h=h, oc=oc: e.matmul(
                            Obk.t[:, oc], lhsT=pm.t[:, pp * 256:pp * 256 + 128], rhs=vlo.t[:, h, :],
                            start=True, stop=False))
                        fns.append(lambda e, pm=pm, pp=pp, h=h, oc=oc: e.matmul(
                            Obk.t[:, oc], lhsT=pm.t[:, pp * 256 + 128:pp * 256 + 256], rhs=vhi.t[:, h, :],
                            start=False, stop=True))
                K.mm_group(fns, r=pms + [vlo, vhi], w=[Obk])

            if P3MODE > 0:
                for t_ in range(3):
                    load_q(t_)
                ensure_loaded(blocks[2][4])
                tr_q(0)
                ensure_tr(blocks[0][4])
                s_block(0, (0, 1))
                s_block(0, (2, 3))
                tr_q(1)
                ensure_tr(blocks[1][4])
                for b in range(NB):
                    if b + 3 < NB:
                        load_q(b + 3)
                        ensure_loaded(blocks[b + 3][4])
                    if b + 2 < NB:
                        tr_q(b + 2)
                        ensure_tr(blocks[b + 2][4])
                    c, r, qb = blocks[b][0:3]
                    ob, Ob = osb[b % 2], O_ps[b % 2]
                    for half in range(2):
                        if b + 1 < NB:
                            s_block(b + 1, (2 * half, 2 * half + 1))
                        pv_block(b, (2 * half, 2 * half + 1))
                    K.op("act", lambda e: e.activation(out=ob.t[:, 0:260], in_=Ob[0].t[:, 0:260], func=AF.Copy),
                         r=[Ob[0]], w=[ob])
                    K.op("dve", lambda e: e.tensor_copy(out=ob.t[:, 260:520], in_=Ob[1].t[:, 0:260]),
                         r=[Ob[1]], w=[ob])
                    if c == 0:
                        K.dma("sp", views[c][2][r, 128 * qb:128 * qb + 128, :], ob.t[:], ob.dsem, r=[ob])
                    else:
                        if blocks[b - 1][0] != c:
                            for ob_ in osb:
                                for sm in (ob_.dsem, ob_.swsem):
                                    if sm.n > 0 and K.waited["pool"].get(sm, 0) < sm.n:
                                        nc.gpsimd.wait_ge(sm.h, sm.n)
                                        K.waited["pool"][sm] = sm.n
                        K.dma("pool", views[c][2][r, 128 * qb:128 * qb + 128, :], ob.t[:], ob.swsem, r=[ob],
                              accum_op=ALU.add)
            K.drain()


    if STOP >= 4:
        with ExitStack() as es:
            NS = 4
            o_in = [[K.sb(es, "oin%d_%d" % (c, i), [128, 520], F32, dma=True) for c in range(1)] for i in range(NS)]
            ga_in = [K.sb(es, "ga_in%d" % i, [128, 512], BF16, dma=True) for i in range(NS)]
            gf_in = [K.sb(es, "gf_in%d" % i, [128, 512], BF16, dma=True) for i in range(NS)]
            xr = [K.sb(es, "xr%d" % i, [128, D], F32, dma=True) for i in range(NS)]
            osum = [K.sb(es, "osum%d" % i, [128, 520], F32) for i in range(3)]
            rden = [K.sb(es, "rden%d" % i, [128, 8], F32) for i in range(3)]
            ya32 = [K.sb(es, "ya32_%d" % i, [128, 512], F32) for i in range(3)]
            ya_bf = [K.sb(es, "ya_bf%d" % i, [128, 512], BF16) for i in range(3)]
            yT = [K.sb(es, "yT%d" % i, [128, 8, 128], BF16) for i in range(2)]
            out_sb = [K.sb(es, "out_sb%d" % i, [128, D], F32) for i in range(2)]
            for i_, ob_ in enumerate(out_sb):
                ob_.dsem = K.new_swsem("sw_out%d" % i_)
            tpa, tpg = [psb[0], psb[6]], [psb[1], psb[7]]
            outp = [[psb[2], psb[3]], [psb[4], psb[5]]]

            def load4(i):
                o = i % NS
                rows = slice(128 * i, 128 * i + 128)
                K.dma("sp", o_in[o][0].t[:], O_acc[rows, :], o_in[o][0].dsem, w=[o_in[o][0]])
                K.dma("sp", ga_in[o].t[:], GA_d[rows, :], ga_in[o].dsem, w=[ga_in[o]])
                K.dma("sp", gf_in[o].t[:], GF_d[rows, :], gf_in[o].dsem, w=[gf_in[o]])
                K.dma("act", xr[o].t[:], x_d[rows, :], xr[o].dsem, w=[xr[o]])

            def stage_a1(i):
                o3, s3 = i % 3, i % NS
                os_, rd, y32, ybf = o_in[s3][0], rden[o3], ya32[o3], ya_bf[o3]
                osv = os_.t[:].rearrange("p (h e) -> p h e", e=65)
                K.op("dve", lambda e: e.reciprocal(out=rd.t[:], in_=osv[:, :, 64]), r=[os_], w=[rd])
                K.op("dve", lambda e: e.tensor_tensor(out=y32.t[:].rearrange("p (h d) -> p h d", d=64),
                                                      in0=osv[:, :, 0:64],
                                                      in1=bc(rd.t[:].unsqueeze(2), [128, 8, 64]),
                                                      op=ALU.mult), r=[os_, rd], w=[y32])
                K.op("pool", lambda e: e.tensor_tensor(out=ybf.t[:], in0=y32.t[:], in1=ga_in[s3].t[:], op=ALU.mult),
                     r=[y32, ga_in[s3]], w=[ybf])

            def stage_a2(i):
                o, o3, s3 = i % 2, i % 3, i % NS
                ybf, yt = ya_bf[o3], yT[o]
                tav = tpa[o].t[:].bitcast(BF16)
                tgv = tpg[o].t[:].bitcast(BF16)
                K.mm_group([(lambda e, j=j: e.transpose(out=tgv[:, j * 128:(j + 1) * 128],
                                                        in_=gf_in[s3].t[:, j * 128:(j + 1) * 128],
                                                        identity=identb.t[:]))
                            for j in range(4)], r=[gf_in[s3], identb], w=[tpg[o]])
                K.op("dve", lambda e: e.tensor_tensor(out=yt.t[:, 0:4, :],
                                                      in0=tgv[:, 0:512].rearrange("p (a b) -> p a b", b=128),
                                                      in1=fT.t[:, :, 128 * i:128 * i + 128], op=ALU.mult),
                     r=[tpg[o]], w=[yt])
                K.mm_group([(lambda e, j=j: e.transpose(out=tav[:, j * 128:(j + 1) * 128],
                                                        in_=ybf.t[:, j * 128:(j + 1) * 128], identity=identb.t[:]))
                            for j in range(4)], r=[ybf, identb], w=[tpa[o]])
                K.op("act", lambda e: e.activation(out=yt.t[:, 4:8, :].rearrange("p a b -> p (a b)"),
                                                   in_=tav[:, 0:512], func=AF.Copy), r=[tpa[o]], w=[yt])

            def stage_b_mm(i):
                o = i % 2
                yt = yT[o]
                for half in range(2):
                    pb = outp[o][half]
                    K.mm_group([(lambda e, cc=cc: e.matmul(pb.t[:, :], lhsT=yt.t[:, cc, :],
                                                           rhs=wout_bf.t[:, cc, half * 512:(half + 1) * 512],
                                                           start=(cc == 0), stop=(cc == 7))) for cc in range(8)],
                               r=[yt, wout_bf], w=[pb])

            def stage_b_res(i):
                o, s3 = i % 2, i % NS
                rows = slice(128 * i, 128 * i + 128)
                for half in range(2):
                    pb = outp[o][half]
                    K.op("dve", lambda e: e.tensor_tensor(out=out_sb[o].t[:, half * 512:(half + 1) * 512],
                                                          in0=pb.t[:, :],
                                                          in1=xr[s3].t[:, half * 512:(half + 1) * 512],
                                                          op=ALU.add), r=[pb, xr[s3]], w=[out_sb[o]])
                K.dma("pool", y_d[rows, :], out_sb[o].t[:], out_sb[o].dsem, r=[out_sb[o]])

            for t_ in range(3):
                load4(t_)
            stage_a1(0)
            stage_a1(1)
            stage_a2(0)
            for i in range(NT):
                if i + 3 < NT:
                    load4(i + 3)
                if i + 2 < NT:
                    stage_a1(i + 2)
                if i + 1 < NT:
                    stage_a2(i + 1)
                stage_b_mm(i)
                stage_b_res(i)
    K.drain()
    K.root.close()
    return nc


_CONSTS = None


def kernel(x, norm_w, w_in, q_norm_w, k_norm_w, w_fourier, w_out):
    global _CONSTS
    if _CONSTS is None:
        _CONSTS = _consts()
    nc = build()
    x = np.ascontiguousarray(np.asarray(x, dtype=np.float32))
    shared = {
        "norm_w": np.asarray(norm_w, np.float32), "w_in": np.asarray(w_in, np.float32),
        "q_norm_w": np.asarray(q_norm_w, np.float32), "k_norm_w": np.asarray(k_norm_w, np.float32),
        "w_fourier": np.asarray(w_fourier, np.float32), "w_out": np.asarray(w_out, np.float32),
    }
    shared.update(_CONSTS)
    in_maps = [dict(shared, x=x[b]) for b in range(NCORES)]
    if os.environ.get("MK_TRACE") == "1":
        res = run_bass_kernel_spmd(nc, in_maps, core_ids=list(range(NCORES)), trace=True)
        print("EXEC_TIME_NS", res.exec_time_ns)
    else:
        res = run_bass_kernel_spmd(nc, in_maps, core_ids=list(range(NCORES)))
    if DEBUG:
        kernel.last = res
    ys = [np.asarray(r["y"]) for r in res.results]
    ys += [np.zeros_like(ys[0])] * (8 - len(ys))
    return np.stack(ys, axis=0).astype(np.float32)
```

```python
import os
from contextlib import ExitStack

import ml_dtypes
import numpy as np

import concourse.bass as bass
import concourse.mybir as mybir
from concourse.alu_op_type import AluOpType as ALU
from concourse.bass_utils import run_bass_kernel_spmd

F32 = mybir.dt.float32
BF16 = mybir.dt.bfloat16
AF = mybir.ActivationFunctionType
AX = mybir.AxisListType

S = 8192
D = 1024
NT = S // 128
INW = 6144
NCOL = 6656
DILS = (1, 4, 16)
EPS = 1e-6
DEBUG = bool(int(os.environ.get("MK_DEBUG", "0")))
NTILES_DBG = int(os.environ.get("MK_NT", str(NT)))
STOP = int(os.environ.get("MK_STOP", "9"))
NCORES = int(os.environ.get("MK_CORES", "8"))
P3MODE = int(os.environ.get("MK_P3", "9"))
P3X = int(os.environ.get("MK_P3X", "9"))
SKIP01 = bool(int(os.environ.get("MK_SKIP01", "0")))


def _consts():
    bf = ml_dtypes.bfloat16
    c = {}
    c["identb"] = np.eye(128, dtype=np.float32).astype(bf)
    kap = 1.0 / np.sqrt(S * 64.0)
    cc = np.arange(64)
    ang = 2 * np.pi * np.outer(cc, cc) / 64.0
    cos2 = np.concatenate([np.cos(ang), np.cos(ang)], axis=1) * kap
    sin2 = np.concatenate([np.sin(ang), np.sin(ang)], axis=1) * kap
    c["cs64"] = np.concatenate([cos2, sin2], axis=1).astype(np.float32)
    s1 = np.arange(128)[:, None, None].astype(np.float64)
    s2 = np.arange(64)[None, :, None].astype(np.float64)
    k1 = np.arange(128)[None, None, :].astype(np.float64)
    ph = 2 * np.pi * (s1 * k1 / 128.0 + s2 * k1 / 8192.0)
    A = np.cos(ph)
    B = np.sin(ph)
    c["tabA"] = A.reshape(128, 8192).astype(np.float32).astype(bf)
    c["tabB"] = B.reshape(128, 8192).astype(np.float32).astype(bf)
    c["tabC"] = (-B).reshape(128, 8192).astype(np.float32).astype(bf)
    s2v = np.arange(64)[:, None].astype(np.float64)
    k2v = np.arange(64)[None, :].astype(np.float64)
    ph2 = 2 * np.pi * s2v * k2v / 64.0
    c["cs2"] = np.concatenate([np.cos(ph2), -np.sin(ph2)], axis=0).astype(np.float32).astype(bf)
    j = np.arange(128)[:, None].astype(np.float64)
    i = np.arange(128)[None, :].astype(np.float64)
    em = np.zeros((128, 3, 2, 2, 2, 2, 128), dtype=np.float64)
    for ci, dil in enumerate(DILS):
        for h in range(8):
            slope = 2.0 ** (-(h + 1))
            lo = np.where(j >= i, np.exp(-slope * dil * np.abs(j - 64 - i)), 0.0)
            hi = np.where(j <= i, np.exp(-slope * dil * np.abs(64 + j - i)), 0.0)
            pr, h2 = h // 2, h % 2
            em[:, ci, pr // 2, h2, pr % 2, 0, :] = lo
            em[:, ci, pr // 2, h2, pr % 2, 1, :] = hi
    c["emask"] = em.reshape(128, 12 * 512).astype(np.float32).astype(bf)
    return c


class Sem:
    def __init__(self, K, name):
        self.h = K.root.enter_context(K.nc.semaphore(name))
        self.n = 0


class Buf:
    def __init__(self, t, dsem=None):
        self.t = t
        self.w = None
        self.r = []
        self.dsem = dsem


class KB:
    def __init__(self, nc):
        self.nc = nc
        self.root = ExitStack()
        self.E = {"pe": nc.tensor, "act": nc.scalar, "dve": nc.vector, "pool": nc.gpsimd, "sp": nc.sync}
        self.esem = {e: Sem(self, "e_" + e) for e in ("pe", "act", "dve", "pool")}
        self.waited = {e: {} for e in self.E}
        self.dpool = [Sem(self, "d%d" % i) for i in range(72)]
        self.dnext = 0
        self.phase_sem = Sem(self, "phase")
        self.allbufs = []
        self.swsems = []

    def sb(self, es, name, shape, dt, dma=False):
        t = es.enter_context(self.nc.sbuf_tensor("sb_" + name, list(shape), dt))
        b = Buf(t, self.new_dsem() if dma else None)
        self.allbufs.append(b)
        return b

    def ps(self, es, name):
        t = es.enter_context(self.nc.psum_tensor("ps_" + name, [128, 512], F32))
        b = Buf(t)
        self.allbufs.append(b)
        return b

    def new_swsem(self, name):
        sm = Sem(self, name)
        self.swsems.append(sm)
        return sm

    def new_dsem(self):
        s = self.dpool[self.dnext]
        self.dnext += 1
        return s

    def _waits(self, eng, r, w):
        need = {}

        def add(ev):
            if ev is None:
                return
            s, v = ev
            if need.get(s, 0) < v:
                need[s] = v

        for b in r:
            add(b.w)
        for b in w:
            add(b.w)
            for ev in b.r:
                add(ev)
        for s, v in need.items():
            if self.waited[eng].get(s, 0) < v:
                self.E[eng].wait_ge(s.h, v)
                self.waited[eng][s] = v

    def _record(self, ev, r, w):
        for b in r:
            b.r.append(ev)
        for b in w:
            b.w = ev
            b.r = []

    def op(self, eng, fn, r=(), w=()):
        self._waits(eng, r, w)
        ins = fn(self.E[eng])
        s = self.esem[eng]
        ins.then_inc(s.h, 1)
        s.n += 1
        self._record((s, s.n), r, w)
        return ins

    def mm_group(self, fns, r=(), w=()):
        self._waits("pe", r, w)
        ins = None
        for fn in fns:
            ins = fn(self.nc.tensor)
        s = self.esem["pe"]
        ins.then_inc(s.h, 1)
        s.n += 1
        self._record((s, s.n), r, w)

    def dma(self, q, out, in_, sem, r=(), w=(), **kw):
        self._waits(q, r, w)
        ins = self.E[q].dma_start(out=out, in_=in_, **kw)
        ins.then_inc(sem.h, 16)
        sem.n += 16
        self._record((sem, sem.n), r, w)

    def drain(self):
        sp = self.nc.sync
        for s in list(self.esem.values()) + self.dpool[: self.dnext] + self.swsems:
            if s.n > 0 and self.waited["sp"].get(s, 0) < s.n:
                sp.wait_ge(s.h, s.n)
                self.waited["sp"][s] = s.n
        ps = self.phase_sem
        sp.sem_inc(ps.h, 1)
        ps.n += 1
        for e in ("pe", "act", "dve", "pool"):
            self.E[e].wait_ge(ps.h, ps.n)
            for s in list(self.esem.values()) + self.dpool[: self.dnext]:
                self.waited[e][s] = s.n
        for b in self.allbufs:
            b.w = None
            b.r = []
        self.allbufs = []
        self.dnext = 0


def bc(ap, shape):
    return ap.to_broadcast(list(shape))


def build():
    nc = bass.Bass("TRN2", target_bir_lowering=False)
    K = KB(nc)
    dt_in = lambda n, s, d=F32: nc.dram_tensor(n, list(s), d, kind="ExternalInput").ap()
    x_d = dt_in("x", [S, D])
    nw_d = dt_in("norm_w", [D])
    win_d = dt_in("w_in", [D, INW])
    qn_d = dt_in("q_norm_w", [24, 64])
    kn_d = dt_in("k_norm_w", [24, 64])
    wf_d = dt_in("w_fourier", [8, 64, 64])
    wout_d = dt_in("w_out", [D, D])
    identb_d = dt_in("identb", [128, 128], BF16)
    cs64_d = dt_in("cs64", [64, 256])
    tabA_d = dt_in("tabA", [128, 8192], BF16)
    tabB_d = dt_in("tabB", [128, 8192], BF16)
    tabC_d = dt_in("tabC", [128, 8192], BF16)
    cs2_d = dt_in("cs2", [128, 64], BF16)
    em_d = dt_in("emask", [128, 12 * 512], BF16)
    y_d = nc.dram_tensor("y", [S, D], F32, kind="ExternalOutput").ap()
    skind = "ExternalOutput" if DEBUG else "Internal"
    scr = lambda n, s, d: nc.dram_tensor(n, list(s), d, kind=skind).ap()
    PQ_d = scr("PQ_s", [S, 1024], BF16)
    GF_d = scr("GF_s", [S, 512], BF16)
    GA_d = scr("GA_s", [S, 512], BF16)
    QK_d = scr("QK_s", [S, 3072], BF16)
    V_d = scr("V_s", [S, 1536], BF16)
    Y_d = scr("Y_s", [64, 128, 2, 512], BF16)
    O_acc = scr("O_s", [S, 520], F32)
    O_d = [O_acc, O_acc, O_acc]

    es0 = K.root
    identb = K.sb(es0, "identb", [128, 128], BF16, dma=True)
    mhalf = K.sb(es0, "mhalf", [128, 48], F32)
    psb = [K.ps(es0, "psb%d" % i) for i in range(8)]
    wout_bf = K.sb(es0, "wout_bf", [128, 8, 1024], BF16)
    K.dma("sp", identb.t[:], identb_d[:, :], identb.dsem, w=[identb])
    K.op("pool", lambda e: e.memset(mhalf.t[:], -0.5), w=[mhalf])

    with ExitStack() as es:
      if not SKIP01:
        w_bf = K.sb(es, "w_bf", [128, 8, NCOL], BF16)
        with ExitStack() as es_p0:
            nw = K.sb(es_p0, "nw", [128, 8], F32, dma=True)
            stage = [K.sb(es_p0, "stage%d" % i, [128, INW], F32, dma=True) for i in range(2)]
            wu_bf = [K.sb(es_p0, "wu_bf%d" % i, [128, 512], BF16) for i in range(8)]
            wuT = [K.sb(es_p0, "wuT%d" % i, [128, 4, 128], BF16) for i in range(2)]
            cs64 = K.sb(es_p0, "cs64", [64, 256], F32, dma=True)
            wf_sb = K.sb(es_p0, "wf_sb", [64, 8, 64], F32, dma=True)
            BD = K.sb(es_p0, "BD", [128, 4, 256], BF16)
            K.dma("sp", nw.t[:], nw_d.rearrange("(kc p) -> p kc", p=128), nw.dsem, w=[nw],
                  allow_slow_non_contiguous=True)
            K.dma("sp", cs64.t[:], cs64_d[:, :], cs64.dsem, w=[cs64])
            K.dma("sp", wf_sb.t[:], wf_d.rearrange("g c d -> c g d"), wf_sb.dsem, w=[wf_sb])
            wf2 = wf_sb.t[:].rearrange("c g d -> c (g d)")
            K.mm_group([lambda e: e.matmul(psb[0].t[:, :], lhsT=cs64.t[:, 0:128], rhs=wf2, start=True, stop=True)],
                       r=[cs64, wf_sb], w=[psb[0]])
            K.mm_group([lambda e: e.matmul(psb[1].t[:, :], lhsT=cs64.t[:, 128:256], rhs=wf2, start=True, stop=True)],
                       r=[cs64, wf_sb], w=[psb[1]])
            K.op("pool", lambda e: e.memset(BD.t[:], 0.0), w=[BD])
            for half in range(2):
                pr = slice(64 * half, 64 * half + 64)
                for t in range(2):
                    src = psb[t].t[pr, :].rearrange("p (j two d) -> p j two d", two=2, d=64)[:, :, half, :]
                    dst = BD.t[pr, :, 128 * t + 64 * half:128 * t + 64 * half + 64]
                    K.op("dve", lambda e: e.tensor_copy(out=dst, in_=src), r=[psb[t]], w=[BD])
            def fold_kc(kc):
                pt = psb[2 + kc % 2]
                pv = pt.t[:].bitcast(BF16)
                K.mm_group([(lambda e, j=j: e.transpose(out=pv[:, j * 128:(j + 1) * 128],
                                                        in_=wu_bf[kc].t[:, j * 128:(j + 1) * 128],
                                                        identity=identb.t[:])) for j in range(4)],
                           r=[wu_bf[kc], identb], w=[pt])
                wT = wuT[kc % 2]
                K.op("act", lambda e: e.activation(out=wT.t[:].rearrange("p a b -> p (a b)"), in_=pv[:, 0:512],
                                                   func=AF.Copy), r=[pt], w=[wT])
                pa, pb2 = psb[4 + 2 * (kc % 2)], psb[5 + 2 * (kc % 2)]
                for jj, pb in ((0, pa), (1, pb2)):
                    K.mm_group([(lambda e, j=j: e.matmul(pb.t[:, (j % 2) * 256:(j % 2) * 256 + 256],
                                                         lhsT=wT.t[:, j, :], rhs=BD.t[:, j, :],
                                                         start=True, stop=True))
                                for j in (2 * jj, 2 * jj + 1)], r=[wT, BD], w=[pb])
                    for t in range(2):
                        src = pb.t[:, :].rearrange("p (j two d) -> p j two d", two=2, d=128)[:, :, t, :]
                        dst = w_bf.t[:, kc, 512 * t + 256 * jj:512 * t + 256 * jj + 256].rearrange(
                            "p (j d) -> p j d", d=128)
                        K.op("dve" if t == 0 else "act",
                             (lambda e: e.tensor_copy(out=dst, in_=src)) if t == 0 else
                             (lambda e: e.activation(out=dst, in_=src, func=AF.Copy)),
                             r=[pb], w=[w_bf])

            win_v = win_d.rearrange("(kc p) n -> p kc n", p=128)
            for kc in range(8):
                st = stage[kc % 2]
                K.dma("sp", st.t[:], win_v[:, kc, :], st.dsem, w=[st])
                sc = nw.t[:, kc:kc + 1]
                K.op("dve", lambda e: e.tensor_scalar(out=w_bf.t[:, kc, 1024:4096], in0=st.t[:, 512:3584],
                                                      scalar1=sc, scalar2=None, op0=ALU.mult),
                     r=[st, nw], w=[w_bf])
                K.op("act", lambda e: e.activation(out=w_bf.t[:, kc, 4096:NCOL], in_=st.t[:, 3584:INW],
                                                   func=AF.Copy, scale=sc), r=[st, nw], w=[w_bf])
                K.op("pool", lambda e: e.tensor_scalar(out=wu_bf[kc].t[:], in0=st.t[:, 0:512],
                                                       scalar1=sc, scalar2=None, op0=ALU.mult),
                     r=[st, nw], w=[wu_bf[kc]])
                fold_kc(kc)
            wout_v = wout_d.rearrange("(cc p) n -> p cc n", p=128)
            for h in range(2):
                st = stage[h]
                K.dma("sp", st.t[:, 0:4096].rearrange("p (c n) -> p c n", c=4), wout_v[:, 4 * h:4 * h + 4, :],
                      st.dsem, w=[st])
                K.op("dve", lambda e: e.tensor_scalar(
                    out=wout_bf.t[:, 4 * h:4 * h + 4, :], in0=st.t[:, 0:4096].rearrange("p (c n) -> p c n", c=4),
                    scalar1=0.5, scalar2=None, op0=ALU.mult), r=[st], w=[wout_bf])
            K.drain()
        with ExitStack() as es1:
            xb = [K.sb(es1, "xb%d" % i, [128, D], F32, dma=True) for i in range(3)]
            xs = [K.sb(es1, "xs%d" % i, [128, D], BF16) for i in range(2)]
            xT = [K.sb(es1, "xT%d" % i, [128, D], BF16) for i in range(2)]
            ssx = [K.sb(es1, "ssx%d" % i, [128, 2], F32) for i in range(2)]
            pq_sb = [K.sb(es1, "pq_sb%d" % i, [128, 1024], BF16, dma=True) for i in range(2)]
            gf_sb = [K.sb(es1, "gf_sb%d" % i, [128, 512], BF16, dma=True) for i in range(2)]
            ga_sb = [K.sb(es1, "ga_sb%d" % i, [128, 512], BF16, dma=True) for i in range(2)]
            qk_sb = [K.sb(es1, "qk_sb%d" % i, [128, 3072], BF16, dma=True) for i in range(2)]
            v_sb = [K.sb(es1, "v_sb%d" % i, [128, 1536], BF16, dma=True) for i in range(2)]
            sq = [K.sb(es1, "sq%d" % i, [128, 512], BF16) for i in range(3)]
            th = [K.sb(es1, "th%d" % i, [128, 512], F32) for i in range(2)]
            ss8 = [K.sb(es1, "ss8_%d" % i, [128, 8], F32) for i in range(3)]
            rs8 = [K.sb(es1, "rs8_%d" % i, [128, 8], F32) for i in range(3)]
            xTp = psb[7]
            gps = psb[0:6]
            gcount = 0
            xTv = xTp.t[:].bitcast(BF16)

            def load_x(i):
                xbi = xb[i % 3]
                K.dma("sp", xbi.t[:], x_d[128 * i:128 * i + 128, :], xbi.dsem, w=[xbi])

            def xchain_a(i):
                xbi, xsi, ssi = xb[i % 3], xs[i % 2], ssx[i % 2]
                K.op("act", lambda e: e.activation(out=xsi.t[:], in_=xbi.t[:], func=AF.Square,
                                                   accum_out=ssi.t[:, 0:1]), r=[xbi], w=[xsi, ssi])
                K.op("pool", lambda e: e.tensor_scalar(out=ssi.t[:, 1:2], in0=ssi.t[:, 0:1], scalar1=1.0 / D,
                                                       scalar2=EPS, op0=ALU.mult, op1=ALU.add), r=[ssi], w=[ssi])
                K.op("pool", lambda e: e.tensor_tensor(out=ssi.t[:, 1:2], in0=ssi.t[:, 1:2], in1=mhalf.t[:, 0:1],
                                                       op=ALU.pow), r=[ssi, mhalf], w=[ssi])

            def xchain_b(i):
                xbi, xsi, ssi = xb[i % 3], xs[i % 2], ssx[i % 2]
                K.op("dve", lambda e: e.tensor_scalar(out=xsi.t[:], in0=xbi.t[:], scalar1=ssi.t[:, 1:2],
                                                      scalar2=None, op0=ALU.mult), r=[xbi, ssi], w=[xsi])

            def xtrans(i):
                xsi, xTi = xs[i % 2], xT[i % 2]
                K.mm_group([(lambda e, kc=kc: e.transpose(out=xTv[:, kc * 128:(kc + 1) * 128],
                                                          in_=xsi.t[:, kc * 128:(kc + 1) * 128],
                                                          identity=identb.t[:])) for kc in range(8)],
                           r=[xsi, identb], w=[xTp])
                K.op("dve", lambda e: e.tensor_copy(out=xTi.t[:], in_=xTv), r=[xTp], w=[xTi])

            load_x(0)
            if NTILES_DBG > 1:
                load_x(1)
            xchain_a(0)
            xchain_b(0)
            xtrans(0)
            pending = []

            def flush_pending():
                while pending:
                    pb_, r8_, dst_, g_ = pending.pop(0)
                    K.op("dve", lambda e: e.tensor_tensor(
                        out=dst_.t[:, (g_ - 3) * 512:(g_ - 2) * 512].rearrange("p (h d) -> p h d", d=64),
                        in0=pb_.t[:, :].rearrange("p (h d) -> p h d", d=64),
                        in1=bc(r8_.t[:].unsqueeze(2), [128, 8, 64]), op=ALU.mult), r=[pb_, r8_], w=[dst_])

            for i in range(NTILES_DBG):
                xTi = xT[i % 2]
                if i + 2 < NTILES_DBG:
                    load_x(i + 2)
                if i + 1 < NTILES_DBG:
                    xchain_a(i + 1)
                o = i % 2
                for g in range(13):
                    pb = gps[gcount % 6]
                    gcount += 1
                    K.mm_group([(lambda e, kc=kc: e.matmul(pb.t[:, :], lhsT=xTi.t[:, kc * 128:(kc + 1) * 128],
                                                           rhs=w_bf.t[:, kc, g * 512:(g + 1) * 512],
                                                           start=(kc == 0), stop=(kc == 7))) for kc in range(8)],
                               r=[xTi, w_bf], w=[pb])
                    if g < 2:
                        dst = pq_sb[o]
                        K.op("act", lambda e: e.activation(out=dst.t[:, g * 512:(g + 1) * 512], in_=pb.t[:, :],
                                                           func=AF.Copy), r=[pb], w=[dst])
                    elif g == 2 or g == 12:
                        dst = gf_sb[o] if g == 2 else ga_sb[o]
                        tb = th[0 if g == 2 else 1]
                        K.op("act", lambda e: e.activation(out=tb.t[:], in_=pb.t[:, :], func=AF.Tanh, scale=0.5),
                             r=[pb], w=[tb])
                        flush_pending()
                        K.op("dve", lambda e: e.scalar_tensor_tensor(out=dst.t[:], in0=tb.t[:], scalar=1.0,
                                                                     in1=pb.t[:, :], op0=ALU.add, op1=ALU.mult),
                             r=[tb, pb], w=[dst])
                    elif g < 9:
                        k3 = gcount % 3
                        sqb, s8, r8 = sq[k3], ss8[k3], rs8[k3]
                        dst = qk_sb[o]
                        K.op("act", lambda e: e.activation(out=sqb.t[:], in_=pb.t[:, :], func=AF.Square),
                             r=[pb], w=[sqb])
                        K.op("dve", lambda e: e.tensor_reduce(out=s8.t[:], in_=sqb.t[:].rearrange(
                            "p (h d) -> p h d", d=64), axis=AX.X, op=ALU.add), r=[sqb], w=[s8])
                        flush_pending()
                        K.op("pool", lambda e: e.tensor_scalar(out=r8.t[:], in0=s8.t[:], scalar1=1.0 / 64,
                                                               scalar2=EPS, op0=ALU.mult, op1=ALU.add),
                             r=[s8], w=[r8])
                        K.op("pool", lambda e: e.tensor_tensor(out=r8.t[:], in0=r8.t[:], in1=mhalf.t[:, 0:8],
                                                               op=ALU.pow), r=[r8, mhalf], w=[r8])
                        pending.append((pb, r8, dst, g))
                    else:
                        flush_pending()
                        dst = v_sb[o]
                        K.op("act", lambda e: e.activation(out=dst.t[:, (g - 9) * 512:(g - 8) * 512],
                                                           in_=pb.t[:, :], func=AF.Copy), r=[pb], w=[dst])
                    if g == 3 and i + 1 < NTILES_DBG:
                        xchain_b(i + 1)
                    if g == 6 and i + 1 < NTILES_DBG:
                        xtrans(i + 1)
                rows = slice(128 * i, 128 * i + 128)
                K.dma("sp", PQ_d[rows, :], pq_sb[o].t[:], pq_sb[o].dsem, r=[pq_sb[o]])
                K.dma("sp", GF_d[rows, :], gf_sb[o].t[:], gf_sb[o].dsem, r=[gf_sb[o]])
                K.dma("sp", GA_d[rows, :], ga_sb[o].t[:], ga_sb[o].dsem, r=[ga_sb[o]])
                K.dma("sp", QK_d[rows, :], qk_sb[o].t[:], qk_sb[o].dsem, r=[qk_sb[o]])
                K.dma("sp", V_d[rows, :], v_sb[o].t[:], v_sb[o].dsem, r=[v_sb[o]])
            K.drain()


    fT = K.sb(es0, "fT", [128, 4, S], BF16)
    if STOP >= 2 and not SKIP01:
        with ExitStack() as es:
            tabs = [K.sb(es, "tab%d" % t, [128, 8192], BF16, dma=True) for t in range(3)]
            zin = [K.sb(es, "zin%d" % i, [128, 8, 1024], BF16, dma=True) for i in range(2)]
            ysb = [K.sb(es, "ysb%d" % i, [128, 2, 512], BF16, dma=True) for i in range(4)]
            for t, td in enumerate((tabA_d, tabB_d, tabC_d)):
                K.dma("sp", tabs[t].t[:], td[:, :], tabs[t].dsem, w=[tabs[t]])
            PQv = PQ_d.rearrange("(s1 s2) c -> s1 s2 c", s2=64)
            def load_z(s2c):
                z_ = zin[s2c % 2]
                K.dma("sp", z_.t[:], PQv[:, s2c * 8:(s2c + 1) * 8, :], z_.dsem, w=[z_])

            load_z(0)
            for s2c in range(8):
                z = zin[s2c % 2]
                if s2c + 1 < 8:
                    load_z(s2c + 1)
                for s2l in range(8):
                    s2 = s2c * 8 + s2l
                    cs = slice(s2 * 128, (s2 + 1) * 128)
                    A, B, C = tabs[0].t[:, cs], tabs[1].t[:, cs], tabs[2].t[:, cs]
                    P, Q = z.t[:, s2l, 0:512], z.t[:, s2l, 512:1024]
                    pr, pi = psb[(2 * s2) % 8], psb[(2 * s2 + 1) % 8]
                    K.mm_group([lambda e: e.matmul(pr.t[:, :], lhsT=A, rhs=P, start=True, stop=False),
                                lambda e: e.matmul(pr.t[:, :], lhsT=C, rhs=Q, start=False, stop=True)],
                               r=[tabs[0], tabs[2], z], w=[pr])
                    K.mm_group([lambda e: e.matmul(pi.t[:, :], lhsT=B, rhs=P, start=True, stop=False),
                                lambda e: e.matmul(pi.t[:, :], lhsT=A, rhs=Q, start=False, stop=True)],
                               r=[tabs[0], tabs[1], z], w=[pi])
                    yb = ysb[s2 % 4]
                    K.op("act", lambda e: e.activation(out=yb.t[:, 0, :], in_=pr.t[:, :], func=AF.Copy), r=[pr], w=[yb])
                    K.op("dve", lambda e: e.tensor_copy(out=yb.t[:, 1, :], in_=pi.t[:, :]), r=[pi], w=[yb])
                    K.dma("sp", Y_d[s2], yb.t[:], yb.dsem, r=[yb])
            K.drain()
        with ExitStack() as es:
            cs2 = K.sb(es, "cs2", [128, 64], BF16, dma=True)
            K.dma("sp", cs2.t[:], cs2_d[:, :], cs2.dsem, w=[cs2])
            yin_t = [es.enter_context(nc.sbuf_tensor("sb_yin%d" % i, [128, 16, 512], BF16)) for i in range(2)]
            yin = [[Buf(yin_t[i], K.new_dsem()) for _ in range(2)] for i in range(2)]
            for i in range(2):
                K.allbufs.extend(yin[i])
            cnt = 0
            def load_y(k1c):
                yt_ = yin_t[k1c % 2]
                for ri, yb in enumerate(yin[k1c % 2]):
                    K.dma("sp" if ri == 0 else "act", yt_[64 * ri:64 * ri + 64, :, :],
                          Y_d[:, k1c * 16:(k1c + 1) * 16, ri, :], yb.dsem, w=[yb])

            load_y(0)
            for k1c in range(8):
                yt = yin_t[k1c % 2]
                yl, yh = yin[k1c % 2]
                if k1c + 1 < 8:
                    load_y(k1c + 1)
                for chc in range(4):
                    for hh in range(2):
                        pb = psb[cnt % 8]
                        K.mm_group([(lambda e, k=k: e.matmul(pb.t[:, k * 64:(k + 1) * 64],
                                                             lhsT=yt[:, hh * 8 + k, chc * 128:(chc + 1) * 128],
                                                             rhs=cs2.t[:, :], start=True, stop=True)) for k in range(8)],
                                   r=[yl, yh, cs2], w=[pb])
                        k10 = k1c * 16 + hh * 8
                        dst = fT.t[:, chc, :].rearrange("p (k2 k1) -> p k2 k1", k1=128)[:, :, k10:k10 + 8]
                        src = pb.t[:, :].rearrange("p (k1 k2) -> p k2 k1", k2=64)
                        if cnt % 2 == 0:
                            K.op("act", lambda e: e.activation(out=dst, in_=src, func=AF.Copy), r=[pb])
                        else:
                            K.op("dve", lambda e: e.tensor_copy(out=dst, in_=src), r=[pb])
                        cnt += 1
            K.drain()

    if STOP >= 3:
        with ExitStack() as es:
            em = K.sb(es, "em", [128, 12, 512], BF16, dma=True)
            gqT = K.sb(es, "gqT", [128, 12], F32, dma=True)
            gkT = K.sb(es, "gkT", [128, 12], F32, dma=True)
            GT = K.sb(es, "GT", [128, 12], F32)
            K.dma("sp", em.t[:], em_d.rearrange("p (a b) -> p a b", b=512), em.dsem, w=[em])
            K.dma("sp", gqT.t[:], qn_d.rearrange("(pr h2) d -> (h2 d) pr", h2=2), gqT.dsem, w=[gqT],
                  allow_slow_non_contiguous=True)
            K.dma("sp", gkT.t[:], kn_d.rearrange("(pr h2) d -> (h2 d) pr", h2=2), gkT.dsem, w=[gkT],
                  allow_slow_non_contiguous=True)
            K.op("dve", lambda e: e.scalar_tensor_tensor(out=GT.t[:], in0=gqT.t[:], scalar=0.125, in1=gkT.t[:],
                                                         op0=ALU.mult, op1=ALU.mult), r=[gqT, gkT], w=[GT])
            RQ, RK, RV = 4, 6, 12
            qsb = [K.sb(es, "qsb%d" % i, [128, 512], BF16, dma=True) for i in range(RQ)]
            ksb = [K.sb(es, "ksb%d" % i, [128, 512], BF16, dma=True) for i in range(RK)]
            vsb = [K.sb(es, "vsb%d" % i, [128, 8, 128], BF16, dma=True) for i in range(RV)]
            for vb_ in vsb:
                K.op("pool", lambda e: e.memset(vb_.t[:], 0.0), w=[vb_])
            qTb = [K.sb(es, "qT%d" % i, [128, 4, 128], BF16) for i in range(2)]
            kTb = [K.sb(es, "kT%d" % i, [128, 4, 128], BF16) for i in range(RK)]
            pex = [K.sb(es, "pex%d" % i, [128, 512], BF16) for i in range(3)]
            pmk = [K.sb(es, "pmk%d" % i, [128, 512], BF16) for i in range(8)]
            osb = [K.sb(es, "osb%d" % i, [128, 520], F32, dma=True) for i in range(2)]
            for i_, ob_ in enumerate(osb):
                ob_.swsem = K.new_swsem("sw_osb%d" % i_)
            tq_ps, tk_ps = psb[0], psb[1]
            S_ps = [psb[2], psb[3]]
            O_ps = [[psb[4], psb[5]], [psb[6], psb[7]]]
            blocks, ktiles = [], []
            for c, dil in enumerate(DILS):
                nqb = S // dil // 128
                for r in range(dil):
                    base = len(ktiles)
                    for kt in range(nqb + 1):
                        ktiles.append((c, r, kt))
                    for qb in range(nqb):
                        blocks.append((c, r, qb, base + qb, base + qb + 1))
            NB = len(blocks)
            views = []
            for c, dil in enumerate(DILS):
                views.append((QK_d.rearrange("(i d) c -> d i c", d=dil), V_d.rearrange("(i d) c -> d i c", d=dil),
                              O_d[c].rearrange("(i d) e -> d i e", d=dil)))
            st = {"kl": 0, "kt": 0, "sc": 0}

            def load_q(b):
                c, r, qb = blocks[b][0:3]
                qb_ = qsb[b % RQ]
                K.dma("sp", qb_.t[:], views[c][0][r, 128 * qb:128 * qb + 128, c * 512:(c + 1) * 512], qb_.dsem,
                      w=[qb_])

            def load_kv(ki):
                c, r, kt = ktiles[ki]
                dil = DILS[c]
                L = S // dil
                nqb = L // 128
                kb, vb = ksb[ki % RK], vsb[ki % RV]
                a, b_ = 128 * kt - 64, 128 * kt + 64
                p0, p1 = 0, 128
                if kt == 0:
                    a, p0 = 0, 64
                if kt == nqb:
                    b_, p1 = L, 64
                if kt == 0 or kt == nqb:
                    K.op("pool", lambda e: e.memset(kb.t[:], 0.0), w=[kb])
                    K.op("pool", lambda e: e.memset(vb.t[:], 0.0), w=[vb])
                K.op("pool", lambda e: e.memset(vb.t[p0:p1, :, 64:65], 1.0), w=[vb])
                K.dma("sp", kb.t[p0:p1, :], views[c][0][r, a:b_, 1536 + c * 512:1536 + (c + 1) * 512], kb.dsem,
                      w=[kb])
                K.dma("sp", vb.t[p0:p1, :, 0:64], views[c][1][r, a:b_, c * 512:(c + 1) * 512].rearrange(
                    "n (h d) -> n h d", d=64), vb.dsem, w=[vb])

            def ensure_loaded(upto):
                while st["kl"] <= upto:
                    load_kv(st["kl"])
                    st["kl"] += 1

            def tr_q(b):
                qb_, qt = qsb[b % RQ], qTb[b % 2]
                tv = tq_ps.t[:].bitcast(BF16)
                K.mm_group([(lambda e, pr=pr: e.transpose(out=tv[:, pr * 128:(pr + 1) * 128],
                                                          in_=qb_.t[:, pr * 128:(pr + 1) * 128],
                                                          identity=identb.t[:])) for pr in range(4)],
                           r=[qb_, identb], w=[tq_ps])
                K.op("dve", lambda e: e.tensor_copy(out=qt.t[:].rearrange("p a b -> p (a b)"), in_=tv[:, 0:512]),
                     r=[tq_ps], w=[qt])

            def tr_k(ki):
                c = ktiles[ki][0]
                kb, kt_ = ksb[ki % RK], kTb[ki % RK]
                tv = tk_ps.t[:].bitcast(BF16)
                K.mm_group([(lambda e, pr=pr: e.transpose(out=tv[:, pr * 128:(pr + 1) * 128],
                                                          in_=kb.t[:, pr * 128:(pr + 1) * 128],
                                                          identity=identb.t[:])) for pr in range(4)],
                           r=[kb, identb], w=[tk_ps])
                K.op("dve", lambda e: e.tensor_tensor(
                    out=kt_.t[:], in0=tv[:, 0:512].rearrange("p (a b) -> p a b", b=128),
                    in1=bc(GT.t[:, 4 * c:4 * c + 4].unsqueeze(2), [128, 4, 128]), op=ALU.mult),
                    r=[tk_ps, GT], w=[kt_])

            def ensure_tr(upto):
                while st["kt"] <= upto:
                    tr_k(st["kt"])
                    st["kt"] += 1

            def s_block(b, prs):
                c, r, qb, kl, kh = blocks[b]
                qt, klo, khi = qTb[b % 2], kTb[kl % RK], kTb[kh % RK]
                half = prs[0] // 2
                fns = []
                for pp, pr in enumerate(prs):
                    for kk, off in ((klo, 0), (khi, 128)):
                        for h2 in range(2):
                            rs = slice(64 * h2, 64 * h2 + 64)
                            fns.append(lambda e, h2=h2, kk=kk, off=off, pp=pp, pr=pr, rs=rs: e.matmul(
                                S_ps[h2].t[:, pp * 256 + off:pp * 256 + off + 128], lhsT=kk.t[rs, pr, :],
                                rhs=qt.t[rs, pr, :], start=True, stop=True))
                K.mm_group(fns, r=[klo, khi, qt], w=[S_ps[0], S_ps[1]])
                for h2 in range(2):
                    px, pm = pex[st["sc"] % 3], pmk[(b % 2) * 4 + half * 2 + h2]
                    st["sc"] += 1
                    Sp = S_ps[h2]
                    K.op("act", lambda e: e.activation(out=px.t[:], in_=Sp.t[:, :], func=AF.Exp), r=[Sp], w=[px])
                    K.op("dve", lambda e: e.tensor_tensor(out=pm.t[:], in0=px.t[:],
                                                          in1=em.t[:, (c * 2 + half) * 2 + h2, :],
                                                          op=ALU.mult), r=[px, em], w=[pm])

            def pv_block(b, prs):
                c, r, qb, kl, kh = blocks[b]
                vlo, vhi = vsb[kl % RV], vsb[kh % RV]
                Ob = O_ps[b % 2]
                fns, pms = [], []
                Obk = Ob[prs[0] // 2]
                for pr in prs:
                    for h2 in range(2):
                        pm = pmk[(b % 2) * 4 + (pr // 2) * 2 + h2]
                        pms.append(pm)
                        pp = pr % 2
                        h = 2 * pr + h2
                        oc = slice((h % 4) * 128, (h % 4) * 128 + 128)
                        fns.append(lambda e, pm=pm, pp=pp, h=h, oc=oc: e.matmul(
                            Obk.t[:, oc], lhsT=pm.t[:, pp * 256:pp * 256 + 128], rhs=vlo.t[:, h, :],
                            start=True, stop=False))
                        fns.append(lambda e, pm=pm, pp=pp, h=h, oc=oc: e.matmul(
                            Obk.t[:, oc], lhsT=pm.t[:, pp * 256 + 128:pp * 256 + 256], rhs=vhi.t[:, h, :],
                            start=False, stop=True))
                K.mm_group(fns, r=pms + [vlo, vhi], w=[Obk])

            if P3MODE > 0:
                for t_ in range(3):
                    load_q(t_)
                ensure_loaded(blocks[2][4])
                tr_q(0)
                ensure_tr(blocks[0][4])
                s_block(0, (0, 1))
                s_block(0, (2, 3))
                tr_q(1)
                ensure_tr(blocks[1][4])
                for b in range(NB):
                    if b + 3 < NB:
                        load_q(b + 3)
                        ensure_loaded(blocks[b + 3][4])
                    if b + 2 < NB:
                        tr_q(b + 2)
                        ensure_tr(blocks[b + 2][4])
                    c, r, qb = blocks[b][0:3]
                    ob, Ob = osb[b % 2], O_ps[b % 2]
                    for half in range(2):
                        if b + 1 < NB:
                            s_block(b + 1, (2 * half, 2 * half + 1))
                        pv_block(b, (2 * half, 2 * half + 1))
                    K.op("act", lambda e: e.activation(
                        out=ob.t[:, 0:260].rearrange("p (h e) -> p h e", e=65),
                        in_=Ob[0].t[:, :].rearrange("p (h e) -> p h e", e=128)[:, :, 0:65], func=AF.Copy),
                        r=[Ob[0]], w=[ob])
                    K.op("dve", lambda e: e.tensor_copy(
                        out=ob.t[:, 260:520].rearrange("p (h e) -> p h e", e=65),
                        in_=Ob[1].t[:, :].rearrange("p (h e) -> p h e", e=128)[:, :, 0:65]),
                        r=[Ob[1]], w=[ob])
                    if c == 0:
                        K.dma("sp", views[c][2][r, 128 * qb:128 * qb + 128, :], ob.t[:], ob.dsem, r=[ob])
                    else:
                        if blocks[b - 1][0] != c:
                            for ob_ in osb:
                                for sm in (ob_.dsem, ob_.swsem):
                                    if sm.n > 0 and K.waited["pool"].get(sm, 0) < sm.n:
                                        nc.gpsimd.wait_ge(sm.h, sm.n)
                                        K.waited["pool"][sm] = sm.n
                        K.dma("pool", views[c][2][r, 128 * qb:128 * qb + 128, :], ob.t[:], ob.swsem, r=[ob],
                              accum_op=ALU.add)
            K.drain()


    if STOP >= 4:
        with ExitStack() as es:
            NS = 4
            o_in = [[K.sb(es, "oin%d_%d" % (c, i), [128, 520], F32, dma=True) for c in range(1)] for i in range(NS)]
            ga_in = [K.sb(es, "ga_in%d" % i, [128, 512], BF16, dma=True) for i in range(NS)]
            gf_in = [K.sb(es, "gf_in%d" % i, [128, 512], BF16, dma=True) for i in range(NS)]
            xr = [K.sb(es, "xr%d" % i, [128, D], F32, dma=True) for i in range(NS)]
            osum = [K.sb(es, "osum%d" % i, [128, 520], F32) for i in range(3)]
            rden = [K.sb(es, "rden%d" % i, [128, 8], F32) for i in range(3)]
            ya32 = [K.sb(es, "ya32_%d" % i, [128, 512], F32) for i in range(3)]
            ya_bf = [K.sb(es, "ya_bf%d" % i, [128, 512], BF16) for i in range(3)]
            yT = [K.sb(es, "yT%d" % i, [128, 8, 128], BF16) for i in range(2)]
            out_sb = [K.sb(es, "out_sb%d" % i, [128, D], F32) for i in range(2)]
            for i_, ob_ in enumerate(out_sb):
                ob_.dsem = K.new_swsem("sw_out%d" % i_)
            tpa, tpg = [psb[0], psb[6]], [psb[1], psb[7]]
            outp = [[psb[2], psb[3]], [psb[4], psb[5]]]

            def load4(i):
                o = i % NS
                rows = slice(128 * i, 128 * i + 128)
                K.dma("sp", o_in[o][0].t[:], O_acc[rows, :], o_in[o][0].dsem, w=[o_in[o][0]])
                K.dma("sp", ga_in[o].t[:], GA_d[rows, :], ga_in[o].dsem, w=[ga_in[o]])
                K.dma("sp", gf_in[o].t[:], GF_d[rows, :], gf_in[o].dsem, w=[gf_in[o]])
                K.dma("act", xr[o].t[:], x_d[rows, :], xr[o].dsem, w=[xr[o]])

            def stage_a1(i):
                o3, s3 = i % 3, i % NS
                os_, rd, y32, ybf = o_in[s3][0], rden[o3], ya32[o3], ya_bf[o3]
                osv = os_.t[:].rearrange("p (h e) -> p h e", e=65)
                K.op("dve", lambda e: e.reciprocal(out=rd.t[:], in_=osv[:, :, 64]), r=[os_], w=[rd])
                K.op("dve", lambda e: e.tensor_tensor(out=y32.t[:].rearrange("p (h d) -> p h d", d=64),
                                                      in0=osv[:, :, 0:64],
                                                      in1=bc(rd.t[:].unsqueeze(2), [128, 8, 64]),
                                                      op=ALU.mult), r=[os_, rd], w=[y32])
                K.op("pool", lambda e: e.tensor_tensor(out=ybf.t[:], in0=y32.t[:], in1=ga_in[s3].t[:], op=ALU.mult),
                     r=[y32, ga_in[s3]], w=[ybf])

            def stage_a2(i):
                o, o3, s3 = i % 2, i % 3, i % NS
                ybf, yt = ya_bf[o3], yT[o]
                tav = tpa[o].t[:].bitcast(BF16)
                tgv = tpg[o].t[:].bitcast(BF16)
                K.mm_group([(lambda e, j=j: e.transpose(out=tgv[:, j * 128:(j + 1) * 128],
                                                        in_=gf_in[s3].t[:, j * 128:(j + 1) * 128],
                                                        identity=identb.t[:]))
                            for j in range(4)], r=[gf_in[s3], identb], w=[tpg[o]])
                K.op("dve", lambda e: e.tensor_tensor(out=yt.t[:, 0:4, :],
                                                      in0=tgv[:, 0:512].rearrange("p (a b) -> p a b", b=128),
                                                      in1=fT.t[:, :, 128 * i:128 * i + 128], op=ALU.mult),
                     r=[tpg[o]], w=[yt])
                K.mm_group([(lambda e, j=j: e.transpose(out=tav[:, j * 128:(j + 1) * 128],
                                                        in_=ybf.t[:, j * 128:(j + 1) * 128], identity=identb.t[:]))
                            for j in range(4)], r=[ybf, identb], w=[tpa[o]])
                K.op("act", lambda e: e.activation(out=yt.t[:, 4:8, :].rearrange("p a b -> p (a b)"),
                                                   in_=tav[:, 0:512], func=AF.Copy), r=[tpa[o]], w=[yt])

            def stage_b_mm(i):
                o = i % 2
                yt = yT[o]
                for half in range(2):
                    pb = outp[o][half]
                    K.mm_group([(lambda e, cc=cc: e.matmul(pb.t[:, :], lhsT=yt.t[:, cc, :],
                                                           rhs=wout_bf.t[:, cc, half * 512:(half + 1) * 512],
                                                           start=(cc == 0), stop=(cc == 7))) for cc in range(8)],
                               r=[yt, wout_bf], w=[pb])

            def stage_b_res(i):
                o, s3 = i % 2, i % NS
                rows = slice(128 * i, 128 * i + 128)
                for half in range(2):
                    pb = outp[o][half]
                    K.op("dve", lambda e: e.tensor_tensor(out=out_sb[o].t[:, half * 512:(half + 1) * 512],
                                                          in0=pb.t[:, :],
                                                          in1=xr[s3].t[:, half * 512:(half + 1) * 512],
                                                          op=ALU.add), r=[pb, xr[s3]], w=[out_sb[o]])
                K.dma("pool", y_d[rows, :], out_sb[o].t[:], out_sb[o].dsem, r=[out_sb[o]])

            for t_ in range(3):
                load4(t_)
            stage_a1(0)
            stage_a1(1)
            stage_a2(0)
            for i in range(NT):
                if i + 3 < NT:
                    load4(i + 3)
                if i + 2 < NT:
                    stage_a1(i + 2)
                if i + 1 < NT:
                    stage_a2(i + 1)
                stage_b_mm(i)
                stage_b_res(i)
    K.drain()
    K.root.close()
    return nc


_CONSTS = None


def kernel(x, norm_w, w_in, q_norm_w, k_norm_w, w_fourier, w_out):
    global _CONSTS
    if _CONSTS is None:
        _CONSTS = _consts()
    nc = build()
    x = np.ascontiguousarray(np.asarray(x, dtype=np.float32))
    shared = {
        "norm_w": np.asarray(norm_w, np.float32), "w_in": np.asarray(w_in, np.float32),
        "q_norm_w": np.asarray(q_norm_w, np.float32), "k_norm_w": np.asarray(k_norm_w, np.float32),
        "w_fourier": np.asarray(w_fourier, np.float32), "w_out": np.asarray(w_out, np.float32),
    }
    shared.update(_CONSTS)
    in_maps = [dict(shared, x=x[b]) for b in range(NCORES)]
    if os.environ.get("MK_TRACE") == "1":
        res = run_bass_kernel_spmd(nc, in_maps, core_ids=list(range(NCORES)), trace=True)
        print("EXEC_TIME_NS", res.exec_time_ns)
    else:
        res = run_bass_kernel_spmd(nc, in_maps, core_ids=list(range(NCORES)))
    if DEBUG:
        kernel.last = res
    ys = [np.asarray(r["y"]) for r in res.results]
    ys += [np.zeros_like(ys[0])] * (8 - len(ys))
    return np.stack(ys, axis=0).astype(np.float32)
```

```python
import os
from contextlib import ExitStack

import ml_dtypes
import numpy as np

import concourse.bass as bass
import concourse.mybir as mybir
from concourse.alu_op_type import AluOpType as ALU
from concourse.bass_utils import run_bass_kernel_spmd

F32 = mybir.dt.float32
BF16 = mybir.dt.bfloat16
AF = mybir.ActivationFunctionType
AX = mybir.AxisListType

S = 8192
D = 1024
NT = S // 128
INW = 6144
NCOL = 6656
DILS = (1, 4, 16)
EPS = 1e-6
DEBUG = bool(int(os.environ.get("MK_DEBUG", "0")))
NTILES_DBG = int(os.environ.get("MK_NT", str(NT)))
STOP = int(os.environ.get("MK_STOP", "9"))
NCORES = int(os.environ.get("MK_CORES", "8"))
P3MODE = int(os.environ.get("MK_P3", "9"))
P3X = int(os.environ.get("MK_P3X", "9"))
SKIP01 = bool(int(os.environ.get("MK_SKIP01", "0")))


def _consts():
    bf = ml_dtypes.bfloat16
    c = {}
    c["identb"] = np.eye(128, dtype=np.float32).astype(bf)
    kap = 1.0 / np.sqrt(S * 64.0)
    cc = np.arange(64)
    ang = 2 * np.pi * np.outer(cc, cc) / 64.0
    cos2 = np.concatenate([np.cos(ang), np.cos(ang)], axis=1) * kap
    sin2 = np.concatenate([np.sin(ang), np.sin(ang)], axis=1) * kap
    c["cs64"] = np.concatenate([cos2, sin2], axis=1).astype(np.float32)
    s1 = np.arange(128)[:, None, None].astype(np.float64)
    s2 = np.arange(64)[None, :, None].astype(np.float64)
    k1 = np.arange(128)[None, None, :].astype(np.float64)
    ph = 2 * np.pi * (s1 * k1 / 128.0 + s2 * k1 / 8192.0)
    A = np.cos(ph)
    B = np.sin(ph)
    c["tabA"] = A.reshape(128, 8192).astype(np.float32).astype(bf)
    c["tabB"] = B.reshape(128, 8192).astype(np.float32).astype(bf)
    c["tabC"] = (-B).reshape(128, 8192).astype(np.float32).astype(bf)
    s2v = np.arange(64)[:, None].astype(np.float64)
    k2v = np.arange(64)[None, :].astype(np.float64)
    ph2 = 2 * np.pi * s2v * k2v / 64.0
    c["cs2"] = np.concatenate([np.cos(ph2), -np.sin(ph2)], axis=0).astype(np.float32).astype(bf)
    j = np.arange(128)[:, None].astype(np.float64)
    i = np.arange(128)[None, :].astype(np.float64)
    em = np.zeros((128, 3, 2, 2, 2, 2, 128), dtype=np.float64)
    for ci, dil in enumerate(DILS):
        for h in range(8):
            slope = 2.0 ** (-(h + 1))
            lo = np.where(j >= i, np.exp(-slope * dil * np.abs(j - 64 - i)), 0.0)
            hi = np.where(j <= i, np.exp(-slope * dil * np.abs(64 + j - i)), 0.0)
            pr, h2 = h // 2, h % 2
            em[:, ci, pr // 2, h2, pr % 2, 0, :] = lo
            em[:, ci, pr // 2, h2, pr % 2, 1, :] = hi
    c["emask"] = em.reshape(128, 12 * 512).astype(np.float32).astype(bf)
    return c


class Sem:
    def __init__(self, K, name):
        self.h = K.root.enter_context(K.nc.semaphore(name))
        self.n = 0


class Buf:
    def __init__(self, t, dsem=None):
        self.t = t
        self.w = None
        self.r = []
        self.dsem = dsem


class KB:
    def __init__(self, nc):
        self.nc = nc
        self.root = ExitStack()
        self.E = {"pe": nc.tensor, "act": nc.scalar, "dve": nc.vector, "pool": nc.gpsimd, "sp": nc.sync}
        self.esem = {e: Sem(self, "e_" + e) for e in ("pe", "act", "dve", "pool")}
        self.waited = {e: {} for e in self.E}
        self.dpool = [Sem(self, "d%d" % i) for i in range(72)]
        self.dnext = 0
        self.phase_sem = Sem(self, "phase")
        self.allbufs = []
        self.swsems = []

    def sb(self, es, name, shape, dt, dma=False):
        t = es.enter_context(self.nc.sbuf_tensor("sb_" + name, list(shape), dt))
        b = Buf(t, self.new_dsem() if dma else None)
        self.allbufs.append(b)
        return b

    def ps(self, es, name):
        t = es.enter_context(self.nc.psum_tensor("ps_" + name, [128, 512], F32))
        b = Buf(t)
        self.allbufs.append(b)
        return b

    def new_swsem(self, name):
        sm = Sem(self, name)
        self.swsems.append(sm)
        return sm

    def new_dsem(self):
        s = self.dpool[self.dnext]
        self.dnext += 1
        return s

    def _waits(self, eng, r, w):
        need = {}

        def add(ev):
            if ev is None:
                return
            s, v = ev
            if need.get(s, 0) < v:
                need[s] = v

        for b in r:
            add(b.w)
        for b in w:
            add(b.w)
            for ev in b.r:
                add(ev)
        for s, v in need.items():
            if self.waited[eng].get(s, 0) < v:
                self.E[eng].wait_ge(s.h, v)
                self.waited[eng][s] = v

    def _record(self, ev, r, w):
        for b in r:
            b.r.append(ev)
        for b in w:
            b.w = ev
            b.r = []

    def op(self, eng, fn, r=(), w=()):
        self._waits(eng, r, w)
        ins = fn(self.E[eng])
        s = self.esem[eng]
        ins.then_inc(s.h, 1)
        s.n += 1
        self._record((s, s.n), r, w)
        return ins

    def mm_group(self, fns, r=(), w=()):
        self._waits("pe", r, w)
        ins = None
        for fn in fns:
            ins = fn(self.nc.tensor)
        s = self.esem["pe"]
        ins.then_inc(s.h, 1)
        s.n += 1
        self._record((s, s.n), r, w)

    def dma(self, q, out, in_, sem, r=(), w=(), **kw):
        self._waits(q, r, w)
        ins = self.E[q].dma_start(out=out, in_=in_, **kw)
        ins.then_inc(sem.h, 16)
        sem.n += 16
        self._record((sem, sem.n), r, w)

    def drain(self):
        sp = self.nc.sync
        for s in list(self.esem.values()) + self.dpool[: self.dnext] + self.swsems:
            if s.n > 0 and self.waited["sp"].get(s, 0) < s.n:
                sp.wait_ge(s.h, s.n)
                self.waited["sp"][s] = s.n
        ps = self.phase_sem
        sp.sem_inc(ps.h, 1)
        ps.n += 1
        for e in ("pe", "act", "dve", "pool"):
            self.E[e].wait_ge(ps.h, ps.n)
            for s in list(self.esem.values()) + self.dpool[: self.dnext]:
                self.waited[e][s] = s.n
        for b in self.allbufs:
            b.w = None
            b.r = []
        self.allbufs = []
        self.dnext = 0


def bc(ap, shape):
    return ap.to_broadcast(list(shape))


def build():
    nc = bass.Bass("TRN2", target_bir_lowering=False)
    K = KB(nc)
    dt_in = lambda n, s, d=F32: nc.dram_tensor(n, list(s), d, kind="ExternalInput").ap()
    x_d = dt_in("x", [S, D])
    nw_d = dt_in("norm_w", [D])
    win_d = dt_in("w_in", [D, INW])
    qn_d = dt_in("q_norm_w", [24, 64])
    kn_d = dt_in("k_norm_w", [24, 64])
    wf_d = dt_in("w_fourier", [8, 64, 64])
    wout_d = dt_in("w_out", [D, D])
    identb_d = dt_in("identb", [128, 128], BF16)
    cs64_d = dt_in("cs64", [64, 256])
    tabA_d = dt_in("tabA", [128, 8192], BF16)
    tabB_d = dt_in("tabB", [128, 8192], BF16)
    tabC_d = dt_in("tabC", [128, 8192], BF16)
    cs2_d = dt_in("cs2", [128, 64], BF16)
    em_d = dt_in("emask", [128, 12 * 512], BF16)
    y_d = nc.dram_tensor("y", [S, D], F32, kind="ExternalOutput").ap()
    skind = "ExternalOutput" if DEBUG else "Internal"
    scr = lambda n, s, d: nc.dram_tensor(n, list(s), d, kind=skind).ap()
    PQ_d = scr("PQ_s", [S, 1024], BF16)
    GF_d = scr("GF_s", [S, 512], BF16)
    GA_d = scr("GA_s", [S, 512], BF16)
    QK_d = scr("QK_s", [S, 3072], BF16)
    V_d = scr("V_s", [S, 1536], BF16)
    Y_d = scr("Y_s", [64, 128, 2, 512], BF16)
    O_acc = scr("O_s", [S, 520], F32)
    O_d = [O_acc, O_acc, O_acc]

    es0 = K.root
    identb = K.sb(es0, "identb", [128, 128], BF16, dma=True)
    mhalf = K.sb(es0, "mhalf", [128, 48], F32)
    psb = [K.ps(es0, "psb%d" % i) for i in range(8)]
    wout_bf = K.sb(es0, "wout_bf", [128, 8, 1024], BF16)
    K.dma("sp", identb.t[:], identb_d[:, :], identb.dsem, w=[identb])
    K.op("pool", lambda e: e.memset(mhalf.t[:], -0.5), w=[mhalf])

    with ExitStack() as es:
      if not SKIP01:
        w_bf = K.sb(es, "w_bf", [128, 8, NCOL], BF16)
        with ExitStack() as es_p0:
            nw = K.sb(es_p0, "nw", [128, 8], F32, dma=True)
            stage = [K.sb(es_p0, "stage%d" % i, [128, INW], F32, dma=True) for i in range(2)]
            wu_bf = [K.sb(es_p0, "wu_bf%d" % i, [128, 512], BF16) for i in range(8)]
            wuT = [K.sb(es_p0, "wuT%d" % i, [128, 4, 128], BF16) for i in range(2)]
            cs64 = K.sb(es_p0, "cs64", [64, 256], F32, dma=True)
            wf_sb = K.sb(es_p0, "wf_sb", [64, 8, 64], F32, dma=True)
            BD = K.sb(es_p0, "BD", [128, 4, 256], BF16)
            K.dma("sp", nw.t[:], nw_d.rearrange("(kc p) -> p kc", p=128), nw.dsem, w=[nw],
                  allow_slow_non_contiguous=True)
            K.dma("sp", cs64.t[:], cs64_d[:, :], cs64.dsem, w=[cs64])
            K.dma("sp", wf_sb.t[:], wf_d.rearrange("g c d -> c g d"), wf_sb.dsem, w=[wf_sb])
            wf2 = wf_sb.t[:].rearrange("c g d -> c (g d)")
            K.mm_group([lambda e: e.matmul(psb[0].t[:, :], lhsT=cs64.t[:, 0:128], rhs=wf2, start=True, stop=True)],
                       r=[cs64, wf_sb], w=[psb[0]])
            K.mm_group([lambda e: e.matmul(psb[1].t[:, :], lhsT=cs64.t[:, 128:256], rhs=wf2, start=True, stop=True)],
                       r=[cs64, wf_sb], w=[psb[1]])
            K.op("pool", lambda e: e.memset(BD.t[:], 0.0), w=[BD])
            for half in range(2):
                pr = slice(64 * half, 64 * half + 64)
                for t in range(2):
                    src = psb[t].t[pr, :].rearrange("p (j two d) -> p j two d", two=2, d=64)[:, :, half, :]
                    dst = BD.t[pr, :, 128 * t + 64 * half:128 * t + 64 * half + 64]
                    K.op("dve", lambda e: e.tensor_copy(out=dst, in_=src), r=[psb[t]], w=[BD])
            def fold_kc(kc):
                pt = psb[2 + kc % 2]
                pv = pt.t[:].bitcast(BF16)
                K.mm_group([(lambda e, j=j: e.transpose(out=pv[:, j * 128:(j + 1) * 128],
                                                        in_=wu_bf[kc].t[:, j * 128:(j + 1) * 128],
                                                        identity=identb.t[:])) for j in range(4)],
                           r=[wu_bf[kc], identb], w=[pt])
                wT = wuT[kc % 2]
                K.op("act", lambda e: e.activation(out=wT.t[:].rearrange("p a b -> p (a b)"), in_=pv[:, 0:512],
                                                   func=AF.Copy), r=[pt], w=[wT])
                pa, pb2 = psb[4 + 2 * (kc % 2)], psb[5 + 2 * (kc % 2)]
                for jj, pb in ((0, pa), (1, pb2)):
                    K.mm_group([(lambda e, j=j: e.matmul(pb.t[:, (j % 2) * 256:(j % 2) * 256 + 256],
                                                         lhsT=wT.t[:, j, :], rhs=BD.t[:, j, :],
                                                         start=True, stop=True))
                                for j in (2 * jj, 2 * jj + 1)], r=[wT, BD], w=[pb])
                    for t in range(2):
                        src = pb.t[:, :].rearrange("p (j two d) -> p j two d", two=2, d=128)[:, :, t, :]
                        dst = w_bf.t[:, kc, 512 * t + 256 * jj:512 * t + 256 * jj + 256].rearrange(
                            "p (j d) -> p j d", d=128)
                        K.op("dve" if t == 0 else "act",
                             (lambda e: e.tensor_copy(out=dst, in_=src)) if t == 0 else
                             (lambda e: e.activation(out=dst, in_=src, func=AF.Copy)),
                             r=[pb], w=[w_bf])

            win_v = win_d.rearrange("(kc p) n -> p kc n", p=128)
            for kc in range(8):
                st = stage[kc % 2]
                K.dma("sp", st.t[:], win_v[:, kc, :], st.dsem, w=[st])
                sc = nw.t[:, kc:kc + 1]
                K.op("dve", lambda e: e.tensor_scalar(out=w_bf.t[:, kc, 1024:4096], in0=st.t[:, 512:3584],
                                                      scalar1=sc, scalar2=None, op0=ALU.mult),
                     r=[st, nw], w=[w_bf])
                K.op("act", lambda e: e.activation(out=w_bf.t[:, kc, 4096:NCOL], in_=st.t[:, 3584:INW],
                                                   func=AF.Copy, scale=sc), r=[st, nw], w=[w_bf])
                K.op("pool", lambda e: e.tensor_scalar(out=wu_bf[kc].t[:], in0=st.t[:, 0:512],
                                                       scalar1=sc, scalar2=None, op0=ALU.mult),
                     r=[st, nw], w=[wu_bf[kc]])
                fold_kc(kc)
            wout_v = wout_d.rearrange("(cc p) n -> p cc n", p=128)
            for h in range(2):
                st = stage[h]
                K.dma("sp", st.t[:, 0:4096].rearrange("p (c n) -> p c n", c=4), wout_v[:, 4 * h:4 * h + 4, :],
                      st.dsem, w=[st])
                K.op("dve", lambda e: e.tensor_scalar(
                    out=wout_bf.t[:, 4 * h:4 * h + 4, :], in0=st.t[:, 0:4096].rearrange("p (c n) -> p c n", c=4),
                    scalar1=0.5, scalar2=None, op0=ALU.mult), r=[st], w=[wout_bf])
            K.drain()
        with ExitStack() as es1:
            xb = [K.sb(es1, "xb%d" % i, [128, D], F32, dma=True) for i in range(3)]
            xs = [K.sb(es1, "xs%d" % i, [128, D], BF16) for i in range(2)]
            xT = [K.sb(es1, "xT%d" % i, [128, D], BF16) for i in range(2)]
            ssx = [K.sb(es1, "ssx%d" % i, [128, 2], F32) for i in range(2)]
            pq_sb = [K.sb(es1, "pq_sb%d" % i, [128, 1024], BF16, dma=True) for i in range(2)]
            gf_sb = [K.sb(es1, "gf_sb%d" % i, [128, 512], BF16, dma=True) for i in range(2)]
            ga_sb = [K.sb(es1, "ga_sb%d" % i, [128, 512], BF16, dma=True) for i in range(2)]
            qk_sb = [K.sb(es1, "qk_sb%d" % i, [128, 3072], BF16, dma=True) for i in range(2)]
            v_sb = [K.sb(es1, "v_sb%d" % i, [128, 1536], BF16, dma=True) for i in range(2)]
            sq = [K.sb(es1, "sq%d" % i, [128, 512], BF16) for i in range(3)]
            th = [K.sb(es1, "th%d" % i, [128, 512], F32) for i in range(2)]
            ss8 = [K.sb(es1, "ss8_%d" % i, [128, 8], F32) for i in range(3)]
            rs8 = [K.sb(es1, "rs8_%d" % i, [128, 8], F32) for i in range(3)]
            xTp = psb[7]
            gps = psb[0:6]
            gcount = 0
            xTv = xTp.t[:].bitcast(BF16)

            def load_x(i):
                xbi = xb[i % 3]
                K.dma("sp", xbi.t[:], x_d[128 * i:128 * i + 128, :], xbi.dsem, w=[xbi])

            def xchain_a(i):
                xbi, xsi, ssi = xb[i % 3], xs[i % 2], ssx[i % 2]
                K.op("act", lambda e: e.activation(out=xsi.t[:], in_=xbi.t[:], func=AF.Square,
                                                   accum_out=ssi.t[:, 0:1]), r=[xbi], w=[xsi, ssi])
                K.op("pool", lambda e: e.tensor_scalar(out=ssi.t[:, 1:2], in0=ssi.t[:, 0:1], scalar1=1.0 / D,
                                                       scalar2=EPS, op0=ALU.mult, op1=ALU.add), r=[ssi], w=[ssi])
                K.op("pool", lambda e: e.tensor_tensor(out=ssi.t[:, 1:2], in0=ssi.t[:, 1:2], in1=mhalf.t[:, 0:1],
                                                       op=ALU.pow), r=[ssi, mhalf], w=[ssi])

            def xchain_b(i):
                xbi, xsi, ssi = xb[i % 3], xs[i % 2], ssx[i % 2]
                K.op("dve", lambda e: e.tensor_scalar(out=xsi.t[:], in0=xbi.t[:], scalar1=ssi.t[:, 1:2],
                                                      scalar2=None, op0=ALU.mult), r=[xbi, ssi], w=[xsi])

            def xtrans(i):
                xsi, xTi = xs[i % 2], xT[i % 2]
                K.mm_group([(lambda e, kc=kc: e.transpose(out=xTv[:, kc * 128:(kc + 1) * 128],
                                                          in_=xsi.t[:, kc * 128:(kc + 1) * 128],
                                                          identity=identb.t[:])) for kc in range(8)],
                           r=[xsi, identb], w=[xTp])
                K.op("dve", lambda e: e.tensor_copy(out=xTi.t[:], in_=xTv), r=[xTp], w=[xTi])

            load_x(0)
            if NTILES_DBG > 1:
                load_x(1)
            xchain_a(0)
            xchain_b(0)
            xtrans(0)
            pending = []

            def flush_pending():
                while pending:
                    pb_, r8_, dst_, g_ = pending.pop(0)
                    K.op("dve", lambda e: e.tensor_tensor(
                        out=dst_.t[:, (g_ - 3) * 512:(g_ - 2) * 512].rearrange("p (h d) -> p h d", d=64),
                        in0=pb_.t[:, :].rearrange("p (h d) -> p h d", d=64),
                        in1=bc(r8_.t[:].unsqueeze(2), [128, 8, 64]), op=ALU.mult), r=[pb_, r8_], w=[dst_])

            for i in range(NTILES_DBG):
                xTi = xT[i % 2]
                if i + 2 < NTILES_DBG:
                    load_x(i + 2)
                if i + 1 < NTILES_DBG:
                    xchain_a(i + 1)
                o = i % 2
                for g in range(13):
                    pb = gps[gcount % 6]
                    gcount += 1
                    K.mm_group([(lambda e, kc=kc: e.matmul(pb.t[:, :], lhsT=xTi.t[:, kc * 128:(kc + 1) * 128],
                                                           rhs=w_bf.t[:, kc, g * 512:(g + 1) * 512],
                                                           start=(kc == 0), stop=(kc == 7))) for kc in range(8)],
                               r=[xTi, w_bf], w=[pb])
                    if g < 2:
                        dst = pq_sb[o]
                        K.op("act", lambda e: e.activation(out=dst.t[:, g * 512:(g + 1) * 512], in_=pb.t[:, :],
                                                           func=AF.Copy), r=[pb], w=[dst])
                    elif g == 2 or g == 12:
                        dst = gf_sb[o] if g == 2 else ga_sb[o]
                        tb = th[0 if g == 2 else 1]
                        K.op("act", lambda e: e.activation(out=tb.t[:], in_=pb.t[:, :], func=AF.Tanh, scale=0.5),
                             r=[pb], w=[tb])
                        flush_pending()
                        K.op("dve", lambda e: e.scalar_tensor_tensor(out=dst.t[:], in0=tb.t[:], scalar=1.0,
                                                                     in1=pb.t[:, :], op0=ALU.add, op1=ALU.mult),
                             r=[tb, pb], w=[dst])
                    elif g < 9:
                        k3 = gcount % 3
                        sqb, s8, r8 = sq[k3], ss8[k3], rs8[k3]
                        dst = qk_sb[o]
                        K.op("act", lambda e: e.activation(out=sqb.t[:], in_=pb.t[:, :], func=AF.Square),
                             r=[pb], w=[sqb])
                        K.op("dve", lambda e: e.tensor_reduce(out=s8.t[:], in_=sqb.t[:].rearrange(
                            "p (h d) -> p h d", d=64), axis=AX.X, op=ALU.add), r=[sqb], w=[s8])
                        flush_pending()
                        K.op("pool", lambda e: e.tensor_scalar(out=r8.t[:], in0=s8.t[:], scalar1=1.0 / 64,
                                                               scalar2=EPS, op0=ALU.mult, op1=ALU.add),
                             r=[s8], w=[r8])
                        K.op("pool", lambda e: e.tensor_tensor(out=r8.t[:], in0=r8.t[:], in1=mhalf.t[:, 0:8],
                                                               op=ALU.pow), r=[r8, mhalf], w=[r8])
                        pending.append((pb, r8, dst, g))
                    else:
                        flush_pending()
                        dst = v_sb[o]
                        K.op("act", lambda e: e.activation(out=dst.t[:, (g - 9) * 512:(g - 8) * 512],
                                                           in_=pb.t[:, :], func=AF.Copy), r=[pb], w=[dst])
                    if g == 3 and i + 1 < NTILES_DBG:
                        xchain_b(i + 1)
                    if g == 6 and i + 1 < NTILES_DBG:
                        xtrans(i + 1)
                rows = slice(128 * i, 128 * i + 128)
                K.dma("sp", PQ_d[rows, :], pq_sb[o].t[:], pq_sb[o].dsem, r=[pq_sb[o]])
                K.dma("sp", GF_d[rows, :], gf_sb[o].t[:], gf_sb[o].dsem, r=[gf_sb[o]])
                K.dma("sp", GA_d[rows, :], ga_sb[o].t[:], ga_sb[o].dsem, r=[ga_sb[o]])
                K.dma("sp", QK_d[rows, :], qk_sb[o].t[:], qk_sb[o].dsem, r=[qk_sb[o]])
                K.dma("sp", V_d[rows, :], v_sb[o].t[:], v_sb[o].dsem, r=[v_sb[o]])
            K.drain()


    fT = K.sb(es0, "fT", [128, 4, S], BF16)
    if STOP >= 2 and not SKIP01:
        with ExitStack() as es:
            tabs = [K.sb(es, "tab%d" % t, [128, 8192], BF16, dma=True) for t in range(3)]
            zin = [K.sb(es, "zin%d" % i, [128, 8, 1024], BF16, dma=True) for i in range(2)]
            ysb = [K.sb(es, "ysb%d" % i, [128, 2, 512], BF16, dma=True) for i in range(4)]
            for t, td in enumerate((tabA_d, tabB_d, tabC_d)):
                K.dma("sp", tabs[t].t[:], td[:, :], tabs[t].dsem, w=[tabs[t]])
            PQv = PQ_d.rearrange("(s1 s2) c -> s1 s2 c", s2=64)
            def load_z(s2c):
                z_ = zin[s2c % 2]
                K.dma("sp", z_.t[:], PQv[:, s2c * 8:(s2c + 1) * 8, :], z_.dsem, w=[z_])

            load_z(0)
            for s2c in range(8):
                z = zin[s2c % 2]
                if s2c + 1 < 8:
                    load_z(s2c + 1)
                for s2l in range(8):
                    s2 = s2c * 8 + s2l
                    cs = slice(s2 * 128, (s2 + 1) * 128)
                    A, B, C = tabs[0].t[:, cs], tabs[1].t[:, cs], tabs[2].t[:, cs]
                    P, Q = z.t[:, s2l, 0:512], z.t[:, s2l, 512:1024]
                    pr, pi = psb[(2 * s2) % 8], psb[(2 * s2 + 1) % 8]
                    K.mm_group([lambda e: e.matmul(pr.t[:, :], lhsT=A, rhs=P, start=True, stop=False),
                                lambda e: e.matmul(pr.t[:, :], lhsT=C, rhs=Q, start=False, stop=True)],
                               r=[tabs[0], tabs[2], z], w=[pr])
                    K.mm_group([lambda e: e.matmul(pi.t[:, :], lhsT=B, rhs=P, start=True, stop=False),
                                lambda e: e.matmul(pi.t[:, :], lhsT=A, rhs=Q, start=False, stop=True)],
                               r=[tabs[0], tabs[1], z], w=[pi])
                    yb = ysb[s2 % 4]
                    K.op("act", lambda e: e.activation(out=yb.t[:, 0, :], in_=pr.t[:, :], func=AF.Copy), r=[pr], w=[yb])
                    K.op("dve", lambda e: e.tensor_copy(out=yb.t[:, 1, :], in_=pi.t[:, :]), r=[pi], w=[yb])
                    K.dma("sp", Y_d[s2], yb.t[:], yb.dsem, r=[yb])
            K.drain()
        with ExitStack() as es:
            cs2 = K.sb(es, "cs2", [128, 64], BF16, dma=True)
            K.dma("sp", cs2.t[:], cs2_d[:, :], cs2.dsem, w=[cs2])
            yin_t = [es.enter_context(nc.sbuf_tensor("sb_yin%d" % i, [128, 16, 512], BF16)) for i in range(2)]
            yin = [[Buf(yin_t[i], K.new_dsem()) for _ in range(2)] for i in range(2)]
            for i in range(2):
                K.allbufs.extend(yin[i])
            cnt = 0
            def load_y(k1c):
                yt_ = yin_t[k1c % 2]
                for ri, yb in enumerate(yin[k1c % 2]):
                    K.dma("sp" if ri == 0 else "act", yt_[64 * ri:64 * ri + 64, :, :],
                          Y_d[:, k1c * 16:(k1c + 1) * 16, ri, :], yb.dsem, w=[yb])

            load_y(0)
            for k1c in range(8):
                yt = yin_t[k1c % 2]
                yl, yh = yin[k1c % 2]
                if k1c + 1 < 8:
                    load_y(k1c + 1)
                for chc in range(4):
                    for hh in range(2):
                        pb = psb[cnt % 8]
                        K.mm_group([(lambda e, k=k: e.matmul(pb.t[:, k * 64:(k + 1) * 64],
                                                             lhsT=yt[:, hh * 8 + k, chc * 128:(chc + 1) * 128],
                                                             rhs=cs2.t[:, :], start=True, stop=True)) for k in range(8)],
                                   r=[yl, yh, cs2], w=[pb])
                        k10 = k1c * 16 + hh * 8
                        dst = fT.t[:, chc, :].rearrange("p (k2 k1) -> p k2 k1", k1=128)[:, :, k10:k10 + 8]
                        src = pb.t[:, :].rearrange("p (k1 k2) -> p k2 k1", k2=64)
                        if cnt % 2 == 0:
                            K.op("act", lambda e: e.activation(out=dst, in_=src, func=AF.Copy), r=[pb])
                        else:
                            K.op("dve", lambda e: e.tensor_copy(out=dst, in_=src), r=[pb])
                        cnt += 1
            K.drain()

    if STOP >= 3:
        with ExitStack() as es:
            em = K.sb(es, "em", [128, 12, 512], BF16, dma=True)
            gqT = K.sb(es, "gqT", [128, 12], F32, dma=True)
            gkT = K.sb(es, "gkT", [128, 12], F32, dma=True)
            GT = K.sb(es, "GT", [128, 12], F32)
            K.dma("sp", em.t[:], em_d.rearrange("p (a b) -> p a b", b=512), em.dsem, w=[em])
            K.dma("sp", gqT.t[:], qn_d.rearrange("(pr h2) d -> (h2 d) pr", h2=2), gqT.dsem, w=[gqT],
                  allow_slow_non_contiguous=True)
            K.dma("sp", gkT.t[:], kn_d.rearrange("(pr h2) d -> (h2 d) pr", h2=2), gkT.dsem, w=[gkT],
                  allow_slow_non_contiguous=True)
            K.op("dve", lambda e: e.scalar_tensor_tensor(out=GT.t[:], in0=gqT.t[:], scalar=0.125, in1=gkT.t[:],
                                                         op0=ALU.mult, op1=ALU.mult), r=[gqT, gkT], w=[GT])
            RQ, RK, RV = 4, 6, 12
            qsb = [K.sb(es, "qsb%d" % i, [128, 512], BF16, dma=True) for i in range(RQ)]
            ksb = [K.sb(es, "ksb%d" % i, [128, 512], BF16, dma=True) for i in range(RK)]
            vsb = [K.sb(es, "vsb%d" % i, [128, 8, 128], BF16, dma=True) for i in range(RV)]
            for vb_ in vsb:
                K.op("pool", lambda e: e.memset(vb_.t[:], 0.0), w=[vb_])
            qTb = [K.sb(es, "qT%d" % i, [128, 4, 128], BF16) for i in range(2)]
            kTb = [K.sb(es, "kT%d" % i, [128, 4, 128], BF16) for i in range(RK)]
            pex = [K.sb(es, "pex%d" % i, [128, 512], BF16) for i in range(3)]
            pmk = [K.sb(es, "pmk%d" % i, [128, 512], BF16) for i in range(8)]
            osb = [K.sb(es, "osb%d" % i, [128, 520], F32, dma=True) for i in range(2)]
            for i_, ob_ in enumerate(osb):
                ob_.swsem = K.new_swsem("sw_osb%d" % i_)
            tq_ps, tk_ps = psb[0], psb[1]
            S_ps = [psb[2], psb[3]]
            O_ps = [[psb[4], psb[5]], [psb[6], psb[7]]]
            blocks, ktiles = [], []
            for c, dil in enumerate(DILS):
                nqb = S // dil // 128
                for r in range(dil):
                    base = len(ktiles)
                    for kt in range(nqb + 1):
                        ktiles.append((c, r, kt))
                    for qb in range(nqb):
                        blocks.append((c, r, qb, base + qb, base + qb + 1))
            NB = len(blocks)
            views = []
            for c, dil in enumerate(DILS):
                views.append((QK_d.rearrange("(i d) c -> d i c", d=dil), V_d.rearrange("(i d) c -> d i c", d=dil),
                              O_d[c].rearrange("(i d) e -> d i e", d=dil)))
            st = {"kl": 0, "kt": 0, "sc": 0}

            def load_q(b):
                c, r, qb = blocks[b][0:3]
                qb_ = qsb[b % RQ]
                K.dma("sp", qb_.t[:], views[c][0][r, 128 * qb:128 * qb + 128, c * 512:(c + 1) * 512], qb_.dsem,
                      w=[qb_])

            def load_kv(ki):
                c, r, kt = ktiles[ki]
                dil = DILS[c]
                L = S // dil
                nqb = L // 128
                kb, vb = ksb[ki % RK], vsb[ki % RV]
                a, b_ = 128 * kt - 64, 128 * kt + 64
                p0, p1 = 0, 128
                if kt == 0:
                    a, p0 = 0, 64
                if kt == nqb:
                    b_, p1 = L, 64
                if kt == 0 or kt == nqb:
                    K.op("pool", lambda e: e.memset(kb.t[:], 0.0), w=[kb])
                    K.op("pool", lambda e: e.memset(vb.t[:], 0.0), w=[vb])
                K.op("pool", lambda e: e.memset(vb.t[p0:p1, :, 64:65], 1.0), w=[vb])
                K.dma("sp", kb.t[p0:p1, :], views[c][0][r, a:b_, 1536 + c * 512:1536 + (c + 1) * 512], kb.dsem,
                      w=[kb])
                K.dma("sp", vb.t[p0:p1, :, 0:64], views[c][1][r, a:b_, c * 512:(c + 1) * 512].rearrange(
                    "n (h d) -> n h d", d=64), vb.dsem, w=[vb])

            def ensure_loaded(upto):
                while st["kl"] <= upto:
                    load_kv(st["kl"])
                    st["kl"] += 1

            def tr_q(b):
                qb_, qt = qsb[b % RQ], qTb[b % 2]
                tv = tq_ps.t[:].bitcast(BF16)
                K.mm_group([(lambda e, pr=pr: e.transpose(out=tv[:, pr * 128:(pr + 1) * 128],
                                                          in_=qb_.t[:, pr * 128:(pr + 1) * 128],
                                                          identity=identb.t[:])) for pr in range(4)],
                           r=[qb_, identb], w=[tq_ps])
                K.op("dve", lambda e: e.tensor_copy(out=qt.t[:].rearrange("p a b -> p (a b)"), in_=tv[:, 0:512]),
                     r=[tq_ps], w=[qt])

            def tr_k(ki):
                c = ktiles[ki][0]
                kb, kt_ = ksb[ki % RK], kTb[ki % RK]
                tv = tk_ps.t[:].bitcast(BF16)
                K.mm_group([(lambda e, pr=pr: e.transpose(out=tv[:, pr * 128:(pr + 1) * 128],
                                                          in_=kb.t[:, pr * 128:(pr + 1) * 128],
                                                          identity=identb.t[:])) for pr in range(4)],
                           r=[kb, identb], w=[tk_ps])
                K.op("dve", lambda e: e.tensor_tensor(
                    out=kt_.t[:], in0=tv[:, 0:512].rearrange("p (a b) -> p a b", b=128),
                    in1=bc(GT.t[:, 4 * c:4 * c + 4].unsqueeze(2), [128, 4, 128]), op=ALU.mult),
                    r=[tk_ps, GT], w=[kt_])

            def ensure_tr(upto):
                while st["kt"] <= upto:
                    tr_k(st["kt"])
                    st["kt"] += 1

            def s_block(b, prs):
                c, r, qb, kl, kh = blocks[b]
                qt, klo, khi = qTb[b % 2], kTb[kl % RK], kTb[kh % RK]
                half = prs[0] // 2
                fns = []
                for pp, pr in enumerate(prs):
                    for kk, off in ((klo, 0), (khi, 128)):
                        for h2 in range(2):
                            rs = slice(64 * h2, 64 * h2 + 64)
                            fns.append(lambda e, h2=h2, kk=kk, off=off, pp=pp, pr=pr, rs=rs: e.matmul(
                                S_ps[h2].t[:, pp * 256 + off:pp * 256 + off + 128], lhsT=kk.t[rs, pr, :],
                                rhs=qt.t[rs, pr, :], start=True, stop=True))
                K.mm_group(fns, r=[klo, khi, qt], w=[S_ps[0], S_ps[1]])
                for h2 in range(2):
                    px, pm = pex[st["sc"] % 3], pmk[(b % 2) * 4 + half * 2 + h2]
                    st["sc"] += 1
                    Sp = S_ps[h2]
                    K.op("act", lambda e: e.activation(out=px.t[:], in_=Sp.t[:, :], func=AF.Exp), r=[Sp], w=[px])
                    K.op("dve", lambda e: e.tensor_tensor(out=pm.t[:], in0=px.t[:],
                                                          in1=em.t[:, (c * 2 + half) * 2 + h2, :],
                                                          op=ALU.mult), r=[px, em], w=[pm])

            def pv_block(b, prs):
                c, r, qb, kl, kh = blocks[b]
                vlo, vhi = vsb[kl % RV], vsb[kh % RV]
                Ob = O_ps[b % 2]
                fns, pms = [], []
                Obk = Ob[prs[0] // 2]
                for pr in prs:
                    for h2 in range(2):
                        pm = pmk[(b % 2) * 4 + (pr // 2) * 2 + h2]
                        pms.append(pm)
                        pp = pr % 2
                        h = 2 * pr + h2
                        oc = slice((h % 4) * 128, (h % 4) * 128 + 128)
                        fns.append(lambda e, pm=pm, pp=pp, h=h, oc=oc: e.matmul(
                            Obk.t[:, oc], lhsT=pm.t[:, pp * 256:pp * 256 + 128], rhs=vlo.t[:, h, :],
                            start=True, stop=False))
                        fns.append(lambda e, pm=pm, pp=pp, h=h, oc=oc: e.matmul(
                            Obk.t[:, oc], lhsT=pm.t[:, pp * 256 + 128:pp * 256 + 256], rhs=vhi.t[:, h, :],
                            start=False, stop=True))
                K.mm_group(fns, r=pms + [vlo, vhi], w=[Obk])

            if P3MODE > 0:
                for t_ in range(3):
                    load_q(t_)
                ensure_loaded(blocks[2][4])
                tr_q(0)
                ensure_tr(blocks[0][4])
                s_block(0, (0, 1))
                s_block(0, (2, 3))
                tr_q(1)
                ensure_tr(blocks[1][4])
                for b in range(NB):
                    if b + 3 < NB:
                        load_q(b + 3)
                        ensure_loaded(blocks[b + 3][4])
                    c, r, qb = blocks[b][0:3]
                    ob, Ob = osb[b % 2], O_ps[b % 2]
                    for half in range(2):
                        if b + 1 < NB:
                            s_block(b + 1, (2 * half, 2 * half + 1))
                        pv_block(b, (2 * half, 2 * half + 1))
                        if half == 0 and b + 2 < NB:
                            tr_q(b + 2)
                            ensure_tr(blocks[b + 2][4])
                    K.op("act", lambda e: e.activation(
                        out=ob.t[:, 0:260].rearrange("p (h e) -> p h e", e=65),
                        in_=Ob[0].t[:, :].rearrange("p (h e) -> p h e", e=128)[:, :, 0:65], func=AF.Copy),
                        r=[Ob[0]], w=[ob])
                    K.op("dve", lambda e: e.tensor_copy(
                        out=ob.t[:, 260:520].rearrange("p (h e) -> p h e", e=65),
                        in_=Ob[1].t[:, :].rearrange("p (h e) -> p h e", e=128)[:, :, 0:65]),
                        r=[Ob[1]], w=[ob])
                    if c == 0:
                        K.dma("sp", views[c][2][r, 128 * qb:128 * qb + 128, :], ob.t[:], ob.dsem, r=[ob])
                    else:
                        if blocks[b - 1][0] != c:
                            for ob_ in osb:
                                for sm in (ob_.dsem, ob_.swsem):
                                    if sm.n > 0 and K.waited["pool"].get(sm, 0) < sm.n:
                                        nc.gpsimd.wait_ge(sm.h, sm.n)
                                        K.waited["pool"][sm] = sm.n
                        K.dma("pool", views[c][2][r, 128 * qb:128 * qb + 128, :], ob.t[:], ob.swsem, r=[ob],
                              accum_op=ALU.add)
            K.drain()


    if STOP >= 4:
        with ExitStack() as es:
            NS = 4
            o_in = [[K.sb(es, "oin%d_%d" % (c, i), [128, 520], F32, dma=True) for c in range(1)] for i in range(NS)]
            ga_in = [K.sb(es, "ga_in%d" % i, [128, 512], BF16, dma=True) for i in range(NS)]
            gf_in = [K.sb(es, "gf_in%d" % i, [128, 512], BF16, dma=True) for i in range(NS)]
            xr = [K.sb(es, "xr%d" % i, [128, D], F32, dma=True) for i in range(NS)]
            osum = [K.sb(es, "osum%d" % i, [128, 520], F32) for i in range(3)]
            rden = [K.sb(es, "rden%d" % i, [128, 8], F32) for i in range(3)]
            ya32 = [K.sb(es, "ya32_%d" % i, [128, 512], F32) for i in range(3)]
            ya_bf = [K.sb(es, "ya_bf%d" % i, [128, 512], BF16) for i in range(3)]
            yT = [K.sb(es, "yT%d" % i, [128, 8, 128], BF16) for i in range(2)]
            out_sb = [K.sb(es, "out_sb%d" % i, [128, D], F32) for i in range(2)]
            for i_, ob_ in enumerate(out_sb):
                ob_.dsem = K.new_swsem("sw_out%d" % i_)
            tpa, tpg = [psb[0], psb[6]], [psb[1], psb[7]]
            outp = [[psb[2], psb[3]], [psb[4], psb[5]]]

            def load4(i):
                o = i % NS
                rows = slice(128 * i, 128 * i + 128)
                K.dma("sp", o_in[o][0].t[:], O_acc[rows, :], o_in[o][0].dsem, w=[o_in[o][0]])
                K.dma("sp", ga_in[o].t[:], GA_d[rows, :], ga_in[o].dsem, w=[ga_in[o]])
                K.dma("sp", gf_in[o].t[:], GF_d[rows, :], gf_in[o].dsem, w=[gf_in[o]])
                K.dma("act", xr[o].t[:], x_d[rows, :], xr[o].dsem, w=[xr[o]])

            def stage_a1(i):
                o3, s3 = i % 3, i % NS
                os_, rd, y32, ybf = o_in[s3][0], rden[o3], ya32[o3], ya_bf[o3]
                osv = os_.t[:].rearrange("p (h e) -> p h e", e=65)
                K.op("dve", lambda e: e.reciprocal(out=rd.t[:], in_=osv[:, :, 64]), r=[os_], w=[rd])
                K.op("dve", lambda e: e.tensor_tensor(out=y32.t[:].rearrange("p (h d) -> p h d", d=64),
                                                      in0=osv[:, :, 0:64],
                                                      in1=bc(rd.t[:].unsqueeze(2), [128, 8, 64]),
                                                      op=ALU.mult), r=[os_, rd], w=[y32])
                K.op("pool", lambda e: e.tensor_tensor(out=ybf.t[:], in0=y32.t[:], in1=ga_in[s3].t[:], op=ALU.mult),
                     r=[y32, ga_in[s3]], w=[ybf])

            def stage_a2(i):
                o, o3, s3 = i % 2, i % 3, i % NS
                ybf, yt = ya_bf[o3], yT[o]
                tav = tpa[o].t[:].bitcast(BF16)
                tgv = tpg[o].t[:].bitcast(BF16)
                K.mm_group([(lambda e, j=j: e.transpose(out=tgv[:, j * 128:(j + 1) * 128],
                                                        in_=gf_in[s3].t[:, j * 128:(j + 1) * 128],
                                                        identity=identb.t[:]))
                            for j in range(4)], r=[gf_in[s3], identb], w=[tpg[o]])
                K.op("dve", lambda e: e.tensor_tensor(out=yt.t[:, 0:4, :],
                                                      in0=tgv[:, 0:512].rearrange("p (a b) -> p a b", b=128),
                                                      in1=fT.t[:, :, 128 * i:128 * i + 128], op=ALU.mult),
                     r=[tpg[o]], w=[yt])
                K.mm_group([(lambda e, j=j: e.transpose(out=tav[:, j * 128:(j + 1) * 128],
                                                        in_=ybf.t[:, j * 128:(j + 1) * 128], identity=identb.t[:]))
                            for j in range(4)], r=[ybf, identb], w=[tpa[o]])
                K.op("act", lambda e: e.activation(out=yt.t[:, 4:8, :].rearrange("p a b -> p (a b)"),
                                                   in_=tav[:, 0:512], func=AF.Copy), r=[tpa[o]], w=[yt])

            def stage_b_mm(i):
                o = i % 2
                yt = yT[o]
                for half in range(2):
                    pb = outp[o][half]
                    K.mm_group([(lambda e, cc=cc: e.matmul(pb.t[:, :], lhsT=yt.t[:, cc, :],
                                                           rhs=wout_bf.t[:, cc, half * 512:(half + 1) * 512],
                                                           start=(cc == 0), stop=(cc == 7))) for cc in range(8)],
                               r=[yt, wout_bf], w=[pb])

            def stage_b_res(i):
                o, s3 = i % 2, i % NS
                rows = slice(128 * i, 128 * i + 128)
                for half in range(2):
                    pb = outp[o][half]
                    K.op("dve", lambda e: e.tensor_tensor(out=out_sb[o].t[:, half * 512:(half + 1) * 512],
                                                          in0=pb.t[:, :],
                                                          in1=xr[s3].t[:, half * 512:(half + 1) * 512],
                                                          op=ALU.add), r=[pb, xr[s3]], w=[out_sb[o]])
                K.dma("pool", y_d[rows, :], out_sb[o].t[:], out_sb[o].dsem, r=[out_sb[o]])

            for t_ in range(3):
                load4(t_)
            stage_a1(0)
            stage_a1(1)
            stage_a2(0)
            for i in range(NT):
                if i + 3 < NT:
                    load4(i + 3)
                if i + 2 < NT:
                    stage_a1(i + 2)
                if i + 1 < NT:
                    stage_a2(i + 1)
                stage_b_mm(i)
                stage_b_res(i)
    K.drain()
    K.root.close()
    return nc


_CONSTS = None


def kernel(x, norm_w, w_in, q_norm_w, k_norm_w, w_fourier, w_out):
    global _CONSTS
    if _CONSTS is None:
        _CONSTS = _consts()
    nc = build()
    x = np.ascontiguousarray(np.asarray(x, dtype=np.float32))
    shared = {
        "norm_w": np.asarray(norm_w, np.float32), "w_in": np.asarray(w_in, np.float32),
        "q_norm_w": np.asarray(q_norm_w, np.float32), "k_norm_w": np.asarray(k_norm_w, np.float32),
        "w_fourier": np.asarray(w_fourier, np.float32), "w_out": np.asarray(w_out, np.float32),
    }
    shared.update(_CONSTS)
    in_maps = [dict(shared, x=x[b]) for b in range(NCORES)]
    if os.environ.get("MK_TRACE") == "1":
        res = run_bass_kernel_spmd(nc, in_maps, core_ids=list(range(NCORES)), trace=True)
        print("EXEC_TIME_NS", res.exec_time_ns)
    else:
        res = run_bass_kernel_spmd(nc, in_maps, core_ids=list(range(NCORES)))
    if DEBUG:
        kernel.last = res
    ys = [np.asarray(r["y"]) for r in res.results]
    ys += [np.zeros_like(ys[0])] * (8 - len(ys))
    return np.stack(ys, axis=0).astype(np.float32)
```
